# Optimizing a Trainium2 kernel written in Bass

```python
import math
import jax, jax.numpy as jnp
from jax import lax
import numpy as np

D_MODEL = 1024
BATCH = 4
SEQ = 8192
DEPTH = 1

N_META = 16
BLOCK = 128
PAD_FRONT = BLOCK - N_META
NORM_EPS = 1e-6
NEG = -1e30

SSD_HEADDIM = 64
SSD_HEADS = 16
SSD_INNER = SSD_HEADS * SSD_HEADDIM
SSD_GROUPS = 2
SSD_STATE = 128
SSD_CONV = 4
SSD_CONV_DIM = SSD_INNER + 2 * SSD_GROUPS * SSD_STATE

ATT_HEADS = 8
ATT_HEAD_DIM = 64
ATT_QK = ATT_HEADS * 2 * ATT_HEAD_DIM
ATT_V = ATT_HEADS * 2 * ATT_HEAD_DIM

D_FF = ((8 * D_MODEL // 3 + 255) // 256) * 256

SPLIT_SIZES = [SSD_INNER,
               SSD_CONV_DIM,
               SSD_HEADS,
               ATT_QK, ATT_QK, ATT_V,
               2 * D_MODEL]
IN_COLS = sum(SPLIT_SIZES)
SPLIT_POINTS = list(np.cumsum(SPLIT_SIZES)[:-1])

kernel_name = "hybrid_ssd_diffattn_gated_block"


def rmsnorm(x, g):
    xf = x.astype(jnp.float32)
    y = xf * lax.rsqrt(jnp.mean(xf * xf, axis=-1, keepdims=True) + NORM_EPS)
    return (y * g.astype(jnp.float32)).astype(x.dtype)


def causal_depthwise_conv(u, w, b):
    c = u.shape[-1]
    kern = jnp.transpose(w)[:, None, :].astype(u.dtype)
    y = lax.conv_general_dilated(u, kern, window_strides=(1,), padding=[(SSD_CONV - 1, 0)],
                                 dimension_numbers=('NWC', 'WIO', 'NWC'), feature_group_count=c)
    return y + b.astype(u.dtype)


def ssd_chunked(xs, bmat, cmat, dt_raw, dt_bias, a_log, d_skip, valid):
    bsz, L = xs.shape[0], xs.shape[1]
    nc = L // BLOCK
    hpg = SSD_HEADS // SSD_GROUPS
    dt = jax.nn.softplus(dt_raw.astype(jnp.float32) + dt_bias.astype(jnp.float32))
    dt = jnp.where(valid[None, :, None], dt, 0.0)
    a = -jnp.exp(a_log.astype(jnp.float32))
    da = (dt * a).reshape(bsz, nc, BLOCK, SSD_GROUPS, hpg)
    dt = dt.reshape(bsz, nc, BLOCK, SSD_GROUPS, hpg)
    x = xs.astype(jnp.float32).reshape(bsz, nc, BLOCK, SSD_GROUPS, hpg, SSD_HEADDIM)
    xdt = x * dt[..., None]
    bc = bmat.astype(jnp.float32).reshape(bsz, nc, BLOCK, SSD_GROUPS, SSD_STATE)
    cc = cmat.astype(jnp.float32).reshape(bsz, nc, BLOCK, SSD_GROUPS, SSD_STATE)
    acs = jnp.cumsum(da, axis=2)

    tri = jnp.arange(BLOCK)[:, None] >= jnp.arange(BLOCK)[None, :]
    seg = acs[:, :, :, None] - acs[:, :, None, :]
    decay = jnp.exp(jnp.where(tri[:, :, None, None], seg, -jnp.inf))
    cb = jnp.einsum('bclgn,bcsgn->bclsg', cc, bc)
    y_diag = jnp.einsum('bclsg,bclsge,bcsgep->bclgep', cb, decay, xdt)

    decay_states = jnp.exp(acs[:, :, -1:] - acs)
    states = jnp.einsum('bclgn,bclge,bclgep->bcgepn', bc, decay_states, xdt)
    chunk_decay = jnp.exp(acs[:, :, -1])

    def step(h, inp):
        s_c, d_c = inp
        return h * d_c[..., None, None] + s_c, h

    h0 = jnp.zeros((bsz, SSD_GROUPS, hpg, SSD_HEADDIM, SSD_STATE), jnp.float32)
    _, prev = lax.scan(step, h0, (jnp.moveaxis(states, 1, 0), jnp.moveaxis(chunk_decay, 1, 0)))
    prev = jnp.moveaxis(prev, 0, 1)
    y_off = jnp.einsum('bclgn,bcgepn,bclge->bclgep', cc, prev, jnp.exp(acs))

    y = y_diag + y_off + x * d_skip.astype(jnp.float32).reshape(SSD_GROUPS, hpg)[:, :, None]
    return y.reshape(bsz, L, SSD_INNER)


def diff_attention(q, k, v, lam, lam_init, subln_g):
    bsz, L = q.shape[0], q.shape[1]
    nb = L // BLOCK
    slopes = 2.0 ** (-8.0 * (jnp.arange(ATT_HEADS, dtype=jnp.float32) + 1.0) / ATT_HEADS)
    scale = ATT_HEAD_DIM ** -0.5
    kpos = jnp.arange(L)
    kvalid = kpos >= PAD_FRONT
    vf = v.astype(jnp.float32)
    qb = jnp.moveaxis(q.reshape(bsz, nb, BLOCK, ATT_HEADS, 2, ATT_HEAD_DIM), 1, 0)

    def block(args):
        qblk, i = args
        qpos = i * BLOCK + jnp.arange(BLOCK)
        s = jnp.einsum('bqhcd,bkhcd->bhcqk', qblk, k,
                       preferred_element_type=jnp.float32) * scale
        dist = (qpos[:, None] - kpos[None, :]).astype(jnp.float32)
        allowed = (kpos[None, :] <= qpos[:, None]) & kvalid[None, :]
        s = s - slopes[None, :, None, None, None] * dist
        s = jnp.where(allowed, s, NEG)
        p = jax.nn.softmax(s, axis=-1)
        a = p[:, :, 0] - lam * p[:, :, 1]
        return jnp.einsum('bhqk,bkhe->bqhe', a, vf)

    out = lax.map(block, (qb, jnp.arange(nb)))
    out = jnp.moveaxis(out, 0, 1).reshape(bsz, L, ATT_HEADS, 2 * ATT_HEAD_DIM)
    out = rmsnorm(out, subln_g) * (1.0 - lam_init)
    return out.reshape(bsz, L, ATT_V)


def setup_inputs(seed: int = 0) -> dict:
    key = jax.random.key(seed)
    ks = jax.random.split(key, 24)
    f32 = jnp.float32
    nrm = lambda k, shp, s: jax.random.normal(k, shp, f32) * s
    dt0 = jnp.exp(jax.random.uniform(ks[9], (DEPTH, SSD_HEADS), f32)
                  * (math.log(0.1) - math.log(0.001)) + math.log(0.001))
    return {
        "x": nrm(ks[0], (BATCH, SEQ, D_MODEL), 1.0),
        "meta_tokens": nrm(ks[1], (N_META, D_MODEL), 1.0),
        "norm_mix_g": 1.0 + nrm(ks[2], (DEPTH, D_MODEL), 0.02),
        "w_in": nrm(ks[3], (DEPTH, D_MODEL, IN_COLS), D_MODEL ** -0.5),
        "gate_bias": nrm(ks[4], (DEPTH, 2 * D_MODEL), 0.01),
        "conv_w": nrm(ks[5], (DEPTH, SSD_CONV_DIM, SSD_CONV), SSD_CONV ** -0.5),
        "conv_b": nrm(ks[6], (DEPTH, SSD_CONV_DIM), 0.01),
        "dt_bias": dt0 + jnp.log(-jnp.expm1(-dt0)),
        "a_log": jnp.log(jax.random.uniform(ks[7], (DEPTH, SSD_HEADS), f32, 1.0, 16.0)),
        "d_skip": 1.0 + nrm(ks[8], (DEPTH, SSD_HEADS), 0.1),
        "ssd_norm_g": 1.0 + nrm(ks[10], (DEPTH, SSD_INNER), 0.02),
        "lambda_q1": nrm(ks[11], (DEPTH, ATT_HEAD_DIM), 0.1),
        "lambda_k1": nrm(ks[12], (DEPTH, ATT_HEAD_DIM), 0.1),
        "lambda_q2": nrm(ks[13], (DEPTH, ATT_HEAD_DIM), 0.1),
        "lambda_k2": nrm(ks[14], (DEPTH, ATT_HEAD_DIM), 0.1),
        "subln_g": 1.0 + nrm(ks[15], (DEPTH, 2 * ATT_HEAD_DIM), 0.02),
        "w_ssd_branch": nrm(ks[16], (DEPTH, SSD_INNER, D_MODEL), SSD_INNER ** -0.5),
        "w_attn_branch": nrm(ks[17], (DEPTH, ATT_V, D_MODEL), ATT_V ** -0.5),
        "w_out": nrm(ks[18], (DEPTH, D_MODEL, D_MODEL), D_MODEL ** -0.5),
        "norm_ffn_g": 1.0 + nrm(ks[19], (DEPTH, D_MODEL), 0.02),
        "w_gate_ffn": nrm(ks[20], (DEPTH, D_MODEL, D_FF), D_MODEL ** -0.5),
        "w_up_ffn": nrm(ks[21], (DEPTH, D_MODEL, D_FF), D_MODEL ** -0.5),
        "w_down_ffn": nrm(ks[22], (DEPTH, D_FF, D_MODEL), D_FF ** -0.5),
        "norm_final_g": 1.0 + nrm(ks[23], (D_MODEL,), 0.02),
    }


def reference(x, meta_tokens, norm_mix_g, w_in, gate_bias, conv_w, conv_b, dt_bias, a_log,
              d_skip, ssd_norm_g, lambda_q1, lambda_k1, lambda_q2, lambda_k2, subln_g,
              w_ssd_branch, w_attn_branch, w_out, norm_ffn_g, w_gate_ffn, w_up_ffn,
              w_down_ffn, norm_final_g):
    bsz = x.shape[0]
    dt_ = x.dtype
    meta = jnp.broadcast_to(meta_tokens.astype(dt_)[None], (bsz, N_META, D_MODEL))
    h = jnp.concatenate([jnp.zeros((bsz, PAD_FRONT, D_MODEL), dt_), meta, x], axis=1)
    L = h.shape[1]
    valid = jnp.arange(L) >= PAD_FRONT
    vmask = valid.astype(dt_)[None, :, None]

    for l in range(DEPTH):
        u = rmsnorm(h, norm_mix_g[l]) * vmask
        proj = u @ w_in[l]
        z, xbc, dt_raw, q, k, v, gates = jnp.split(proj, SPLIT_POINTS, axis=-1)

        xbc = jax.nn.silu(causal_depthwise_conv(xbc, conv_w[l], conv_b[l]))
        xs, bm, cm = jnp.split(xbc, [SSD_INNER, SSD_INNER + SSD_GROUPS * SSD_STATE], axis=-1)
        bm = bm.reshape(bsz, L, SSD_GROUPS, SSD_STATE)
        cm = cm.reshape(bsz, L, SSD_GROUPS, SSD_STATE)
        y_ssd = ssd_chunked(xs, bm, cm, dt_raw, dt_bias[l], a_log[l], d_skip[l], valid)
        y_ssd = rmsnorm(y_ssd * jax.nn.silu(z.astype(jnp.float32)), ssd_norm_g[l]).astype(dt_)

        lam_init = 0.8 - 0.6 * math.exp(-0.3 * l)
        lam = (jnp.exp(jnp.sum(lambda_q1[l].astype(jnp.float32) * lambda_k1[l].astype(jnp.float32)))
               - jnp.exp(jnp.sum(lambda_q2[l].astype(jnp.float32) * lambda_k2[l].astype(jnp.float32)))
               + lam_init)
        q = q.reshape(bsz, L, ATT_HEADS, 2, ATT_HEAD_DIM)
        k = k.reshape(bsz, L, ATT_HEADS, 2, ATT_HEAD_DIM)
        v = v.reshape(bsz, L, ATT_HEADS, 2 * ATT_HEAD_DIM)
        y_att = diff_attention(q, k, v, lam, lam_init, subln_g[l]).astype(dt_)

        g_ssd, g_att = jnp.split(jax.nn.sigmoid(gates + gate_bias[l]), 2, axis=-1)
        merged = g_ssd * (y_ssd @ w_ssd_branch[l]) + g_att * (y_att @ w_attn_branch[l])
        h = h + merged @ w_out[l]

        u2 = rmsnorm(h, norm_ffn_g[l])
        h = h + (jax.nn.silu(u2 @ w_gate_ffn[l]) * (u2 @ w_up_ffn[l])) @ w_down_ffn[l]

    out = rmsnorm(h, norm_final_g)
    return out[:, PAD_FRONT + N_META:]
```

```python
import numpy as np
import ml_dtypes
from contextlib import ExitStack
import concourse.bass as bass
import concourse.mybir as mybir
from concourse.bass_utils import run_bass_kernel_spmd

F32 = mybir.dt.float32
BF16 = mybir.dt.bfloat16
AF = mybir.ActivationFunctionType
ALU = mybir.AluOpType

D = 1024
NV = 65
NT = 33
TV = NV * 128
TO = NT * 128
DFF = 2816
NKC = 8
EPS = 1e-6
NEGB = -60000.0
DEBUG = False

COMPUTE = ("tensor", "vector", "scalar", "gpsimd")
ALLQ = ("sync",) + COMPUTE


class Fw:
    def __init__(self, nc, es):
        self.nc = nc
        self.es = es
        self.E = {"sync": nc.sync, "tensor": nc.tensor, "vector": nc.vector, "scalar": nc.scalar, "gpsimd": nc.gpsimd}
        self.res = {}
        self.waited = {e: {} for e in ALLQ}
        self.esem = {}
        self.ecnt = {}
        self.dsem = {}
        self.dcnt = {}
        self.nsem = 0
        self.new_epoch()

    def _newsem(self, name):
        self.nsem += 1
        h = self.es.enter_context(self.nc.semaphore(f"s{self.nsem}_{name}"))
        self.keep = getattr(self, "keep", [])
        self.keep.append(h)
        return h

    def new_epoch(self):
        for e in COMPUTE:
            self.esem[e] = self._newsem(e)
            self.ecnt[e] = 0

    def _deps(self, r, w):
        deps = []
        for k in r:
            ent = self.res.get(k)
            if ent and ent[0] is not None:
                deps.append(ent[0])
        for k in w:
            ent = self.res.get(k)
            if ent:
                if ent[0] is not None:
                    deps.append(ent[0])
                deps.extend(ent[1])
        return deps

    def _emit_waits(self, eng, deps):
        wd = self.waited[eng]
        need = {}
        for (sem, val) in deps:
            if eng == "tensor" and sem is self.esem["tensor"]:
                continue
            if wd.get(id(sem), 0) < val:
                if need.get(id(sem), (None, 0))[1] < val:
                    need[id(sem)] = (sem, val)
        for sid, (sem, val) in need.items():
            wd[sid] = val
            self.E[eng].wait_ge(sem, val)

    def _record(self, tok, r, w):
        for k in r:
            ent = self.res.setdefault(k, [None, []])
            ent[1].append(tok)
        for k in w:
            self.res[k] = [tok, []]

    def op(self, eng, fn, r=(), w=()):
        self._emit_waits(eng, self._deps(r, w))
        self.ecnt[eng] += 1
        sem = self.esem[eng]
        tok = (sem, self.ecnt[eng])
        fn(self.E[eng]).then_inc(sem, 1)
        self._record(tok, r, w)
        return tok

    def pe(self, fns, r=(), w=()):
        eng = "tensor"
        self._emit_waits(eng, self._deps(r, w))
        for fn in fns[:-1]:
            fn(self.E[eng])
        self.ecnt[eng] += 1
        sem = self.esem[eng]
        tok = (sem, self.ecnt[eng])
        fns[-1](self.E[eng]).then_inc(sem, 1)
        self._record(tok, r, w)
        return tok

    def dma(self, qe, out, in_, r=(), w=(), sem=None):
        sem = "k_" + w[0]
        if sem not in self.dsem:
            self.dsem[sem] = self._newsem("d")
            self.dcnt[sem] = 0
        self._emit_waits(qe, self._deps(r, w))
        self.dcnt[sem] += 1
        s = self.dsem[sem]
        tok = (s, 16 * self.dcnt[sem])
        self.E[qe].dma_start(out=out, in_=in_).then_inc(s, 16)
        self._record(tok, r, w)
        return tok

    def barrier(self):
        toks = []
        for e in COMPUTE:
            if self.ecnt[e] > 0:
                toks.append((self.esem[e], self.ecnt[e]))
        for k, s in self.dsem.items():
            if self.dcnt[k] > 0:
                toks.append((s, 16 * self.dcnt[k]))
        for e in ALLQ:
            self._emit_waits(e, toks)
        self.res = {}

    def final_wait(self, qe, toks):
        self._emit_waits(qe, toks)

    def run(self, block):
        q = self.q

        @block.sync
        def _(e):
            for c in q["sync"]:
                c(e)

        @block.scalar
        def _(e):
            for c in q["scalar"]:
                c(e)

        @block.vector
        def _(e):
            for c in q["vector"]:
                c(e)

        @block.gpsimd
        def _(e):
            for c in q["gpsimd"]:
                c(e)

        @block.tensor
        def _(e):
            for c in q["tensor"]:
                c(e)


def bc3(ap2, n):
    return ap2.unsqueeze(2).to_broadcast([ap2.shape[0], ap2.shape[1], n])


def build_nc(nv=NV, nt=NT, do_attn=True, do_post=True, debug=DEBUG, steps=99):
    nc = bass.Bass("TRN2", target_bir_lowering=False)
    dk = "ExternalOutput" if debug else "Internal"

    def din(name, shape, dt=F32):
        return nc.dram_tensor(name, shape, dt, kind="ExternalInput").ap()

    hv = din("hv", [TV, D])
    vmask_d = din("vmask", [128, NV])
    kaug_d = din("kaug", [8, 3, TV], BF16)
    qaug_d = din("qaug", [8, 3, TO], BF16)
    trim_d = din("trim", [128, 128], BF16)
    ident_d = din("ident", [128, 128], BF16)
    umask_d = din("umask", [128, 128])
    slmask_d = din("slmask", [128, 128])
    w_in = din("w_in", [D, 7696])
    norm_mix_g = din("norm_mix_g", [D])
    gate_bias = din("gate_bias", [2048])
    conv_w = din("conv_w", [1536, 4])
    conv_b = din("conv_b", [1536])
    dt_bias = din("dt_bias", [16])
    a_log = din("a_log", [16])
    d_skip = din("d_skip", [16])
    ssd_norm_g = din("ssd_norm_g", [D])
    lq1 = din("lambda_q1", [64]); lk1 = din("lambda_k1", [64])
    lq2 = din("lambda_q2", [64]); lk2 = din("lambda_k2", [64])
    subln_g = din("subln_g", [128])
    w_ssd = din("w_ssd_branch", [D, D])
    w_att = din("w_attn_branch", [D, D])
    w_out = din("w_out", [D, D])
    norm_ffn_g = din("norm_ffn_g", [D])
    w_gate = din("w_gate_ffn", [D, DFF])
    w_up = din("w_up_ffn", [D, DFF])
    w_down = din("w_down_ffn", [DFF, D])
    norm_final_g = din("norm_final_g", [D])

    out_d = nc.dram_tensor("out", [TO, D], F32, kind="ExternalOutput").ap()
    KT = nc.dram_tensor("KT", [8, 128, TV], BF16, kind=dk).ap()
    VS = nc.dram_tensor("VS", [TV, D], BF16, kind=dk).ap()
    QS = nc.dram_tensor("QS", [8, 128, TO], BF16, kind=dk).ap()
    GS = nc.dram_tensor("GS", [TO, 2048], BF16, kind=dk).ap()
    YS = nc.dram_tensor("YS", [TO, D], BF16, kind=dk).ap()
    YA = nc.dram_tensor("YA", [TO, D], BF16, kind=dk).ap()
    H1 = nc.dram_tensor("H1", [TO, D], F32, kind=dk).ap()

    es = ExitStack()
    with es:
        fw = Fw(nc, es)

        def sb(es_, name, shape, dt=F32):
            return es_.enter_context(nc.sbuf_tensor("sb_" + name, shape, dt))

        ident = sb(es, "ident", [128, 128], BF16)
        ps = [es.enter_context(nc.psum_tensor(f"ps{i}", [128, 512], F32)) for i in range(8)]
        PK = [f"ps{i}" for i in range(8)]
        ssq_all = sb(es, "ssq_all", [128, NT])
        lam_t = sb(es, "lam_t", [128, 4])
        fw.dma("sync", ident[:], ident_d, w=["ident"], sem="c0")

        dumped = set()

        def dump(name, ap, key, shape, dt):
            if not debug or name in dumped:
                return
            dumped.add(name)
            dd = nc.dram_tensor("dbg_" + name, shape, dt, kind="ExternalOutput").ap()
            fw.dma("gpsimd", dd, ap, r=[key], w=["dbg_" + name], sem="dbg")

        def psbf(i):
            return ps[i][:].bitcast(BF16)

        def rstd_from_ssq(ssq_ap, out_ap, n, rkeys, wkey, tmp_ap, tmpkey):
            fw.op("scalar", lambda e: e.activation(out=tmp_ap, in_=ssq_ap, func=AF.Ln, scale=1.0 / n, bias=EPS),
                  r=rkeys, w=[tmpkey])
            fw.op("scalar", lambda e: e.activation(out=out_ap, in_=tmp_ap, func=AF.Exp, scale=-0.5),
                  r=[tmpkey], w=[wkey])

        def transposes(src, srckey, nb, bank, dst, dstkey, eng="vector"):
            pv = psbf(bank)
            fns = [(lambda e, j=j: e.transpose(out=pv[:, j * 128:(j + 1) * 128], in_=src[:, j * 128:(j + 1) * 128],
                                               identity=ident[:])) for j in range(nb)]
            sk = list(srckey) if isinstance(srckey, (list, tuple)) else [srckey]
            fw.pe(fns, r=sk + ["ident"], w=[PK[bank]])
            if eng == "vector":
                fw.op("vector", lambda e: e.tensor_copy(out=dst.rearrange("p a b -> p (a b)"), in_=pv[:, 0:nb * 128]),
                      r=[PK[bank]], w=[dstkey])
            else:
                fw.op("scalar", lambda e: e.activation(out=dst.rearrange("p a b -> p (a b)"), in_=pv[:, 0:nb * 128],
                                                       func=AF.Copy), r=[PK[bank]], w=[dstkey])

        def load_w(tile, src2d, ncols, key, kc=NKC, step=4):
            v = src2d.rearrange("(k p) c -> p k c", p=128)
            for k0 in range(0, kc, step):
                k1 = min(kc, k0 + step)
                fw.dma("gpsimd", tile[:, k0:k1, :], v[:, k0:k1, :], w=[key], sem="wload")

        with ExitStack() as p1:
            wA = sb(p1, "wA", [128, NKC, 4624], BF16)
            for k0 in range(0, NKC, 2):
                v = w_in.rearrange("(k p) c -> p k c", p=128)
                fw.dma("gpsimd", wA[:, k0:k0 + 2, 0:2576], v[:, k0:k0 + 2, 0:2576], w=["wA"], sem="wload")
                fw.dma("gpsimd", wA[:, k0:k0 + 2, 2576:4624], v[:, k0:k0 + 2, 3600:5648], w=["wA"], sem="wload")
            gmix = sb(p1, "gmix", [128, D])
            vmask = sb(p1, "vmask", [128, NV])
            dtb = sb(p1, "dtb", [128, 16]); aneg = sb(p1, "aneg", [128, 16]); dsk = sb(p1, "dsk", [128, 16])
            cw = sb(p1, "cw", [128, 12, 4]); cb = sb(p1, "cb", [128, 12])
            diagw = sb(p1, "diagw", [128, 12, 5, 128], BF16)
            ones_bf = sb(p1, "ones_bf", [128, 128], BF16)
            umask = sb(p1, "umask", [128, 128]); slmask = sb(p1, "slmask", [128, 128]); ones_f = sb(p1, "ones_f", [128, 128])
            fw.dma("sync", gmix[:], norm_mix_g.partition_broadcast(128), w=["gmix"], sem="c0")
            fw.dma("sync", vmask[:], vmask_d, w=["vmask"], sem="c0")
            fw.dma("sync", dtb[:], dt_bias.partition_broadcast(128), w=["dtb"], sem="c0")
            fw.dma("sync", aneg[:], a_log.partition_broadcast(128), w=["aneg"], sem="c0")
            fw.dma("sync", dsk[:], d_skip.partition_broadcast(128), w=["dsk"], sem="c0")
            fw.dma("sync", cw[:], conv_w.rearrange("(b p) k -> p b k", p=128), w=["cw"], sem="c0")
            cbv = conv_b.rearrange("(b p o) -> b p o", p=128, o=1)
            for blk in range(12):
                fw.dma("sync", cb[:, blk:blk + 1], cbv[blk], w=["cb"], sem="c0")
            fw.dma("sync", umask[:], umask_d, w=["umask"], sem="c0")
            fw.dma("sync", slmask[:], slmask_d, w=["slmask"], sem="c0")
            fw.op("vector", lambda e: e.memset(ones_bf[:], 1.0), w=["ones_bf"])
            fw.op("vector", lambda e: e.memset(ones_f[:], 1.0), w=["ones_f"])
            fw.op("scalar", lambda e: e.activation(out=aneg[:], in_=aneg[:], func=AF.Exp), r=["aneg"], w=["aneg"])
            fw.op("vector", lambda e: e.tensor_scalar(out=aneg[:], in0=aneg[:], scalar1=-1.0, scalar2=None, op0=ALU.mult),
                  r=["aneg"], w=["aneg"])
            for blk in range(12):
                for k in range(4):
                    fw.op("vector", lambda e, blk=blk, k=k: e.tensor_scalar(
                        out=diagw[:, blk, k, :], in0=ident[:], scalar1=cw[:, blk, k:k + 1], scalar2=None, op0=ALU.mult),
                        r=["ident", "cw"], w=["diagw"])
                fw.op("vector", lambda e, blk=blk: e.tensor_scalar(
                    out=diagw[:, blk, 4, :], in0=ident[:], scalar1=cb[:, blk:blk + 1], scalar2=None, op0=ALU.mult),
                    r=["ident", "cb"], w=["diagw"])

            hc = [sb(p1, f"hc{i}", [128, D]) for i in range(2)]
            sqj = sb(p1, "sqj", [128, D], BF16)
            st4 = sb(p1, "st4", [128, 8])
            ubf = sb(p1, "ubf", [128, D], BF16)
            uT = sb(p1, "uT", [128, NKC, 128], BF16)
            xr = [sb(p1, f"xr{i}", [128, 12, 131], BF16) for i in range(2)]
            kt_sb = sb(p1, "kt_sb", [128, 8, 128], BF16)
            tokb = [sb(p1, f"tokb{i}", [128, 512], BF16) for i in range(2)]
            v_sb = sb(p1, "v_sb", [128, D], BF16)
            ex = sb(p1, "ex", [128, 1536])
            xbcT_l = [sb(p1, f"xbcT{i}", [128, 12, 128], BF16) for i in range(2)]
            xtok = sb(p1, "xtok", [128, D], BF16)
            btok = sb(p1, "btok", [128, 256], BF16)
            dts_l = [sb(p1, f"dts{i}", [128, 8, 16]) for i in range(2)]
            xdd = sb(p1, "xdd", [128, D], BF16)
            state = sb(p1, "state", [128, D])
            statebf = sb(p1, "statebf", [128, D], BF16)
            zc_l = [sb(p1, f"zc{i}", [128, D]) for i in range(2)]; ez_l = [sb(p1, f"ez{i}", [128, D]) for i in range(2)]
            Xm = sb(p1, "Xm", [128, 8, 128])
            dec = sb(p1, "dec", [128, 8, 128], BF16)
            cbm = sb(p1, "cbm", [128, 2, 128], BF16)
            MT = sb(p1, "MT", [128, 16, 128], BF16)
            xdt = sb(p1, "xdt", [128, D], BF16)
            yacc = sb(p1, "yacc", [128, D]); ytmp = sb(p1, "ytmp", [128, D])
            ysb = sb(p1, "ysb", [128, D], BF16)

            fw.op("vector", lambda e: e.memset(state[:], 0.0), w=["state"])
            fw.op("vector", lambda e: e.memset(xr[0][:, :, 0:3], 0.0), w=["xrh0"])

            def load_h(v_):
                fw.dma("sync", hc[v_ % 2][:], hv[v_ * 128:(v_ + 1) * 128, :], w=[f"hc{v_ % 2}"], sem=f"hc{v_ % 2}")

            load_h(0)
            def front1a(v_):
                own = (v_ % 2 == 0)
                t_ = v_ // 2
                sl = v_ % 2
                hck = f"hc{sl}"
                xbcT = xbcT_l[sl]; dts = dts_l[sl]; zc = zc_l[t_ % 2]; ez = ez_l[t_ % 2]
                if v_ + 1 < nv:
                    load_h(v_ + 1)
                fw.op("scalar", lambda e, sl=sl: e.activation(out=sqj[:], in_=hc[sl][:], func=AF.Square, accum_out=st4[:, 0:1]),
                      r=[hck], w=["sqj", "st_ssq"])
                rstd_from_ssq(st4[:, 0:1], st4[:, 2:3], D, ["st_ssq"], "st_rstd", st4[:, 1:2], "st_ln")
                fw.op("vector", lambda e, v_=v_: e.tensor_tensor(out=st4[:, 3:4], in0=st4[:, 2:3], in1=vmask[:, v_:v_ + 1], op=ALU.mult),
                      r=["st_rstd", "vmask"], w=["st_rm"])
                fw.op("vector", lambda e, sl=sl: e.scalar_tensor_tensor(out=ubf[:], in0=hc[sl][:], scalar=st4[:, 3:4], in1=gmix[:],
                                                                      op0=ALU.mult, op1=ALU.mult),
                      r=[hck, "st_rm", "gmix"], w=["ubf"])
                transposes(ubf, "ubf", 8, 0, uT[:], "uT")
                dump("hc", hc[sl][:], hck, [128, D], F32)
                dump("st4", st4[:], "st_rm", [128, 8], F32)
                dump("ubf", ubf[:], "ubf", [128, D], BF16)
                dump("uT", uT[:].rearrange("p a b -> p (a b)"), "uT", [128, D], BF16)
                if steps < 2:
                    return
                xcur = xr[sl]; xnext = xr[1 - sl]
                groups = [("x", 0), ("x", 4), ("x", 8), ("k", 0), ("k", 4)]
                for gi, (kind, b0) in enumerate(groups):
                    bank = 1 + (gi % 2)
                    tb = 3 + (gi % 2)
                    c0 = (1024 + b0 * 128) if kind == "x" else (2576 + b0 * 128)
                    fw.pe([(lambda e, kc=kc: e.matmul(ps[bank][:, :], lhsT=uT[:, kc, :], rhs=wA[:, kc, c0:c0 + 512],
                                                      start=(kc == 0), stop=(kc == NKC - 1))) for kc in range(NKC)],
                          r=["wA", "uT"], w=[PK[bank]])
                    tk = tokb[gi % 2]
                    tkk = f"tokb{gi % 2}"
                    fw.op("vector", lambda e: e.tensor_copy(out=tk[:], in_=ps[bank][:]), r=[PK[bank]], w=[tkk])
                    pvt = psbf(tb)
                    fw.pe([(lambda e, j=j: e.transpose(out=pvt[:, j * 128:(j + 1) * 128], in_=tk[:, j * 128:(j + 1) * 128],
                                                       identity=ident[:])) for j in range(4)], r=[tkk, "ident"], w=[PK[tb]])
                    if kind == "x":
                        fw.op("scalar", lambda e: e.activation(
                            out=xcur[:, b0:b0 + 4, 3:131], in_=pvt[:, 0:512].rearrange("p (a b) -> p a b", a=4), func=AF.Copy),
                            r=[PK[tb]], w=[f"xr{sl}"])
                    else:
                        fw.op("vector", lambda e: e.tensor_copy(
                            out=kt_sb[:, b0:b0 + 4, :], in_=pvt[:, 0:512].rearrange("p (a b) -> p a b", a=4)),
                            r=[PK[tb]], w=["kt_sb"])
                    yield
                fw.dma("sync", KT[:, :, v_ * 128:(v_ + 1) * 128].rearrange("h p t -> p h t"), kt_sb[:],
                       r=["kt_sb"], w=["KT"], sem="st_k")
                fw.op("gpsimd", lambda e, xcur=xcur, xnext=xnext: e.tensor_copy(out=xnext[:, :, 0:3], in_=xcur[:, :, 128:131]),
                      r=[f"xr{sl}"], w=[f"xrh{1 - sl}"])
                if steps < 3:
                    return
                for half in range(2):
                    bank = 1 + half
                    c0 = 2576 + 1024 + half * 512
                    fns = [(lambda e, bank=bank, c0=c0, kc=kc: e.matmul(ps[bank][:, :], lhsT=uT[:, kc, :], rhs=wA[:, kc, c0:c0 + 512],
                                                                        start=(kc == 0), stop=(kc == NKC - 1))) for kc in range(NKC)]
                    fw.pe(fns, r=["wA", "uT"], w=[PK[bank]])
                    fw.op("scalar", lambda e, bank=bank, half=half: e.activation(out=v_sb[:, half * 512:(half + 1) * 512], in_=ps[bank][:],
                                                                                 func=AF.Copy), r=[PK[bank]], w=["v_sb"])
                fw.dma("sync", VS[v_ * 128:(v_ + 1) * 128, :], v_sb[:], r=["v_sb"], w=["VS"], sem="st_v")
                yield
                if steps < 3.3:
                    return
                fns = [(lambda e, kc=kc: e.matmul(ps[5][:, 0:16], lhsT=uT[:, kc, :], rhs=wA[:, kc, 2560:2576],
                                                  start=(kc == 0), stop=(kc == NKC - 1))) for kc in range(NKC)]
                fw.pe(fns, r=["wA", "uT"], w=[PK[5]])
                fw.op("vector", lambda e: e.tensor_tensor(out=dts[:, 0, :], in0=ps[5][:, 0:16], in1=dtb[:], op=ALU.add),
                      r=[PK[5], "dtb"], w=[f"dt_y@{sl}"])
                fw.op("scalar", lambda e: e.activation(out=dts[:, 7, :], in_=dts[:, 0, :], func=AF.Exp), r=[f"dt_y@{sl}"], w=[f"dt_tmp@{sl}"])
                fw.op("scalar", lambda e: e.activation(out=dts[:, 7, :], in_=dts[:, 7, :], func=AF.Ln, bias=1.0), r=[f"dt_tmp@{sl}"], w=[f"dt_tmp@{sl}"])
                fw.op("vector", lambda e, v_=v_: e.tensor_scalar(out=dts[:, 1, :], in0=dts[:, 7, :], scalar1=vmask[:, v_:v_ + 1], scalar2=None,
                                                              op0=ALU.mult), r=[f"dt_tmp@{sl}", "vmask"], w=[f"dt_dt@{sl}"])
                fw.op("vector", lambda e: e.tensor_tensor(out=dts[:, 2, :], in0=dts[:, 1, :], in1=aneg[:], op=ALU.mult),
                      r=[f"dt_dt@{sl}", "aneg"], w=[f"dt_da@{sl}"])
                yield
                if steps < 3.6:
                    return
                if own:
                    for half in range(2):
                        bank = 1 + half
                        c0 = half * 512
                        fns = [(lambda e, bank=bank, c0=c0, kc=kc: e.matmul(ps[bank][:, :], lhsT=uT[:, kc, :], rhs=wA[:, kc, c0:c0 + 512],
                                                                            start=(kc == 0), stop=(kc == NKC - 1))) for kc in range(NKC)]
                        fw.pe(fns, r=["wA", "uT"], w=[PK[bank]])
                        fw.op("scalar", lambda e, bank=bank, half=half: e.activation(out=zc[:, half * 512:(half + 1) * 512], in_=ps[bank][:],
                                                                                     func=AF.Copy), r=[PK[bank]], w=[f"zc@{t_ % 2}"])
                        fw.op("scalar", lambda e, half=half: e.activation(out=ez[:, half * 512:(half + 1) * 512], in_=zc[:, half * 512:(half + 1) * 512],
                                                                          func=AF.Sigmoid), r=[f"zc@{t_ % 2}"], w=[f"ez@{t_ % 2}"])
                if steps < 4:
                    return
                for grp in range(3):
                    bank = (3, 4, 0)[grp]
                    fns = []
                    for j in range(4):
                        blk = grp * 4 + j
                        for k in range(4):
                            fns.append(lambda e, bank=bank, j=j, blk=blk, k=k, xcur=xcur: e.matmul(
                                ps[bank][:, j * 128:(j + 1) * 128], lhsT=diagw[:, blk, k, :], rhs=xcur[:, blk, k:k + 128],
                                start=(k == 0), stop=False))
                        fns.append(lambda e, bank=bank, j=j, blk=blk: e.matmul(
                            ps[bank][:, j * 128:(j + 1) * 128], lhsT=diagw[:, blk, 4, :], rhs=ones_bf[:], start=False, stop=True))
                    fw.pe(fns, r=["diagw", f"xr{sl}", f"xrh{sl}", "ones_bf"], w=[PK[bank]])
                    fw.op("scalar", lambda e, bank=bank, grp=grp: e.activation(out=ex[:, grp * 512:(grp + 1) * 512], in_=ps[bank][:],
                                                                               func=AF.Sigmoid), r=[PK[bank]], w=[f"ex{grp}"])
                    fw.op("vector", lambda e, bank=bank, grp=grp: e.tensor_tensor(
                        out=xbcT[:, grp * 4:(grp + 1) * 4, :].rearrange("p a b -> p (a b)"), in0=ps[bank][:],
                        in1=ex[:, grp * 512:(grp + 1) * 512], op=ALU.mult), r=[PK[bank], f"ex{grp}"], w=[f"xbcT{grp}@{sl}"])
                    yield
            def back1a(v_):
                own = (v_ % 2 == 0)
                t_ = v_ // 2
                sl = v_ % 2
                xbcT = xbcT_l[sl]; dts = dts_l[sl]; zc = zc_l[t_ % 2]; ez = ez_l[t_ % 2]
                if steps < 5:
                    return
                pv3 = psbf(6)
                fns = [(lambda e, j=j: e.transpose(out=pv3[:, j * 128:(j + 1) * 128], in_=xbcT[:, j, :], identity=ident[:])) for j in range(8)]
                fw.pe(fns, r=[f"xbcT0@{sl}", f"xbcT1@{sl}", "ident"], w=[PK[6]])
                fw.op("vector", lambda e: e.tensor_copy(out=xtok[:], in_=pv3[:, :]), r=[PK[6]], w=["xtok"])
                pv4 = psbf(7)
                fns = [(lambda e, j=j: e.transpose(out=pv4[:, j * 128:(j + 1) * 128], in_=xbcT[:, 8 + j, :], identity=ident[:])) for j in range(2)]
                fw.pe(fns, r=[f"xbcT2@{sl}", "ident"], w=[PK[7]])
                fw.op("vector", lambda e: e.tensor_copy(out=btok[:], in_=pv4[:, 0:256]), r=[PK[7]], w=["btok"])
                yield
                if steps < 6:
                    return
                fw.pe([lambda e: e.matmul(ps[5][:, 16:32], lhsT=slmask[:], rhs=dts[:, 2, :], start=True, stop=True),
                       lambda e: e.matmul(ps[5][:, 32:48], lhsT=ones_f[:], rhs=dts[:, 2, :], start=True, stop=True),
                       lambda e: e.matmul(ps[5][:, 48:64], lhsT=umask[:], rhs=dts[:, 2, :], start=True, stop=True)],
                      r=["slmask", "ones_f", "umask", f"dt_da@{sl}"], w=[PK[5]])
                fw.op("scalar", lambda e: e.activation(out=dts[:, 3, :], in_=ps[5][:, 16:32], func=AF.Exp), r=[PK[5]], w=[f"dt_de@{sl}"])
                fw.op("scalar", lambda e: e.activation(out=dts[:, 6, :], in_=ps[5][:, 32:48], func=AF.Exp), r=[PK[5]], w=[f"dt_cd@{sl}"])
                if own:
                    fw.op("scalar", lambda e: e.activation(out=dts[:, 5, :], in_=ps[5][:, 48:64], func=AF.Exp), r=[PK[5]], w=[f"dt_ea@{sl}"])
                fw.op("vector", lambda e: e.tensor_tensor(out=dts[:, 4, :], in0=dts[:, 1, :], in1=dts[:, 3, :], op=ALU.mult),
                      r=[f"dt_dt@{sl}", f"dt_de@{sl}"], w=[f"dt_w1@{sl}"])
                fw.op("vector", lambda e: e.tensor_tensor(out=xdd[:].rearrange("p (h c) -> p h c", h=16),
                                                          in0=xtok[:].rearrange("p (h c) -> p h c", h=16),
                                                          in1=bc3(dts[:, 4, :], 64), op=ALU.mult), r=["xtok", f"dt_w1@{sl}"], w=["xdd"])
                yield
                fw.pe([(lambda e, g=g: e.matmul(ps[6 + g][:, :], lhsT=btok[:, g * 128:(g + 1) * 128], rhs=xdd[:, g * 512:(g + 1) * 512],
                                                start=True, stop=True)) for g in range(2)],
                      r=["btok", "xdd"], w=[PK[6], PK[7]])
                yield
                if own:
                    fw.op("gpsimd", lambda e: e.tensor_copy(out=statebf[:], in_=state[:]), r=["state"], w=["statebf"])
                fw.op("vector", lambda e: e.tensor_tensor(out=state[:].rearrange("p (h c) -> p h c", h=16),
                                                          in0=state[:].rearrange("p (h c) -> p h c", h=16),
                                                          in1=bc3(dts[:, 6, :], 64), op=ALU.mult), r=["state", f"dt_cd@{sl}"], w=["state"])
                for g in range(2):
                    fw.op("vector", lambda e, g=g: e.tensor_tensor(out=state[:, g * 512:(g + 1) * 512], in0=state[:, g * 512:(g + 1) * 512],
                                                                   in1=ps[6 + g][:], op=ALU.add), r=["state", PK[6 + g]], w=["state"])
                if not own:
                    return
                if steps < 7:
                    return
                fw.pe([(lambda e, g=g: e.matmul(ps[6 + g][:, :], lhsT=xbcT[:, 10 + g, :], rhs=statebf[:, g * 512:(g + 1) * 512],
                                                start=True, stop=True)) for g in range(2)],
                      r=[f"xbcT2@{sl}", "statebf"], w=[PK[6], PK[7]])
                for g in range(2):
                    fw.op("vector", lambda e, g=g: e.tensor_tensor(
                        out=yacc[:, g * 512:(g + 1) * 512].rearrange("p (h c) -> p h c", h=8),
                        in0=ps[6 + g][:].rearrange("p (h c) -> p h c", h=8),
                        in1=bc3(dts[:, 5, g * 8:(g + 1) * 8], 64), op=ALU.mult), r=[PK[6 + g], f"dt_ea@{sl}"], w=["yacc"])
                yield
                fw.pe([(lambda e, g=g: e.matmul(ps[5][:, 128 + g * 128:256 + g * 128], lhsT=xbcT[:, 8 + g, :], rhs=xbcT[:, 10 + g, :],
                                                start=True, stop=True)) for g in range(2)], r=[f"xbcT2@{sl}"], w=[PK[5]])
                fw.op("vector", lambda e: e.tensor_tensor(out=cbm[:], in0=ps[5][:, 128:384].rearrange("p (g l) -> p g l", g=2),
                                                          in1=umask[:].unsqueeze(1).to_broadcast([128, 2, 128]), op=ALU.mult),
                      r=[PK[5], "umask"], w=["cbm"])
                fw.op("gpsimd", lambda e: e.tensor_tensor(out=xdt[:].rearrange("p (h c) -> p h c", h=16),
                                                          in0=xtok[:].rearrange("p (h c) -> p h c", h=16),
                                                          in1=bc3(dts[:, 1, :], 64), op=ALU.mult), r=["xtok", f"dt_dt@{sl}"], w=["xdt"])
                for g in range(2):
                    fw.op("vector", lambda e, g=g: e.tensor_tensor(out=Xm[:], in0=bc3(dts[:, 2, g * 8:(g + 1) * 8], 128),
                                                                   in1=umask[:].unsqueeze(1).to_broadcast([128, 8, 128]), op=ALU.mult),
                          r=[f"dt_da@{sl}", "umask"], w=["Xm"])
                    fw.pe([(lambda e, hh=hh: e.matmul(ps[6 + hh][:, :], lhsT=slmask[:],
                                                      rhs=Xm[:, hh * 4:(hh + 1) * 4, :].rearrange("p a b -> p (a b)"),
                                                      start=True, stop=True)) for hh in range(2)],
                          r=["slmask", "Xm"], w=[PK[6], PK[7]])
                    for hh in range(2):
                        fw.op("scalar", lambda e, hh=hh: e.activation(out=dec[:, hh * 4:(hh + 1) * 4, :].rearrange("p a b -> p (a b)"),
                                                                      in_=ps[6 + hh][:], func=AF.Exp), r=[PK[6 + hh]], w=[f"dec{hh}"])
                    fw.op("vector", lambda e, g=g: e.tensor_tensor(out=MT[:, g * 8:(g + 1) * 8, :], in0=dec[:],
                                                                   in1=cbm[:, g, :].unsqueeze(1).to_broadcast([128, 8, 128]), op=ALU.mult),
                          r=["dec0", "dec1", "cbm"], w=[f"MT{g}"])
                    yield
                for g in range(2):
                    fw.pe([(lambda e, g=g, hh=hh: e.matmul(ps[6 + g][:, hh * 64:(hh + 1) * 64], lhsT=MT[:, g * 8 + hh, :],
                                                           rhs=xdt[:, (g * 8 + hh) * 64:(g * 8 + hh + 1) * 64], start=True, stop=True))
                           for hh in range(8)], r=[f"MT{g}", "xdt"], w=[PK[6 + g]])
                    fw.op("vector", lambda e, g=g: e.tensor_tensor(out=yacc[:, g * 512:(g + 1) * 512], in0=yacc[:, g * 512:(g + 1) * 512],
                                                                   in1=ps[6 + g][:], op=ALU.add), r=["yacc", PK[6 + g]], w=["yacc"])
                    yield
                fw.op("gpsimd", lambda e: e.tensor_tensor(out=ytmp[:].rearrange("p (h c) -> p h c", h=16),
                                                          in0=xtok[:].rearrange("p (h c) -> p h c", h=16),
                                                          in1=bc3(dsk[:], 64), op=ALU.mult), r=["xtok", "dsk"], w=["ytmp"])
                fw.op("vector", lambda e: e.tensor_tensor(out=yacc[:], in0=yacc[:], in1=ytmp[:], op=ALU.add), r=["yacc", "ytmp"], w=["yacc"])
                fw.op("gpsimd", lambda e: e.tensor_tensor(out=zc[:], in0=zc[:], in1=ez[:], op=ALU.mult), r=[f"zc@{t_ % 2}", f"ez@{t_ % 2}"], w=[f"zc@{t_ % 2}"])
                fw.op("vector", lambda e: e.tensor_tensor(out=yacc[:], in0=yacc[:], in1=zc[:], op=ALU.mult), r=["yacc", f"zc@{t_ % 2}"], w=["yacc"])
                fw.op("scalar", lambda e, t_=t_: e.activation(out=ytmp[:], in_=yacc[:], func=AF.Square, accum_out=ssq_all[:, t_:t_ + 1]),
                      r=["yacc"], w=["ytmp", "ssq_all"])
                fw.op("gpsimd", lambda e: e.tensor_copy(out=ysb[:], in_=yacc[:]), r=["yacc"], w=["ysb"])
                fw.dma("sync", YS[t_ * 128:(t_ + 1) * 128, :], ysb[:], r=["ysb"], w=["YS"], sem="st_y")

            def drain(g):
                for _ in g:
                    pass

            def interleave(ga, gb):
                a_alive, b_alive = True, True
                while a_alive or b_alive:
                    if a_alive:
                        try:
                            next(ga)
                        except StopIteration:
                            a_alive = False
                    if b_alive:
                        try:
                            next(gb)
                        except StopIteration:
                            b_alive = False

            drain(front1a(0))
            for v_ in range(nv):
                if v_ + 1 < nv:
                    interleave(back1a(v_), front1a(v_ + 1))
                else:
                    drain(back1a(v_))
            fw.barrier()
        fw.new_epoch()

        with ExitStack() as p1:
            wB = sb(p1, "wB", [128, NKC, 3072], BF16)
            v = w_in.rearrange("(k p) c -> p k c", p=128)
            for k0 in range(0, NKC, 2):
                fw.dma("gpsimd", wB[:, k0:k0 + 2, 0:1024], v[:, k0:k0 + 2, 2576:3600], w=["wB"], sem="wload")
                fw.dma("gpsimd", wB[:, k0:k0 + 2, 1024:3072], v[:, k0:k0 + 2, 5648:7696], w=["wB"], sem="wload")
            gmix = sb(p1, "gmixb", [128, D])
            gbias = sb(p1, "gbias", [128, 2048])
            vmask = sb(p1, "vmaskb", [128, NV])
            fw.dma("sync", gmix[:], norm_mix_g.partition_broadcast(128), w=["gmix"], sem="c0")
            fw.dma("sync", gbias[:], gate_bias.partition_broadcast(128), w=["gbias"], sem="c0")
            fw.dma("sync", vmask[:], vmask_d, w=["vmask"], sem="c0")
            hc = [sb(p1, f"hcb{i}", [128, D]) for i in range(2)]
            sqj = sb(p1, "sqjb", [128, D], BF16)
            st4 = sb(p1, "st4b", [128, 8])
            ubf = sb(p1, "ubfb", [128, D], BF16)
            uT = sb(p1, "uTb", [128, NKC, 128], BF16)
            q_sb = sb(p1, "q_sb", [128, 8, 128], BF16)
            gsf = sb(p1, "gsf", [128, 2048])
            gsb = sb(p1, "gsb", [128, 2048], BF16)

            def load_hb(t_):
                v_ = 2 * t_
                fw.dma("sync", hc[t_ % 2][:], hv[v_ * 128:(v_ + 1) * 128, :], w=[f"hc{t_ % 2}"], sem=f"hcb{t_ % 2}")

            load_hb(0)
            for t_ in range(nt if steps >= 8 else 0):
                v_ = 2 * t_
                sl = t_ % 2
                hck = f"hc{sl}"
                if t_ + 1 < nt:
                    load_hb(t_ + 1)
                fw.op("scalar", lambda e, sl=sl: e.activation(out=sqj[:], in_=hc[sl][:], func=AF.Square, accum_out=st4[:, 0:1]),
                      r=[hck], w=["sqj", "st_ssq"])
                rstd_from_ssq(st4[:, 0:1], st4[:, 2:3], D, ["st_ssq"], "st_rstd", st4[:, 1:2], "st_ln")
                fw.op("vector", lambda e, v_=v_: e.tensor_tensor(out=st4[:, 3:4], in0=st4[:, 2:3], in1=vmask[:, v_:v_ + 1], op=ALU.mult),
                      r=["st_rstd", "vmask"], w=["st_rm"])
                fw.op("vector", lambda e, sl=sl: e.scalar_tensor_tensor(out=ubf[:], in0=hc[sl][:], scalar=st4[:, 3:4], in1=gmix[:],
                                                                      op0=ALU.mult, op1=ALU.mult),
                      r=[hck, "st_rm", "gmix"], w=["ubf"])
                transposes(ubf, "ubf", 8, 0, uT[:], "uT")
                for gi in range(2):
                    bank = 1 + gi
                    fns = []
                    for j in range(4):
                        c0 = (gi * 4 + j) * 128
                        for kc in range(NKC):
                            fns.append(lambda e, bank=bank, j=j, c0=c0, kc=kc: e.matmul(
                                ps[bank][:, j * 128:(j + 1) * 128], lhsT=wB[:, kc, c0:c0 + 128], rhs=uT[:, kc, :],
                                start=(kc == 0), stop=(kc == NKC - 1)))
                    fw.pe(fns, r=["wB", "uT"], w=[PK[bank]])
                    fw.op("vector", lambda e, bank=bank, gi=gi: e.tensor_copy(
                        out=q_sb[:, gi * 4:(gi + 1) * 4, :], in_=ps[bank][:].rearrange("p (a b) -> p a b", a=4)),
                        r=[PK[bank]], w=["q_sb"])
                fw.dma("sync", QS[:, :, t_ * 128:(t_ + 1) * 128].rearrange("h p t -> p h t"), q_sb[:], r=["q_sb"], w=["QS"], sem="st_q")
                for gi in range(4):
                    bank = 3 + gi
                    c0 = 1024 + gi * 512
                    fns = [(lambda e, bank=bank, c0=c0, kc=kc: e.matmul(ps[bank][:, :], lhsT=uT[:, kc, :], rhs=wB[:, kc, c0:c0 + 512],
                                                                        start=(kc == 0), stop=(kc == NKC - 1))) for kc in range(NKC)]
                    fw.pe(fns, r=["wB", "uT"], w=[PK[bank]])
                    sl_ = slice(gi * 512, (gi + 1) * 512)
                    fw.op("vector", lambda e, bank=bank, sl_=sl_: e.tensor_tensor(out=gsf[:, sl_], in0=ps[bank][:], in1=gbias[:, sl_], op=ALU.add),
                          r=[PK[bank], "gbias"], w=[f"gsf{gi}"])
                    fw.op("scalar", lambda e, sl_=sl_: e.activation(out=gsb[:, sl_], in_=gsf[:, sl_], func=AF.Sigmoid),
                          r=[f"gsf{gi}"], w=[f"gsb{gi}"])
                fw.dma("sync", GS[t_ * 128:(t_ + 1) * 128, :], gsb[:], r=[f"gsb{i}" for i in range(4)], w=["GS"], sem="st_g")
            fw.barrier()
        fw.new_epoch()

        p23 = ExitStack()
        p23.__enter__()
        wS = sb(p23, "wS", [128, NKC, D], BF16); wT = sb(p23, "wT", [128, NKC, D], BF16); wO = sb(p23, "wO", [128, NKC, D], BF16)
        if do_post:
            load_w(wS, w_ssd, D, "wS"); load_w(wT, w_att, D, "wT"); load_w(wO, w_out, D, "wO")
        if do_attn:
          with ExitStack() as p2:
            KTa = [sb(p2, f"KTa{i}", [67, 2, TV], BF16) for i in range(2)]
            QTa = [sb(p2, f"QTa{i}", [67, 2, TO], BF16) for i in range(2)]
            Va = [sb(p2, f"Va{i}", [128, NV, 130], BF16) for i in range(2)]
            trim = sb(p2, "trim", [128, 128], BF16)
            sg = sb(p2, "sg", [128, 128])
            lw = sb(p2, "lw", [128, 4, 64])
            Pt = [sb(p2, f"Pt{i}", [128, 512], BF16) for i in range(4)]
            OSETS = [[4, 5], [6, 7]]
            SBANKS = [0, 1, 2, 3]
            ogrp = 0
            av = sb(p2, "av", [128, 128]); avj = sb(p2, "avj", [128, 128])
            ast = sb(p2, "ast", [128, 8])
            yab = [sb(p2, f"yab{i}", [128, 128], BF16) for i in range(2)]
            fw.dma("sync", trim[:], trim_d, w=["trim"], sem="c0")
            fw.dma("sync", sg[:], subln_g.partition_broadcast(128), w=["sg"], sem="c0")
            fw.op("vector", lambda e: e.tensor_scalar(out=sg[:], in0=sg[:], scalar1=0.8, scalar2=None, op0=ALU.mult), r=["sg"], w=["sg"])
            for i, l_ in enumerate((lq1, lk1, lq2, lk2)):
                fw.dma("sync", lw[:, i, :], l_.partition_broadcast(128), w=["lw"], sem="c0")
            fw.op("vector", lambda e: e.tensor_tensor(out=lw[:, 0, :], in0=lw[:, 0, :], in1=lw[:, 1, :], op=ALU.mult), r=["lw"], w=["lw"])
            fw.op("vector", lambda e: e.tensor_tensor(out=lw[:, 2, :], in0=lw[:, 2, :], in1=lw[:, 3, :], op=ALU.mult), r=["lw"], w=["lw"])
            fw.op("vector", lambda e: e.reduce_sum(out=lam_t[:, 0:1], in_=lw[:, 0, :], axis=mybir.AxisListType.X), r=["lw"], w=["lam"])
            fw.op("vector", lambda e: e.reduce_sum(out=lam_t[:, 1:2], in_=lw[:, 2, :], axis=mybir.AxisListType.X), r=["lw", "lam"], w=["lam"])
            fw.op("scalar", lambda e: e.activation(out=lam_t[:, 0:2], in_=lam_t[:, 0:2], func=AF.Exp), r=["lam"], w=["lam"])
            fw.op("vector", lambda e: e.tensor_tensor(out=lam_t[:, 2:3], in0=lam_t[:, 1:2], in1=lam_t[:, 0:1], op=ALU.subtract), r=["lam"], w=["lam"])
            fw.op("vector", lambda e: e.tensor_scalar(out=lam_t[:, 3:4], in0=lam_t[:, 2:3], scalar1=-0.2, scalar2=None, op0=ALU.add),
                  r=["lam"], w=["lam"])
            for i in range(2):
                fw.op("vector", lambda e, i=i: e.memset(Va[i][:, :, 128:130], 1.0), w=[f"Va{i}"])

            def load_head(h):
                s_ = h % 2
                fw.dma("sync", KTa[s_][0:64, :, :], KT[h].rearrange("(c d) t -> d c t", c=2), r=["KT"], w=[f"KTa{s_}"], sem=f"hd{s_}")
                for c in range(2):
                    fw.dma("sync", KTa[s_][64:67, c, :], kaug_d[h], w=[f"KTa{s_}"], sem=f"hd{s_}")
                    fw.dma("sync", QTa[s_][64:67, c, :], qaug_d[h], w=[f"QTa{s_}"], sem=f"hd{s_}")
                fw.dma("sync", QTa[s_][0:64, :, :], QS[h].rearrange("(c d) t -> d c t", c=2), r=["QS"], w=[f"QTa{s_}"], sem=f"hd{s_}")
                vsv = VS[:, h * 128:(h + 1) * 128].rearrange("(v p) e -> p v e", p=128)
                for v0 in range(0, NV, 13):
                    fw.dma("sync", Va[s_][:, v0:v0 + 13, 0:128], vsv[:, v0:v0 + 13, :], r=["VS"], w=[f"Va{s_}"], sem=f"hd{s_}")

            load_head(0)
            sidx = 0
            oidx = 0
            for h in range(8):
                s_ = h % 2
                if h + 1 < 8:
                    load_head(h + 1)
                K_, Q_, V_ = KTa[s_], QTa[s_], Va[s_]
                rk = [f"KTa{s_}", f"QTa{s_}"]
                for g0 in range(0, nt, 2):
                    G = min(2, nt - g0)
                    t0 = g0
                    oset = OSETS[ogrp % 2]; ogrp += 1
                    started = set()
                    kb_max = 2 * (t0 + G - 1)

                    def oslot(j, c):
                        idx = c * G + j
                        return oset[idx // 3], (idx % 3) * 130

                    def jmin_of(kb):
                        return 0 if kb <= 2 * t0 else 1

                    def emit_qk(kb, sbk):
                        jm = jmin_of(kb)
                        dj = None
                        if kb >= 2 * t0 and (kb - 2 * t0) % 2 == 0:
                            dj = (kb - 2 * t0) // 2
                        fns = []
                        for c in range(2):
                            base = c * 256
                            fns.append(lambda e, sbk=sbk, base=base, jm=jm, c=c, kb=kb, dj=dj: e.matmul(
                                ps[sbk][:, base + jm * 128:base + G * 128], lhsT=K_[:, c, kb * 128:(kb + 1) * 128],
                                rhs=Q_[:, c, (t0 + jm) * 128:(t0 + G) * 128], start=True, stop=(dj is None)))
                            if dj is not None:
                                fns.append(lambda e, sbk=sbk, base=base, dj=dj: e.matmul(
                                    ps[sbk][:, base + dj * 128:base + (dj + 1) * 128], lhsT=ident[:], rhs=trim[:],
                                    start=False, stop=True))
                        fw.pe(fns, r=rk + ["ident", "trim"], w=[PK[sbk]])

                    def emit_act_av(kb, sbk):
                        jm = jmin_of(kb)

                        def view(ap):
                            return ap.rearrange("p (c j q) -> p c j q", c=2, j=2)[:, :, jm:G, :]
                        fw.op("scalar", lambda e: e.activation(out=view(Pt[sbk][:, 0:512]), in_=view(ps[sbk][:, 0:512]),
                                                               func=AF.Exp, scale=0.125), r=[PK[sbk]], w=[f"Pt{sbk}"])
                        fns = []
                        for c in range(2):
                            for j in range(jm, G):
                                bk, c0 = oslot(j, c)
                                st = bk not in started
                                started.add(bk)
                                col = c * 256 + j * 128
                                fns.append(lambda e, bk=bk, c0=c0, st=st, col=col, j=j: e.matmul(
                                    ps[bk][:, c0:c0 + 130], lhsT=Pt[sbk][:, col:col + 128], rhs=V_[:, kb, :],
                                    start=st, stop=(kb == 2 * (t0 + j)), skip_group_check=True))
                        fw.pe(fns, r=[f"Pt{sbk}", f"Va{s_}"], w=[PK[b_] for b_ in oset])

                    sbs = {}
                    for kb in range(min(2, kb_max + 1)):
                        sbs[kb] = SBANKS[sidx % 4]; sidx += 1
                        emit_qk(kb, sbs[kb])
                    for kb in range(kb_max + 1):
                        if kb + 2 <= kb_max:
                            sbs[kb + 2] = SBANKS[sidx % 4]; sidx += 1
                            emit_qk(kb + 2, sbs[kb + 2])
                        emit_act_av(kb, sbs[kb])
                    for j in range(G):
                        t_ = t0 + j
                        b0_, c0_ = oslot(j, 0)
                        b1_, c1_ = oslot(j, 1)
                        ya = yab[t_ % 2]
                        yk = f"yab{t_ % 2}"
                        fw.op("vector", lambda e: e.tensor_scalar(out=ast[:, 0:1], in0=ps[b0_][:, c0_ + 128:c0_ + 129], scalar1=1e-30,
                                                                  scalar2=None, op0=ALU.max), r=[PK[b0_]], w=["ast_r"])
                        fw.op("vector", lambda e: e.tensor_scalar(out=ast[:, 1:2], in0=ps[b1_][:, c1_ + 128:c1_ + 129], scalar1=1e-30,
                                                                  scalar2=None, op0=ALU.max), r=[PK[b1_]], w=["ast_r1"])
                        fw.op("vector", lambda e: e.reciprocal(out=ast[:, 0:2], in_=ast[:, 0:2]), r=["ast_r", "ast_r1"], w=["ast_r", "ast_r1"])
                        fw.op("vector", lambda e: e.tensor_tensor(out=ast[:, 2:3], in0=ast[:, 1:2], in1=lam_t[:, 3:4], op=ALU.mult),
                              r=["ast_r1", "lam"], w=["ast_r2"])
                        fw.op("vector", lambda e: e.tensor_scalar(out=av[:], in0=ps[b0_][:, c0_:c0_ + 128], scalar1=ast[:, 0:1], scalar2=None,
                                                                  op0=ALU.mult), r=[PK[b0_], "ast_r"], w=["av"])
                        fw.op("vector", lambda e: e.scalar_tensor_tensor(out=av[:], in0=ps[b1_][:, c1_:c1_ + 128], scalar=ast[:, 2:3], in1=av[:],
                                                                        op0=ALU.mult, op1=ALU.add), r=[PK[b1_], "ast_r2", "av"], w=["av"])
                        fw.op("scalar", lambda e: e.activation(out=avj[:], in_=av[:], func=AF.Square, accum_out=ast[:, 4:5]), r=["av"], w=["avj", "ast_s"])
                        rstd_from_ssq(ast[:, 4:5], ast[:, 6:7], 128, ["ast_s"], "ast_rs", ast[:, 5:6], "ast_ln")
                        fw.op("vector", lambda e: e.scalar_tensor_tensor(out=ya[:], in0=av[:], scalar=ast[:, 6:7], in1=sg[:],
                                                                        op0=ALU.mult, op1=ALU.mult), r=["av", "ast_rs", "sg"], w=[yk])
                        fw.dma("gpsimd", YA[t_ * 128:(t_ + 1) * 128, h * 128:(h + 1) * 128], ya[:], r=[yk], w=["YA"], sem=yk)
            fw.barrier()
          fw.new_epoch()

        last_tok = []
        if do_post:
          with ExitStack() as p3:
            gss = sb(p3, "gss", [128, D])
            fw.dma("sync", gss[:], ssd_norm_g.partition_broadcast(128), w=["gss"], sem="c0")
            ys_t = [sb(p3, f"ys{i}", [128, D], BF16) for i in range(2)]
            ya_t = [sb(p3, f"ya{i}", [128, D], BF16) for i in range(2)]
            gs_t = [sb(p3, f"gs{i}", [128, 2048], BF16) for i in range(2)]
            h_t = [sb(p3, f"h3_{i}", [128, D]) for i in range(2)]
            st3 = sb(p3, "st3", [128, 4])
            ysn = sb(p3, "ysn", [128, D], BF16)
            ysT = sb(p3, "ysT", [128, NKC, 128], BF16); yaT = sb(p3, "yaT", [128, NKC, 128], BF16)
            m1 = sb(p3, "m1", [128, D]); m2 = sb(p3, "m2", [128, D]); mb = sb(p3, "mb", [128, D], BF16)
            mT = sb(p3, "mT", [128, NKC, 128], BF16)
            h1s = sb(p3, "h1s", [128, D])

            def load3(t_):
                s_ = t_ % 2
                rows = slice(t_ * 128, (t_ + 1) * 128)
                fw.dma("sync", ys_t[s_][:], YS[rows, :], r=["YS"], w=[f"ys{s_}"], sem=f"l3{s_}")
                fw.dma("sync", ya_t[s_][:], YA[rows, :], r=["YA"], w=[f"ya{s_}"], sem=f"l3{s_}")
                fw.dma("sync", gs_t[s_][:], GS[rows, :], r=["GS"], w=[f"gs{s_}"], sem=f"l3{s_}")
                fw.dma("sync", h_t[s_][:], hv[2 * t_ * 128:(2 * t_ + 1) * 128, :], w=[f"h3{s_}"], sem=f"l3{s_}")

            load3(0)
            for t_ in range(nt):
                s_ = t_ % 2
                if t_ + 1 < nt:
                    load3(t_ + 1)
                rstd_from_ssq(ssq_all[:, t_:t_ + 1], st3[:, 1:2], D, ["ssq_all"], "st3_r", st3[:, 0:1], "st3_l")
                fw.op("vector", lambda e, s_=s_: e.scalar_tensor_tensor(out=ysn[:], in0=ys_t[s_][:], scalar=st3[:, 1:2], in1=gss[:],
                                                                       op0=ALU.mult, op1=ALU.mult), r=[f"ys{s_}", "st3_r", "gss"], w=["ysn"])
                transposes(ysn, "ysn", 8, 0, ysT[:], "ysT")
                transposes(ya_t[s_], f"ya{s_}", 8, 1, yaT[:], "yaT", eng="scalar")
                for half in range(2):
                    cs = slice(half * 512, (half + 1) * 512)
                    fw.pe([(lambda e, kc=kc, half=half, cs=cs: e.matmul(ps[2 + half][:, :], lhsT=ysT[:, kc, :], rhs=wS[:, kc, cs],
                                                                       start=(kc == 0), stop=(kc == NKC - 1))) for kc in range(NKC)],
                          r=["ysT", "wS"], w=[PK[2 + half]])
                    fw.pe([(lambda e, kc=kc, half=half, cs=cs: e.matmul(ps[4 + half][:, :], lhsT=yaT[:, kc, :], rhs=wT[:, kc, cs],
                                                                       start=(kc == 0), stop=(kc == NKC - 1))) for kc in range(NKC)],
                          r=["yaT", "wT"], w=[PK[4 + half]])
                    fw.op("vector", lambda e, half=half, cs=cs, s_=s_: e.tensor_tensor(out=m1[:, cs], in0=ps[2 + half][:], in1=gs_t[s_][:, cs], op=ALU.mult),
                          r=[PK[2 + half], f"gs{s_}"], w=[f"m1{half}"])
                    fw.op("vector", lambda e, half=half, cs=cs, s_=s_: e.tensor_tensor(
                        out=m2[:, cs], in0=ps[4 + half][:], in1=gs_t[s_][:, 1024 + half * 512:1024 + (half + 1) * 512], op=ALU.mult),
                        r=[PK[4 + half], f"gs{s_}"], w=[f"m2{half}"])
                    fw.op("vector", lambda e, cs=cs: e.tensor_tensor(out=mb[:, cs], in0=m1[:, cs], in1=m2[:, cs], op=ALU.add),
                          r=[f"m1{half}", f"m2{half}"], w=[f"mb{half}"])
                transposes(mb, ["mb0", "mb1"], 8, 6, mT[:], "mT")
                for half in range(2):
                    cs = slice(half * 512, (half + 1) * 512)
                    fw.pe([(lambda e, kc=kc, half=half, cs=cs: e.matmul(ps[2 + half][:, :], lhsT=mT[:, kc, :], rhs=wO[:, kc, cs],
                                                                       start=(kc == 0), stop=(kc == NKC - 1))) for kc in range(NKC)],
                          r=["mT", "wO"], w=[PK[2 + half]])
                    fw.op("vector", lambda e, half=half, cs=cs, s_=s_: e.tensor_tensor(out=h1s[:, cs], in0=ps[2 + half][:], in1=h_t[s_][:, cs], op=ALU.add),
                          r=[PK[2 + half], f"h3{s_}"], w=["h1s"])
                fw.dma("sync", H1[t_ * 128:(t_ + 1) * 128, :], h1s[:], r=["h1s"], w=["H1"], sem="st_h1")
            fw.barrier()
          fw.new_epoch()

          p23.close()
          with ExitStack() as p4:
            wG = sb(p4, "wG", [128, NKC, DFF], BF16); wU = sb(p4, "wU", [128, NKC, DFF], BF16)
            wD = sb(p4, "wD", [128, 22, D], BF16)
            load_w(wG, w_gate, DFF, "wG", step=2); load_w(wU, w_up, DFF, "wU", step=2)
            load_w(wD, w_down, D, "wD", kc=22, step=6)
            gff = sb(p4, "gff", [128, D]); gfin = sb(p4, "gfin", [128, D])
            fw.dma("sync", gff[:], norm_ffn_g.partition_broadcast(128), w=["gff"], sem="c0")
            fw.dma("sync", gfin[:], norm_final_g.partition_broadcast(128), w=["gfin"], sem="c0")
            h1_t = [sb(p4, f"h1_{i}", [128, D]) for i in range(3)]
            sqj = sb(p4, "sqj4", [128, D], BF16)
            sqj2 = sb(p4, "sqj42", [128, D], BF16)
            st4 = sb(p4, "st44", [128, 8])
            u2 = sb(p4, "u2", [128, D], BF16)
            u2T_l = [sb(p4, f"u2T{i}", [128, NKC, 128], BF16) for i in range(2)]
            eg = sb(p4, "eg", [128, 512]); gg = sb(p4, "gg", [128, 512])
            act_l = [sb(p4, f"act{i}", [128, DFF], BF16) for i in range(2)]
            actT_l = [sb(p4, f"actT{i}", [128, 22, 128], BF16) for i in range(2)]
            h2 = sb(p4, "h2", [128, D])
            ob_ = [sb(p4, f"ob{i}", [128, D]) for i in range(2)]

            def load4(t_):
                fw.dma("sync", h1_t[t_ % 3][:], H1[t_ * 128:(t_ + 1) * 128, :], r=["H1"], w=[f"h1{t_ % 3}"], sem=f"l4{t_ % 3}")

            cgs = [(i * 512, min(DFF, (i + 1) * 512)) for i in range(6)]

            def front4(t_):
                s_ = t_ % 2
                if t_ + 1 < nt:
                    load4(t_ + 1)
                h1 = h1_t[t_ % 3]
                h1k = f"h1{t_ % 3}"
                fw.op("scalar", lambda e: e.activation(out=sqj[:], in_=h1[:], func=AF.Square, accum_out=st4[:, 0:1]),
                      r=[h1k], w=["sqj", "st_ssq"])
                rstd_from_ssq(st4[:, 0:1], st4[:, 2:3], D, ["st_ssq"], "st_rstd", st4[:, 1:2], "st_ln")
                fw.op("vector", lambda e: e.scalar_tensor_tensor(out=u2[:], in0=h1[:], scalar=st4[:, 2:3], in1=gff[:],
                                                                 op0=ALU.mult, op1=ALU.mult), r=[h1k, "st_rstd", "gff"], w=["u2"])
                transposes(u2, "u2", 8, 0, u2T_l[s_][:], f"u2T{s_}")
                yield
                u2T = u2T_l[s_]
                u2Tk = f"u2T{s_}"
                act = act_l[s_]
                for gi, (c0, c1) in enumerate(cgs):
                    w_ = c1 - c0
                    bg = 1 + (gi % 2) * 2
                    bu = bg + 1
                    fw.pe([(lambda e, kc=kc: e.matmul(ps[bg][:, 0:w_], lhsT=u2T[:, kc, :], rhs=wG[:, kc, c0:c1],
                                                      start=(kc == 0), stop=(kc == NKC - 1))) for kc in range(NKC)],
                          r=[u2Tk, "wG"], w=[PK[bg]])
                    fw.pe([(lambda e, kc=kc: e.matmul(ps[bu][:, 0:w_], lhsT=u2T[:, kc, :], rhs=wU[:, kc, c0:c1],
                                                      start=(kc == 0), stop=(kc == NKC - 1))) for kc in range(NKC)],
                          r=[u2Tk, "wU"], w=[PK[bu]])
                    fw.op("scalar", lambda e: e.activation(out=gg[:, 0:w_], in_=ps[bg][:, 0:w_], func=AF.Silu),
                          r=[PK[bg]], w=["gg"])
                    fw.op("vector", lambda e: e.tensor_tensor(out=act[:, c0:c1], in0=ps[bu][:, 0:w_], in1=gg[:, 0:w_], op=ALU.mult),
                          r=[PK[bu], "gg"], w=[f"act{s_}_{gi}"])
                    yield

            def back4(t_):
                s_ = t_ % 2
                h1 = h1_t[t_ % 3]
                h1k = f"h1{t_ % 3}"
                act = act_l[s_]
                actT = actT_l[s_]
                for bi, (b0, b1) in enumerate(((0, 8), (8, 16), (16, 22))):
                    bank = 5 + bi
                    pv = psbf(bank)
                    fw.pe([(lambda e, j=j: e.transpose(out=pv[:, (j - b0) * 128:(j - b0 + 1) * 128],
                                                       in_=act[:, j * 128:(j + 1) * 128], identity=ident[:])) for j in range(b0, b1)],
                          r=[f"act{s_}_{i}" for i in range(6)] + ["ident"], w=[PK[bank]])
                    n_ = (b1 - b0) * 128
                    if bi == 1:
                        fw.op("scalar", lambda e: e.activation(
                            out=actT[:, b0:b1, :].rearrange("p a b -> p (a b)"), in_=pv[:, 0:n_], func=AF.Copy), r=[PK[bank]], w=[f"actT{s_}_{bi}"])
                    else:
                        fw.op("vector", lambda e: e.tensor_copy(
                            out=actT[:, b0:b1, :].rearrange("p a b -> p (a b)"), in_=pv[:, 0:n_]), r=[PK[bank]], w=[f"actT{s_}_{bi}"])
                    yield
                for half in range(2):
                    cs = slice(half * 512, (half + 1) * 512)
                    bank = 5 + half
                    fw.pe([(lambda e, kc=kc: e.matmul(ps[bank][:, :], lhsT=actT[:, kc, :], rhs=wD[:, kc, cs],
                                                      start=(kc == 0), stop=(kc == 21))) for kc in range(22)],
                          r=[f"actT{s_}_0", f"actT{s_}_1", f"actT{s_}_2", "wD"], w=[PK[bank]])
                    fw.op("vector", lambda e: e.tensor_tensor(out=h2[:, cs], in0=ps[bank][:], in1=h1[:, cs], op=ALU.add),
                          r=[PK[bank], h1k], w=["h2"])
                    yield
                fw.op("scalar", lambda e: e.activation(out=sqj2[:], in_=h2[:], func=AF.Square, accum_out=st4[:, 4:5]), r=["h2"], w=["sqj2", "st_ssq2"])
                rstd_from_ssq(st4[:, 4:5], st4[:, 6:7], D, ["st_ssq2"], "st_rstd2", st4[:, 5:6], "st_ln2")
                o_ = ob_[s_]
                fw.op("vector", lambda e: e.scalar_tensor_tensor(out=o_[:], in0=h2[:], scalar=st4[:, 6:7], in1=gfin[:],
                                                                 op0=ALU.mult, op1=ALU.mult), r=["h2", "st_rstd2", "gfin"], w=[f"ob{s_}"])
                last_tok.append(fw.dma("sync", out_d[t_ * 128:(t_ + 1) * 128, :], o_[:], r=[f"ob{s_}"], w=["out"], sem=f"st_o{s_}"))

            def drain4(g):
                for _ in g:
                    pass

            def interleave4(ga, gb):
                a_alive, b_alive = True, True
                while a_alive or b_alive:
                    if a_alive:
                        try:
                            next(ga)
                        except StopIteration:
                            a_alive = False
                    if b_alive:
                        try:
                            next(gb)
                        except StopIteration:
                            b_alive = False

            load4(0)
            drain4(front4(0))
            for t_ in range(nt):
                if t_ + 1 < nt:
                    interleave4(back4(t_), front4(t_ + 1))
                else:
                    drain4(back4(t_))
            fw.barrier()
        fw.barrier()
    return nc


SLOPES = [2.0 ** (-(h + 1)) for h in range(8)]


def _core_inputs(b, j, inputs, consts):
    x = inputs["x"][b]
    meta = inputs["meta_tokens"]
    hvirt = np.zeros((TV, D), np.float32)
    valid = np.ones((TV,), np.float32)
    if j == 0:
        hvirt[112:128] = meta
        hvirt[128:] = x
        valid[:112] = 0
    else:
        hvirt[240:256] = meta
        hvirt[256:] = x[:63 * 128]
        valid[:240] = 0
    kaug = np.zeros((8, 3, TV), np.float32)
    p = np.arange(TV) % 128
    v = np.arange(TV) // 128
    for h in range(8):
        kaug[h, 0] = 8 * SLOPES[h] * (p - 127) + np.where(valid > 0, 0.0, NEGB)
        kaug[h, 1] = 8 * SLOPES[h] * 128 * v
        kaug[h, 2] = 1.0
    m = dict(consts)
    m["hv"] = hvirt
    m["vmask"] = np.ascontiguousarray(valid.reshape(NV, 128).T)
    m["kaug"] = kaug.astype(ml_dtypes.bfloat16)
    return m


def _consts(inputs):
    qaug = np.zeros((8, 3, TO), np.float32)
    vq = 2 * (np.arange(TO) // 128)
    for h in range(8):
        qaug[h, 0] = 1.0
        qaug[h, 1] = 1.0
        qaug[h, 2] = -8 * SLOPES[h] * 128 * vq
    kk = np.arange(128)
    c = {
        "qaug": qaug.astype(ml_dtypes.bfloat16),
        "trim": np.where(kk[:, None] > kk[None, :], NEGB, 0.0).astype(ml_dtypes.bfloat16),
        "ident": np.eye(128, dtype=np.float32).astype(ml_dtypes.bfloat16),
        "umask": (kk[:, None] <= kk[None, :]).astype(np.float32),
        "slmask": (kk[:, None] > kk[None, :]).astype(np.float32),
    }
    for name in ("w_in", "norm_mix_g", "gate_bias", "conv_w", "conv_b", "dt_bias", "a_log", "d_skip", "ssd_norm_g",
                 "lambda_q1", "lambda_k1", "lambda_q2", "lambda_k2", "subln_g", "w_ssd_branch", "w_attn_branch",
                 "w_out", "norm_ffn_g", "w_gate_ffn", "w_up_ffn", "w_down_ffn"):
        c[name] = np.ascontiguousarray(np.asarray(inputs[name], np.float32)[0])
    c["norm_final_g"] = np.ascontiguousarray(np.asarray(inputs["norm_final_g"], np.float32))
    return c


def kernel(**inputs):
    inputs = {k: np.asarray(v) for k, v in inputs.items()}
    consts = _consts(inputs)
    nc = build_nc()
    in_maps = [_core_inputs(c // 2, c % 2, inputs, consts) for c in range(8)]
    res = run_bass_kernel_spmd(nc, in_maps, core_ids=list(range(8)))
    B = inputs["x"].shape[0]
    out = np.zeros((B, 8192, D), np.float32)
    for c in range(8):
        b, j = c // 2, c % 2
        o = np.asarray(res.results[c]["out"]).reshape(NT, 128, D)
        for t in range(1, NT):
            xc = 2 * t - 1 if j == 0 else 2 * t - 2
            out[b, xc * 128:(xc + 1) * 128] = o[t]
    return out
```

```python
import numpy as np
import ml_dtypes
from contextlib import ExitStack
import concourse.bass as bass
import concourse.mybir as mybir
from concourse.bass_utils import run_bass_kernel_spmd

F32 = mybir.dt.float32
BF16 = mybir.dt.bfloat16
AF = mybir.ActivationFunctionType
ALU = mybir.AluOpType

D = 1024
NV = 65
NT = 33
TV = NV * 128
TO = NT * 128
DFF = 2816
NKC = 8
EPS = 1e-6
NEGB = -60000.0
DEBUG = False

COMPUTE = ("tensor", "vector", "scalar", "gpsimd")
ALLQ = ("sync",) + COMPUTE


class Fw:
    def __init__(self, nc, es):
        self.nc = nc
        self.es = es
        self.E = {"sync": nc.sync, "tensor": nc.tensor, "vector": nc.vector, "scalar": nc.scalar, "gpsimd": nc.gpsimd}
        self.res = {}
        self.waited = {e: {} for e in ALLQ}
        self.esem = {}
        self.ecnt = {}
        self.dsem = {}
        self.dcnt = {}
        self.nsem = 0
        self.new_epoch()

    def _newsem(self, name):
        self.nsem += 1
        h = self.es.enter_context(self.nc.semaphore(f"s{self.nsem}_{name}"))
        self.keep = getattr(self, "keep", [])
        self.keep.append(h)
        return h

    def new_epoch(self):
        for e in COMPUTE:
            self.esem[e] = self._newsem(e)
            self.ecnt[e] = 0

    def _deps(self, r, w):
        deps = []
        for k in r:
            ent = self.res.get(k)
            if ent and ent[0] is not None:
                deps.append(ent[0])
        for k in w:
            ent = self.res.get(k)
            if ent:
                if ent[0] is not None:
                    deps.append(ent[0])
                deps.extend(ent[1])
        return deps

    def _emit_waits(self, eng, deps):
        wd = self.waited[eng]
        need = {}
        for (sem, val) in deps:
            if eng == "tensor" and sem is self.esem["tensor"]:
                continue
            if wd.get(id(sem), 0) < val:
                if need.get(id(sem), (None, 0))[1] < val:
                    need[id(sem)] = (sem, val)
        for sid, (sem, val) in need.items():
            wd[sid] = val
            self.E[eng].wait_ge(sem, val)

    def _record(self, tok, r, w):
        for k in r:
            ent = self.res.setdefault(k, [None, []])
            ent[1].append(tok)
        for k in w:
            self.res[k] = [tok, []]

    def op(self, eng, fn, r=(), w=()):
        self._emit_waits(eng, self._deps(r, w))
        self.ecnt[eng] += 1
        sem = self.esem[eng]
        tok = (sem, self.ecnt[eng])
        fn(self.E[eng]).then_inc(sem, 1)
        self._record(tok, r, w)
        return tok

    def pe(self, fns, r=(), w=()):
        eng = "tensor"
        self._emit_waits(eng, self._deps(r, w))
        for fn in fns[:-1]:
            fn(self.E[eng])
        self.ecnt[eng] += 1
        sem = self.esem[eng]
        tok = (sem, self.ecnt[eng])
        fns[-1](self.E[eng]).then_inc(sem, 1)
        self._record(tok, r, w)
        return tok

    def dma(self, qe, out, in_, r=(), w=(), sem=None):
        sem = "k_" + w[0]
        if sem not in self.dsem:
            self.dsem[sem] = self._newsem("d")
            self.dcnt[sem] = 0
        self._emit_waits(qe, self._deps(r, w))
        self.dcnt[sem] += 1
        s = self.dsem[sem]
        tok = (s, 16 * self.dcnt[sem])
        self.E[qe].dma_start(out=out, in_=in_).then_inc(s, 16)
        self._record(tok, r, w)
        return tok

    def barrier(self):
        toks = []
        for e in COMPUTE:
            if self.ecnt[e] > 0:
                toks.append((self.esem[e], self.ecnt[e]))
        for k, s in self.dsem.items():
            if self.dcnt[k] > 0:
                toks.append((s, 16 * self.dcnt[k]))
        for e in ALLQ:
            self._emit_waits(e, toks)
        self.res = {}

    def final_wait(self, qe, toks):
        self._emit_waits(qe, toks)

    def run(self, block):
        q = self.q

        @block.sync
        def _(e):
            for c in q["sync"]:
                c(e)

        @block.scalar
        def _(e):
            for c in q["scalar"]:
                c(e)

        @block.vector
        def _(e):
            for c in q["vector"]:
                c(e)

        @block.gpsimd
        def _(e):
            for c in q["gpsimd"]:
                c(e)

        @block.tensor
        def _(e):
            for c in q["tensor"]:
                c(e)


def bc3(ap2, n):
    return ap2.unsqueeze(2).to_broadcast([ap2.shape[0], ap2.shape[1], n])


def build_nc(nv=NV, nt=NT, do_attn=True, do_post=True, debug=DEBUG, steps=99):
    nc = bass.Bass("TRN2", target_bir_lowering=False)
    dk = "ExternalOutput" if debug else "Internal"

    def din(name, shape, dt=F32):
        return nc.dram_tensor(name, shape, dt, kind="ExternalInput").ap()

    hv = din("hv", [TV, D])
    vmask_d = din("vmask", [128, NV])
    kaug_d = din("kaug", [8, 3, TV], BF16)
    qaug_d = din("qaug", [8, 3, TO], BF16)
    trim_d = din("trim", [128, 128], BF16)
    ident_d = din("ident", [128, 128], BF16)
    umask_d = din("umask", [128, 128])
    slmask_d = din("slmask", [128, 128])
    w_in = din("w_in", [D, 7696])
    norm_mix_g = din("norm_mix_g", [D])
    gate_bias = din("gate_bias", [2048])
    conv_w = din("conv_w", [1536, 4])
    conv_b = din("conv_b", [1536])
    dt_bias = din("dt_bias", [16])
    a_log = din("a_log", [16])
    d_skip = din("d_skip", [16])
    ssd_norm_g = din("ssd_norm_g", [D])
    lq1 = din("lambda_q1", [64]); lk1 = din("lambda_k1", [64])
    lq2 = din("lambda_q2", [64]); lk2 = din("lambda_k2", [64])
    subln_g = din("subln_g", [128])
    w_ssd = din("w_ssd_branch", [D, D])
    w_att = din("w_attn_branch", [D, D])
    w_out = din("w_out", [D, D])
    norm_ffn_g = din("norm_ffn_g", [D])
    w_gate = din("w_gate_ffn", [D, DFF])
    w_up = din("w_up_ffn", [D, DFF])
    w_down = din("w_down_ffn", [DFF, D])
    norm_final_g = din("norm_final_g", [D])

    out_d = nc.dram_tensor("out", [TO, D], F32, kind="ExternalOutput").ap()
    KT = nc.dram_tensor("KT", [8, 128, TV], BF16, kind=dk).ap()
    VS = nc.dram_tensor("VS", [TV, D], BF16, kind=dk).ap()
    QS = nc.dram_tensor("QS", [8, 128, TO], BF16, kind=dk).ap()
    GS = nc.dram_tensor("GS", [TO, 2048], BF16, kind=dk).ap()
    YS = nc.dram_tensor("YS", [TO, D], BF16, kind=dk).ap()
    YA = nc.dram_tensor("YA", [TO, D], BF16, kind=dk).ap()
    H1 = nc.dram_tensor("H1", [TO, D], F32, kind=dk).ap()

    es = ExitStack()
    with es:
        fw = Fw(nc, es)

        def sb(es_, name, shape, dt=F32):
            return es_.enter_context(nc.sbuf_tensor("sb_" + name, shape, dt))

        ident = sb(es, "ident", [128, 128], BF16)
        ps = [es.enter_context(nc.psum_tensor(f"ps{i}", [128, 512], F32)) for i in range(8)]
        PK = [f"ps{i}" for i in range(8)]
        ssq_all = sb(es, "ssq_all", [128, NT])
        lam_t = sb(es, "lam_t", [128, 4])
        fw.dma("sync", ident[:], ident_d, w=["ident"], sem="c0")

        dumped = set()

        def dump(name, ap, key, shape, dt):
            if not debug or name in dumped:
                return
            dumped.add(name)
            dd = nc.dram_tensor("dbg_" + name, shape, dt, kind="ExternalOutput").ap()
            fw.dma("gpsimd", dd, ap, r=[key], w=["dbg_" + name], sem="dbg")

        def psbf(i):
            return ps[i][:].bitcast(BF16)

        def rstd_from_ssq(ssq_ap, out_ap, n, rkeys, wkey, tmp_ap, tmpkey):
            fw.op("scalar", lambda e: e.activation(out=tmp_ap, in_=ssq_ap, func=AF.Ln, scale=1.0 / n, bias=EPS),
                  r=rkeys, w=[tmpkey])
            fw.op("scalar", lambda e: e.activation(out=out_ap, in_=tmp_ap, func=AF.Exp, scale=-0.5),
                  r=[tmpkey], w=[wkey])

        def transposes(src, srckey, nb, bank, dst, dstkey, eng="vector"):
            pv = psbf(bank)
            fns = [(lambda e, j=j: e.transpose(out=pv[:, j * 128:(j + 1) * 128], in_=src[:, j * 128:(j + 1) * 128],
                                               identity=ident[:])) for j in range(nb)]
            sk = list(srckey) if isinstance(srckey, (list, tuple)) else [srckey]
            fw.pe(fns, r=sk + ["ident"], w=[PK[bank]])
            if eng == "vector":
                fw.op("vector", lambda e: e.tensor_copy(out=dst.rearrange("p a b -> p (a b)"), in_=pv[:, 0:nb * 128]),
                      r=[PK[bank]], w=[dstkey])
            else:
                fw.op("scalar", lambda e: e.activation(out=dst.rearrange("p a b -> p (a b)"), in_=pv[:, 0:nb * 128],
                                                       func=AF.Copy), r=[PK[bank]], w=[dstkey])

        def load_w(tile, src2d, ncols, key, kc=NKC, step=4):
            v = src2d.rearrange("(k p) c -> p k c", p=128)
            for k0 in range(0, kc, step):
                k1 = min(kc, k0 + step)
                fw.dma("gpsimd", tile[:, k0:k1, :], v[:, k0:k1, :], w=[key], sem="wload")

        with ExitStack() as p1:
            wA = sb(p1, "wA", [128, NKC, 4624], BF16)
            for k0 in range(0, NKC, 2):
                v = w_in.rearrange("(k p) c -> p k c", p=128)
                fw.dma("gpsimd", wA[:, k0:k0 + 2, 0:2576], v[:, k0:k0 + 2, 0:2576], w=["wA"], sem="wload")
                fw.dma("gpsimd", wA[:, k0:k0 + 2, 2576:4624], v[:, k0:k0 + 2, 3600:5648], w=["wA"], sem="wload")
            gmix = sb(p1, "gmix", [128, D])
            vmask = sb(p1, "vmask", [128, NV])
            dtb = sb(p1, "dtb", [128, 16]); aneg = sb(p1, "aneg", [128, 16]); dsk = sb(p1, "dsk", [128, 16])
            cw = sb(p1, "cw", [128, 12, 4]); cb = sb(p1, "cb", [128, 12])
            diagw = sb(p1, "diagw", [128, 12, 5, 128], BF16)
            ones_bf = sb(p1, "ones_bf", [128, 128], BF16)
            umask = sb(p1, "umask", [128, 128]); slmask = sb(p1, "slmask", [128, 128]); ones_f = sb(p1, "ones_f", [128, 128])
            fw.dma("sync", gmix[:], norm_mix_g.partition_broadcast(128), w=["gmix"], sem="c0")
            fw.dma("sync", vmask[:], vmask_d, w=["vmask"], sem="c0")
            fw.dma("sync", dtb[:], dt_bias.partition_broadcast(128), w=["dtb"], sem="c0")
            fw.dma("sync", aneg[:], a_log.partition_broadcast(128), w=["aneg"], sem="c0")
            fw.dma("sync", dsk[:], d_skip.partition_broadcast(128), w=["dsk"], sem="c0")
            fw.dma("sync", cw[:], conv_w.rearrange("(b p) k -> p b k", p=128), w=["cw"], sem="c0")
            cbv = conv_b.rearrange("(b p o) -> b p o", p=128, o=1)
            for blk in range(12):
                fw.dma("sync", cb[:, blk:blk + 1], cbv[blk], w=["cb"], sem="c0")
            fw.dma("sync", umask[:], umask_d, w=["umask"], sem="c0")
            fw.dma("sync", slmask[:], slmask_d, w=["slmask"], sem="c0")
            fw.op("vector", lambda e: e.memset(ones_bf[:], 1.0), w=["ones_bf"])
            fw.op("vector", lambda e: e.memset(ones_f[:], 1.0), w=["ones_f"])
            fw.op("scalar", lambda e: e.activation(out=aneg[:], in_=aneg[:], func=AF.Exp), r=["aneg"], w=["aneg"])
            fw.op("vector", lambda e: e.tensor_scalar(out=aneg[:], in0=aneg[:], scalar1=-1.0, scalar2=None, op0=ALU.mult),
                  r=["aneg"], w=["aneg"])
            for blk in range(12):
                for k in range(4):
                    fw.op("vector", lambda e, blk=blk, k=k: e.tensor_scalar(
                        out=diagw[:, blk, k, :], in0=ident[:], scalar1=cw[:, blk, k:k + 1], scalar2=None, op0=ALU.mult),
                        r=["ident", "cw"], w=["diagw"])
                fw.op("vector", lambda e, blk=blk: e.tensor_scalar(
                    out=diagw[:, blk, 4, :], in0=ident[:], scalar1=cb[:, blk:blk + 1], scalar2=None, op0=ALU.mult),
                    r=["ident", "cb"], w=["diagw"])

            hc = [sb(p1, f"hc{i}", [128, D]) for i in range(2)]
            sqj = sb(p1, "sqj", [128, D], BF16)
            st4 = sb(p1, "st4", [128, 8])
            ubf = sb(p1, "ubf", [128, D], BF16)
            uT = sb(p1, "uT", [128, NKC, 128], BF16)
            xr = [sb(p1, f"xr{i}", [128, 12, 131], BF16) for i in range(2)]
            kt_sb = sb(p1, "kt_sb", [128, 8, 128], BF16)
            v_sb = sb(p1, "v_sb", [128, D], BF16)
            ex = sb(p1, "ex", [128, 1536])
            xbcT_l = [sb(p1, f"xbcT{i}", [128, 12, 128], BF16) for i in range(2)]
            xtok = sb(p1, "xtok", [128, D], BF16)
            btok = sb(p1, "btok", [128, 256], BF16)
            dts_l = [sb(p1, f"dts{i}", [128, 8, 16]) for i in range(2)]
            xdd = sb(p1, "xdd", [128, D], BF16)
            state = sb(p1, "state", [128, D])
            statebf = sb(p1, "statebf", [128, D], BF16)
            zc_l = [sb(p1, f"zc{i}", [128, D]) for i in range(2)]; ez_l = [sb(p1, f"ez{i}", [128, D]) for i in range(2)]
            Xm = sb(p1, "Xm", [128, 8, 128])
            dec = sb(p1, "dec", [128, 8, 128], BF16)
            cbm = sb(p1, "cbm", [128, 2, 128], BF16)
            MT = sb(p1, "MT", [128, 16, 128], BF16)
            xdt = sb(p1, "xdt", [128, D], BF16)
            yacc = sb(p1, "yacc", [128, D]); ytmp = sb(p1, "ytmp", [128, D])
            ysb = sb(p1, "ysb", [128, D], BF16)

            fw.op("vector", lambda e: e.memset(state[:], 0.0), w=["state"])
            fw.op("vector", lambda e: e.memset(xr[0][:, :, 0:3], 0.0), w=["xrh0"])

            def load_h(v_):
                fw.dma("sync", hc[v_ % 2][:], hv[v_ * 128:(v_ + 1) * 128, :], w=[f"hc{v_ % 2}"], sem=f"hc{v_ % 2}")

            load_h(0)
            def front1a(v_):
                own = (v_ % 2 == 0)
                t_ = v_ // 2
                sl = v_ % 2
                hck = f"hc{sl}"
                xbcT = xbcT_l[sl]; dts = dts_l[sl]; zc = zc_l[t_ % 2]; ez = ez_l[t_ % 2]
                if v_ + 1 < nv:
                    load_h(v_ + 1)
                fw.op("scalar", lambda e, sl=sl: e.activation(out=sqj[:], in_=hc[sl][:], func=AF.Square, accum_out=st4[:, 0:1]),
                      r=[hck], w=["sqj", "st_ssq"])
                rstd_from_ssq(st4[:, 0:1], st4[:, 2:3], D, ["st_ssq"], "st_rstd", st4[:, 1:2], "st_ln")
                fw.op("vector", lambda e, v_=v_: e.tensor_tensor(out=st4[:, 3:4], in0=st4[:, 2:3], in1=vmask[:, v_:v_ + 1], op=ALU.mult),
                      r=["st_rstd", "vmask"], w=["st_rm"])
                fw.op("vector", lambda e, sl=sl: e.scalar_tensor_tensor(out=ubf[:], in0=hc[sl][:], scalar=st4[:, 3:4], in1=gmix[:],
                                                                      op0=ALU.mult, op1=ALU.mult),
                      r=[hck, "st_rm", "gmix"], w=["ubf"])
                transposes(ubf, "ubf", 8, 0, uT[:], "uT")
                dump("hc", hc[sl][:], hck, [128, D], F32)
                dump("st4", st4[:], "st_rm", [128, 8], F32)
                dump("ubf", ubf[:], "ubf", [128, D], BF16)
                dump("uT", uT[:].rearrange("p a b -> p (a b)"), "uT", [128, D], BF16)
                if steps < 2:
                    return
                xcur = xr[sl]; xnext = xr[1 - sl]
                groups = [("x", 0), ("x", 4), ("x", 8), ("k", 0), ("k", 4)]
                for gi, (kind, b0) in enumerate(groups):
                    bank = 1 + (gi % 2)
                    fns = []
                    for j in range(4):
                        c0 = (1024 + (b0 + j) * 128) if kind == "x" else (2576 + (b0 + j) * 128)
                        for kc in range(NKC):
                            fns.append(lambda e, bank=bank, j=j, c0=c0, kc=kc: e.matmul(
                                ps[bank][:, j * 128:(j + 1) * 128], lhsT=wA[:, kc, c0:c0 + 128], rhs=uT[:, kc, :],
                                start=(kc == 0), stop=(kc == NKC - 1)))
                    fw.pe(fns, r=["wA", "uT"], w=[PK[bank]])
                    if kind == "x":
                        fw.op("scalar", lambda e, bank=bank, b0=b0, xcur=xcur: e.activation(
                            out=xcur[:, b0:b0 + 4, 3:131], in_=ps[bank][:].rearrange("p (a b) -> p a b", a=4), func=AF.Copy),
                            r=[PK[bank]], w=[f"xr{sl}"])
                    else:
                        fw.op("vector", lambda e, bank=bank, b0=b0: e.tensor_copy(
                            out=kt_sb[:, b0:b0 + 4, :], in_=ps[bank][:].rearrange("p (a b) -> p a b", a=4)),
                            r=[PK[bank]], w=["kt_sb"])
                    yield
                fw.dma("sync", KT[:, :, v_ * 128:(v_ + 1) * 128].rearrange("h p t -> p h t"), kt_sb[:],
                       r=["kt_sb"], w=["KT"], sem="st_k")
                fw.op("gpsimd", lambda e, xcur=xcur, xnext=xnext: e.tensor_copy(out=xnext[:, :, 0:3], in_=xcur[:, :, 128:131]),
                      r=[f"xr{sl}"], w=[f"xrh{1 - sl}"])
                if steps < 3:
                    return
                for half in range(2):
                    bank = 1 + half
                    c0 = 2576 + 1024 + half * 512
                    fns = [(lambda e, bank=bank, c0=c0, kc=kc: e.matmul(ps[bank][:, :], lhsT=uT[:, kc, :], rhs=wA[:, kc, c0:c0 + 512],
                                                                        start=(kc == 0), stop=(kc == NKC - 1))) for kc in range(NKC)]
                    fw.pe(fns, r=["wA", "uT"], w=[PK[bank]])
                    fw.op("scalar", lambda e, bank=bank, half=half: e.activation(out=v_sb[:, half * 512:(half + 1) * 512], in_=ps[bank][:],
                                                                                 func=AF.Copy), r=[PK[bank]], w=["v_sb"])
                fw.dma("sync", VS[v_ * 128:(v_ + 1) * 128, :], v_sb[:], r=["v_sb"], w=["VS"], sem="st_v")
                yield
                if steps < 3.3:
                    return
                fns = [(lambda e, kc=kc: e.matmul(ps[5][:, 0:16], lhsT=uT[:, kc, :], rhs=wA[:, kc, 2560:2576],
                                                  start=(kc == 0), stop=(kc == NKC - 1))) for kc in range(NKC)]
                fw.pe(fns, r=["wA", "uT"], w=[PK[5]])
                fw.op("vector", lambda e: e.tensor_tensor(out=dts[:, 0, :], in0=ps[5][:, 0:16], in1=dtb[:], op=ALU.add),
                      r=[PK[5], "dtb"], w=[f"dt_y@{sl}"])
                fw.op("scalar", lambda e: e.activation(out=dts[:, 7, :], in_=dts[:, 0, :], func=AF.Exp), r=[f"dt_y@{sl}"], w=[f"dt_tmp@{sl}"])
                fw.op("scalar", lambda e: e.activation(out=dts[:, 7, :], in_=dts[:, 7, :], func=AF.Ln, bias=1.0), r=[f"dt_tmp@{sl}"], w=[f"dt_tmp@{sl}"])
                fw.op("vector", lambda e, v_=v_: e.tensor_scalar(out=dts[:, 1, :], in0=dts[:, 7, :], scalar1=vmask[:, v_:v_ + 1], scalar2=None,
                                                              op0=ALU.mult), r=[f"dt_tmp@{sl}", "vmask"], w=[f"dt_dt@{sl}"])
                fw.op("vector", lambda e: e.tensor_tensor(out=dts[:, 2, :], in0=dts[:, 1, :], in1=aneg[:], op=ALU.mult),
                      r=[f"dt_dt@{sl}", "aneg"], w=[f"dt_da@{sl}"])
                yield
                if steps < 3.6:
                    return
                if own:
                    for half in range(2):
                        bank = 1 + half
                        c0 = half * 512
                        fns = [(lambda e, bank=bank, c0=c0, kc=kc: e.matmul(ps[bank][:, :], lhsT=uT[:, kc, :], rhs=wA[:, kc, c0:c0 + 512],
                                                                            start=(kc == 0), stop=(kc == NKC - 1))) for kc in range(NKC)]
                        fw.pe(fns, r=["wA", "uT"], w=[PK[bank]])
                        fw.op("scalar", lambda e, bank=bank, half=half: e.activation(out=zc[:, half * 512:(half + 1) * 512], in_=ps[bank][:],
                                                                                     func=AF.Copy), r=[PK[bank]], w=[f"zc@{t_ % 2}"])
                        fw.op("scalar", lambda e, half=half: e.activation(out=ez[:, half * 512:(half + 1) * 512], in_=zc[:, half * 512:(half + 1) * 512],
                                                                          func=AF.Sigmoid), r=[f"zc@{t_ % 2}"], w=[f"ez@{t_ % 2}"])
                if steps < 4:
                    return
                for grp in range(3):
                    bank = (3, 4, 0)[grp]
                    fns = []
                    for j in range(4):
                        blk = grp * 4 + j
                        for k in range(4):
                            fns.append(lambda e, bank=bank, j=j, blk=blk, k=k, xcur=xcur: e.matmul(
                                ps[bank][:, j * 128:(j + 1) * 128], lhsT=diagw[:, blk, k, :], rhs=xcur[:, blk, k:k + 128],
                                start=(k == 0), stop=False))
                        fns.append(lambda e, bank=bank, j=j, blk=blk: e.matmul(
                            ps[bank][:, j * 128:(j + 1) * 128], lhsT=diagw[:, blk, 4, :], rhs=ones_bf[:], start=False, stop=True))
                    fw.pe(fns, r=["diagw", f"xr{sl}", f"xrh{sl}", "ones_bf"], w=[PK[bank]])
                    fw.op("scalar", lambda e, bank=bank, grp=grp: e.activation(out=ex[:, grp * 512:(grp + 1) * 512], in_=ps[bank][:],
                                                                               func=AF.Sigmoid), r=[PK[bank]], w=[f"ex{grp}"])
                    fw.op("vector", lambda e, bank=bank, grp=grp: e.tensor_tensor(
                        out=xbcT[:, grp * 4:(grp + 1) * 4, :].rearrange("p a b -> p (a b)"), in0=ps[bank][:],
                        in1=ex[:, grp * 512:(grp + 1) * 512], op=ALU.mult), r=[PK[bank], f"ex{grp}"], w=[f"xbcT{grp}@{sl}"])
                    yield
            def back1a(v_):
                own = (v_ % 2 == 0)
                t_ = v_ // 2
                sl = v_ % 2
                xbcT = xbcT_l[sl]; dts = dts_l[sl]; zc = zc_l[t_ % 2]; ez = ez_l[t_ % 2]
                if steps < 5:
                    return
                pv3 = psbf(6)
                fns = [(lambda e, j=j: e.transpose(out=pv3[:, j * 128:(j + 1) * 128], in_=xbcT[:, j, :], identity=ident[:])) for j in range(8)]
                fw.pe(fns, r=[f"xbcT0@{sl}", f"xbcT1@{sl}", "ident"], w=[PK[6]])
                fw.op("vector", lambda e: e.tensor_copy(out=xtok[:], in_=pv3[:, :]), r=[PK[6]], w=["xtok"])
                pv4 = psbf(7)
                fns = [(lambda e, j=j: e.transpose(out=pv4[:, j * 128:(j + 1) * 128], in_=xbcT[:, 8 + j, :], identity=ident[:])) for j in range(2)]
                fw.pe(fns, r=[f"xbcT2@{sl}", "ident"], w=[PK[7]])
                fw.op("vector", lambda e: e.tensor_copy(out=btok[:], in_=pv4[:, 0:256]), r=[PK[7]], w=["btok"])
                yield
                if steps < 6:
                    return
                fw.pe([lambda e: e.matmul(ps[5][:, 16:32], lhsT=slmask[:], rhs=dts[:, 2, :], start=True, stop=True),
                       lambda e: e.matmul(ps[5][:, 32:48], lhsT=ones_f[:], rhs=dts[:, 2, :], start=True, stop=True),
                       lambda e: e.matmul(ps[5][:, 48:64], lhsT=umask[:], rhs=dts[:, 2, :], start=True, stop=True)],
                      r=["slmask", "ones_f", "umask", f"dt_da@{sl}"], w=[PK[5]])
                fw.op("scalar", lambda e: e.activation(out=dts[:, 3, :], in_=ps[5][:, 16:32], func=AF.Exp), r=[PK[5]], w=[f"dt_de@{sl}"])
                fw.op("scalar", lambda e: e.activation(out=dts[:, 6, :], in_=ps[5][:, 32:48], func=AF.Exp), r=[PK[5]], w=[f"dt_cd@{sl}"])
                if own:
                    fw.op("scalar", lambda e: e.activation(out=dts[:, 5, :], in_=ps[5][:, 48:64], func=AF.Exp), r=[PK[5]], w=[f"dt_ea@{sl}"])
                fw.op("vector", lambda e: e.tensor_tensor(out=dts[:, 4, :], in0=dts[:, 1, :], in1=dts[:, 3, :], op=ALU.mult),
                      r=[f"dt_dt@{sl}", f"dt_de@{sl}"], w=[f"dt_w1@{sl}"])
                fw.op("vector", lambda e: e.tensor_tensor(out=xdd[:].rearrange("p (h c) -> p h c", h=16),
                                                          in0=xtok[:].rearrange("p (h c) -> p h c", h=16),
                                                          in1=bc3(dts[:, 4, :], 64), op=ALU.mult), r=["xtok", f"dt_w1@{sl}"], w=["xdd"])
                yield
                fw.pe([(lambda e, g=g: e.matmul(ps[6 + g][:, :], lhsT=btok[:, g * 128:(g + 1) * 128], rhs=xdd[:, g * 512:(g + 1) * 512],
                                                start=True, stop=True)) for g in range(2)],
                      r=["btok", "xdd"], w=[PK[6], PK[7]])
                yield
                if own:
                    fw.op("gpsimd", lambda e: e.tensor_copy(out=statebf[:], in_=state[:]), r=["state"], w=["statebf"])
                fw.op("vector", lambda e: e.tensor_tensor(out=state[:].rearrange("p (h c) -> p h c", h=16),
                                                          in0=state[:].rearrange("p (h c) -> p h c", h=16),
                                                          in1=bc3(dts[:, 6, :], 64), op=ALU.mult), r=["state", f"dt_cd@{sl}"], w=["state"])
                for g in range(2):
                    fw.op("vector", lambda e, g=g: e.tensor_tensor(out=state[:, g * 512:(g + 1) * 512], in0=state[:, g * 512:(g + 1) * 512],
                                                                   in1=ps[6 + g][:], op=ALU.add), r=["state", PK[6 + g]], w=["state"])
                if not own:
                    return
                if steps < 7:
                    return
                fw.pe([(lambda e, g=g: e.matmul(ps[6 + g][:, :], lhsT=xbcT[:, 10 + g, :], rhs=statebf[:, g * 512:(g + 1) * 512],
                                                start=True, stop=True)) for g in range(2)],
                      r=[f"xbcT2@{sl}", "statebf"], w=[PK[6], PK[7]])
                for g in range(2):
                    fw.op("vector", lambda e, g=g: e.tensor_tensor(
                        out=yacc[:, g * 512:(g + 1) * 512].rearrange("p (h c) -> p h c", h=8),
                        in0=ps[6 + g][:].rearrange("p (h c) -> p h c", h=8),
                        in1=bc3(dts[:, 5, g * 8:(g + 1) * 8], 64), op=ALU.mult), r=[PK[6 + g], f"dt_ea@{sl}"], w=["yacc"])
                yield
                fw.pe([(lambda e, g=g: e.matmul(ps[5][:, 128 + g * 128:256 + g * 128], lhsT=xbcT[:, 8 + g, :], rhs=xbcT[:, 10 + g, :],
                                                start=True, stop=True)) for g in range(2)], r=[f"xbcT2@{sl}"], w=[PK[5]])
                fw.op("vector", lambda e: e.tensor_tensor(out=cbm[:], in0=ps[5][:, 128:384].rearrange("p (g l) -> p g l", g=2),
                                                          in1=umask[:].unsqueeze(1).to_broadcast([128, 2, 128]), op=ALU.mult),
                      r=[PK[5], "umask"], w=["cbm"])
                fw.op("gpsimd", lambda e: e.tensor_tensor(out=xdt[:].rearrange("p (h c) -> p h c", h=16),
                                                          in0=xtok[:].rearrange("p (h c) -> p h c", h=16),
                                                          in1=bc3(dts[:, 1, :], 64), op=ALU.mult), r=["xtok", f"dt_dt@{sl}"], w=["xdt"])
                for g in range(2):
                    fw.op("vector", lambda e, g=g: e.tensor_tensor(out=Xm[:], in0=bc3(dts[:, 2, g * 8:(g + 1) * 8], 128),
                                                                   in1=umask[:].unsqueeze(1).to_broadcast([128, 8, 128]), op=ALU.mult),
                          r=[f"dt_da@{sl}", "umask"], w=["Xm"])
                    fw.pe([(lambda e, hh=hh: e.matmul(ps[6 + hh][:, :], lhsT=slmask[:],
                                                      rhs=Xm[:, hh * 4:(hh + 1) * 4, :].rearrange("p a b -> p (a b)"),
                                                      start=True, stop=True)) for hh in range(2)],
                          r=["slmask", "Xm"], w=[PK[6], PK[7]])
                    for hh in range(2):
                        fw.op("scalar", lambda e, hh=hh: e.activation(out=dec[:, hh * 4:(hh + 1) * 4, :].rearrange("p a b -> p (a b)"),
                                                                      in_=ps[6 + hh][:], func=AF.Exp), r=[PK[6 + hh]], w=[f"dec{hh}"])
                    fw.op("vector", lambda e, g=g: e.tensor_tensor(out=MT[:, g * 8:(g + 1) * 8, :], in0=dec[:],
                                                                   in1=cbm[:, g, :].unsqueeze(1).to_broadcast([128, 8, 128]), op=ALU.mult),
                          r=["dec0", "dec1", "cbm"], w=[f"MT{g}"])
                    yield
                for g in range(2):
                    fw.pe([(lambda e, g=g, hh=hh: e.matmul(ps[6 + g][:, hh * 64:(hh + 1) * 64], lhsT=MT[:, g * 8 + hh, :],
                                                           rhs=xdt[:, (g * 8 + hh) * 64:(g * 8 + hh + 1) * 64], start=True, stop=True))
                           for hh in range(8)], r=[f"MT{g}", "xdt"], w=[PK[6 + g]])
                    fw.op("vector", lambda e, g=g: e.tensor_tensor(out=yacc[:, g * 512:(g + 1) * 512], in0=yacc[:, g * 512:(g + 1) * 512],
                                                                   in1=ps[6 + g][:], op=ALU.add), r=["yacc", PK[6 + g]], w=["yacc"])
                    yield
                fw.op("gpsimd", lambda e: e.tensor_tensor(out=ytmp[:].rearrange("p (h c) -> p h c", h=16),
                                                          in0=xtok[:].rearrange("p (h c) -> p h c", h=16),
                                                          in1=bc3(dsk[:], 64), op=ALU.mult), r=["xtok", "dsk"], w=["ytmp"])
                fw.op("vector", lambda e: e.tensor_tensor(out=yacc[:], in0=yacc[:], in1=ytmp[:], op=ALU.add), r=["yacc", "ytmp"], w=["yacc"])
                fw.op("gpsimd", lambda e: e.tensor_tensor(out=zc[:], in0=zc[:], in1=ez[:], op=ALU.mult), r=[f"zc@{t_ % 2}", f"ez@{t_ % 2}"], w=[f"zc@{t_ % 2}"])
                fw.op("vector", lambda e: e.tensor_tensor(out=yacc[:], in0=yacc[:], in1=zc[:], op=ALU.mult), r=["yacc", f"zc@{t_ % 2}"], w=["yacc"])
                fw.op("scalar", lambda e, t_=t_: e.activation(out=ytmp[:], in_=yacc[:], func=AF.Square, accum_out=ssq_all[:, t_:t_ + 1]),
                      r=["yacc"], w=["ytmp", "ssq_all"])
                fw.op("gpsimd", lambda e: e.tensor_copy(out=ysb[:], in_=yacc[:]), r=["yacc"], w=["ysb"])
                fw.dma("sync", YS[t_ * 128:(t_ + 1) * 128, :], ysb[:], r=["ysb"], w=["YS"], sem="st_y")

            def drain(g):
                for _ in g:
                    pass

            def interleave(ga, gb):
                a_alive, b_alive = True, True
                while a_alive or b_alive:
                    if a_alive:
                        try:
                            next(ga)
                        except StopIteration:
                            a_alive = False
                    if b_alive:
                        try:
                            next(gb)
                        except StopIteration:
                            b_alive = False

            drain(front1a(0))
            for v_ in range(nv):
                if v_ + 1 < nv:
                    interleave(back1a(v_), front1a(v_ + 1))
                else:
                    drain(back1a(v_))
            fw.barrier()
        fw.new_epoch()

        with ExitStack() as p1:
            wB = sb(p1, "wB", [128, NKC, 3072], BF16)
            v = w_in.rearrange("(k p) c -> p k c", p=128)
            for k0 in range(0, NKC, 2):
                fw.dma("gpsimd", wB[:, k0:k0 + 2, 0:1024], v[:, k0:k0 + 2, 2576:3600], w=["wB"], sem="wload")
                fw.dma("gpsimd", wB[:, k0:k0 + 2, 1024:3072], v[:, k0:k0 + 2, 5648:7696], w=["wB"], sem="wload")
            gmix = sb(p1, "gmixb", [128, D])
            gbias = sb(p1, "gbias", [128, 2048])
            vmask = sb(p1, "vmaskb", [128, NV])
            fw.dma("sync", gmix[:], norm_mix_g.partition_broadcast(128), w=["gmix"], sem="c0")
            fw.dma("sync", gbias[:], gate_bias.partition_broadcast(128), w=["gbias"], sem="c0")
            fw.dma("sync", vmask[:], vmask_d, w=["vmask"], sem="c0")
            hc = [sb(p1, f"hcb{i}", [128, D]) for i in range(2)]
            sqj = sb(p1, "sqjb", [128, D], BF16)
            st4 = sb(p1, "st4b", [128, 8])
            ubf = sb(p1, "ubfb", [128, D], BF16)
            uTb_l = [sb(p1, f"uTb{i}", [128, NKC, 128], BF16) for i in range(2)]
            q_sb = sb(p1, "q_sb", [128, 8, 128], BF16)
            gsf = sb(p1, "gsf", [128, 2048])
            gsb = sb(p1, "gsb", [128, 2048], BF16)

            def load_hb(t_):
                v_ = 2 * t_
                fw.dma("sync", hc[t_ % 2][:], hv[v_ * 128:(v_ + 1) * 128, :], w=[f"hc{t_ % 2}"], sem=f"hcb{t_ % 2}")

            def norm1b(tt):
                vv = 2 * tt
                ss = tt % 2
                hk = f"hc{ss}"
                fw.op("scalar", lambda e: e.activation(out=sqj[:], in_=hc[ss][:], func=AF.Square, accum_out=st4[:, 0:1]),
                      r=[hk], w=["sqj", "st_ssq"])
                rstd_from_ssq(st4[:, 0:1], st4[:, 2:3], D, ["st_ssq"], "st_rstd", st4[:, 1:2], "st_ln")
                fw.op("vector", lambda e: e.tensor_tensor(out=st4[:, 3:4], in0=st4[:, 2:3], in1=vmask[:, vv:vv + 1], op=ALU.mult),
                      r=["st_rstd", "vmask"], w=["st_rm"])
                fw.op("vector", lambda e: e.scalar_tensor_tensor(out=ubf[:], in0=hc[ss][:], scalar=st4[:, 3:4], in1=gmix[:],
                                                                 op0=ALU.mult, op1=ALU.mult),
                      r=[hk, "st_rm", "gmix"], w=["ubf"])
                transposes(ubf, "ubf", 8, 0, uTb_l[ss][:], f"uT{ss}")

            load_hb(0)
            if steps >= 8:
                norm1b(0)
            for t_ in range(nt if steps >= 8 else 0):
                v_ = 2 * t_
                sl = t_ % 2
                hck = f"hc{sl}"
                if t_ + 1 < nt:
                    load_hb(t_ + 1)
                uT = uTb_l[sl]
                uTk = f"uT{sl}"
                for gi in range(2):
                    bank = 1 + gi
                    fns = []
                    for j in range(4):
                        c0 = (gi * 4 + j) * 128
                        for kc in range(NKC):
                            fns.append(lambda e, bank=bank, j=j, c0=c0, kc=kc: e.matmul(
                                ps[bank][:, j * 128:(j + 1) * 128], lhsT=wB[:, kc, c0:c0 + 128], rhs=uT[:, kc, :],
                                start=(kc == 0), stop=(kc == NKC - 1)))
                    fw.pe(fns, r=["wB", uTk], w=[PK[bank]])
                    fw.op("vector", lambda e, bank=bank, gi=gi: e.tensor_copy(
                        out=q_sb[:, gi * 4:(gi + 1) * 4, :], in_=ps[bank][:].rearrange("p (a b) -> p a b", a=4)),
                        r=[PK[bank]], w=["q_sb"])
                fw.dma("sync", QS[:, :, t_ * 128:(t_ + 1) * 128].rearrange("h p t -> p h t"), q_sb[:], r=["q_sb"], w=["QS"], sem="st_q")
                if t_ + 1 < nt:
                    norm1b(t_ + 1)
                for gi in range(4):
                    bank = 3 + gi
                    c0 = 1024 + gi * 512
                    fns = [(lambda e, bank=bank, c0=c0, kc=kc: e.matmul(ps[bank][:, :], lhsT=uT[:, kc, :], rhs=wB[:, kc, c0:c0 + 512],
                                                                        start=(kc == 0), stop=(kc == NKC - 1))) for kc in range(NKC)]
                    fw.pe(fns, r=["wB", uTk], w=[PK[bank]])
                    sl_ = slice(gi * 512, (gi + 1) * 512)
                    fw.op("vector", lambda e, bank=bank, sl_=sl_: e.tensor_tensor(out=gsf[:, sl_], in0=ps[bank][:], in1=gbias[:, sl_], op=ALU.add),
                          r=[PK[bank], "gbias"], w=[f"gsf{gi}"])
                    fw.op("scalar", lambda e, sl_=sl_: e.activation(out=gsb[:, sl_], in_=gsf[:, sl_], func=AF.Sigmoid),
                          r=[f"gsf{gi}"], w=[f"gsb{gi}"])
                fw.dma("sync", GS[t_ * 128:(t_ + 1) * 128, :], gsb[:], r=[f"gsb{i}" for i in range(4)], w=["GS"], sem="st_g")
            fw.barrier()
        fw.new_epoch()

        p23 = ExitStack()
        p23.__enter__()
        wS = sb(p23, "wS", [128, NKC, D], BF16); wT = sb(p23, "wT", [128, NKC, D], BF16); wO = sb(p23, "wO", [128, NKC, D], BF16)
        if do_post:
            load_w(wS, w_ssd, D, "wS"); load_w(wT, w_att, D, "wT"); load_w(wO, w_out, D, "wO")
        if do_attn:
          with ExitStack() as p2:
            KTa = [sb(p2, f"KTa{i}", [67, 2, TV], BF16) for i in range(2)]
            QTa = [sb(p2, f"QTa{i}", [67, 2, TO], BF16) for i in range(2)]
            Va = [sb(p2, f"Va{i}", [128, NV, 130], BF16) for i in range(2)]
            trim = sb(p2, "trim", [128, 128], BF16)
            sg = sb(p2, "sg", [128, 128])
            lw = sb(p2, "lw", [128, 4, 64])
            Pt = [sb(p2, f"Pt{i}", [128, 512], BF16) for i in range(4)]
            OSETS = [[4, 5], [6, 7]]
            SBANKS = [0, 1, 2, 3]
            ogrp = 0
            av = sb(p2, "av", [128, 128]); avj = sb(p2, "avj", [128, 128])
            ast = sb(p2, "ast", [128, 8])
            yab = [sb(p2, f"yab{i}", [128, 128], BF16) for i in range(2)]
            fw.dma("sync", trim[:], trim_d, w=["trim"], sem="c0")
            fw.dma("sync", sg[:], subln_g.partition_broadcast(128), w=["sg"], sem="c0")
            fw.op("vector", lambda e: e.tensor_scalar(out=sg[:], in0=sg[:], scalar1=0.8, scalar2=None, op0=ALU.mult), r=["sg"], w=["sg"])
            for i, l_ in enumerate((lq1, lk1, lq2, lk2)):
                fw.dma("sync", lw[:, i, :], l_.partition_broadcast(128), w=["lw"], sem="c0")
            fw.op("vector", lambda e: e.tensor_tensor(out=lw[:, 0, :], in0=lw[:, 0, :], in1=lw[:, 1, :], op=ALU.mult), r=["lw"], w=["lw"])
            fw.op("vector", lambda e: e.tensor_tensor(out=lw[:, 2, :], in0=lw[:, 2, :], in1=lw[:, 3, :], op=ALU.mult), r=["lw"], w=["lw"])
            fw.op("vector", lambda e: e.reduce_sum(out=lam_t[:, 0:1], in_=lw[:, 0, :], axis=mybir.AxisListType.X), r=["lw"], w=["lam"])
            fw.op("vector", lambda e: e.reduce_sum(out=lam_t[:, 1:2], in_=lw[:, 2, :], axis=mybir.AxisListType.X), r=["lw", "lam"], w=["lam"])
            fw.op("scalar", lambda e: e.activation(out=lam_t[:, 0:2], in_=lam_t[:, 0:2], func=AF.Exp), r=["lam"], w=["lam"])
            fw.op("vector", lambda e: e.tensor_tensor(out=lam_t[:, 2:3], in0=lam_t[:, 1:2], in1=lam_t[:, 0:1], op=ALU.subtract), r=["lam"], w=["lam"])
            fw.op("vector", lambda e: e.tensor_scalar(out=lam_t[:, 3:4], in0=lam_t[:, 2:3], scalar1=-0.2, scalar2=None, op0=ALU.add),
                  r=["lam"], w=["lam"])
            for i in range(2):
                fw.op("vector", lambda e, i=i: e.memset(Va[i][:, :, 128:130], 1.0), w=[f"Va{i}"])

            def load_head(h):
                s_ = h % 2
                fw.dma("sync", KTa[s_][0:64, :, :], KT[h].rearrange("(c d) t -> d c t", c=2), r=["KT"], w=[f"KTa{s_}"], sem=f"hd{s_}")
                for c in range(2):
                    fw.dma("sync", KTa[s_][64:67, c, :], kaug_d[h], w=[f"KTa{s_}"], sem=f"hd{s_}")
                    fw.dma("sync", QTa[s_][64:67, c, :], qaug_d[h], w=[f"QTa{s_}"], sem=f"hd{s_}")
                fw.dma("sync", QTa[s_][0:64, :, :], QS[h].rearrange("(c d) t -> d c t", c=2), r=["QS"], w=[f"QTa{s_}"], sem=f"hd{s_}")
                vsv = VS[:, h * 128:(h + 1) * 128].rearrange("(v p) e -> p v e", p=128)
                for v0 in range(0, NV, 13):
                    fw.dma("sync", Va[s_][:, v0:v0 + 13, 0:128], vsv[:, v0:v0 + 13, :], r=["VS"], w=[f"Va{s_}"], sem=f"hd{s_}")

            load_head(0)
            sidx = 0
            oidx = 0
            for h in range(8):
                s_ = h % 2
                if h + 1 < 8:
                    load_head(h + 1)
                K_, Q_, V_ = KTa[s_], QTa[s_], Va[s_]
                rk = [f"KTa{s_}", f"QTa{s_}"]
                for g0 in range(0, nt, 2):
                    G = min(2, nt - g0)
                    t0 = g0
                    oset = OSETS[ogrp % 2]; ogrp += 1
                    started = set()
                    kb_max = 2 * (t0 + G - 1)

                    def oslot(j, c):
                        idx = c * G + j
                        return oset[idx // 3], (idx % 3) * 130

                    def jmin_of(kb):
                        return 0 if kb <= 2 * t0 else 1

                    def emit_qk(kb, sbk):
                        jm = jmin_of(kb)
                        dj = None
                        if kb >= 2 * t0 and (kb - 2 * t0) % 2 == 0:
                            dj = (kb - 2 * t0) // 2
                        fns = []
                        for c in range(2):
                            base = c * 256
                            fns.append(lambda e, sbk=sbk, base=base, jm=jm, c=c, kb=kb, dj=dj: e.matmul(
                                ps[sbk][:, base + jm * 128:base + G * 128], lhsT=K_[:, c, kb * 128:(kb + 1) * 128],
                                rhs=Q_[:, c, (t0 + jm) * 128:(t0 + G) * 128], start=True, stop=(dj is None)))
                            if dj is not None:
                                fns.append(lambda e, sbk=sbk, base=base, dj=dj: e.matmul(
                                    ps[sbk][:, base + dj * 128:base + (dj + 1) * 128], lhsT=ident[:], rhs=trim[:],
                                    start=False, stop=True))
                        fw.pe(fns, r=rk + ["ident", "trim"], w=[PK[sbk]])

                    def emit_act_av(kb, sbk):
                        jm = jmin_of(kb)

                        def view(ap):
                            return ap.rearrange("p (c j q) -> p c j q", c=2, j=2)[:, :, jm:G, :]
                        fw.op("scalar", lambda e: e.activation(out=view(Pt[sbk][:, 0:512]), in_=view(ps[sbk][:, 0:512]),
                                                               func=AF.Exp, scale=0.125), r=[PK[sbk]], w=[f"Pt{sbk}"])
                        fns = []
                        for c in range(2):
                            for j in range(jm, G):
                                bk, c0 = oslot(j, c)
                                st = bk not in started
                                started.add(bk)
                                col = c * 256 + j * 128
                                fns.append(lambda e, bk=bk, c0=c0, st=st, col=col, j=j: e.matmul(
                                    ps[bk][:, c0:c0 + 130], lhsT=Pt[sbk][:, col:col + 128], rhs=V_[:, kb, :],
                                    start=st, stop=(kb == 2 * (t0 + j)), skip_group_check=True))
                        fw.pe(fns, r=[f"Pt{sbk}", f"Va{s_}"], w=[PK[b_] for b_ in oset])

                    sbs = {}
                    for kb in range(min(2, kb_max + 1)):
                        sbs[kb] = SBANKS[sidx % 4]; sidx += 1
                        emit_qk(kb, sbs[kb])
                    for kb in range(kb_max + 1):
                        if kb + 2 <= kb_max:
                            sbs[kb + 2] = SBANKS[sidx % 4]; sidx += 1
                            emit_qk(kb + 2, sbs[kb + 2])
                        emit_act_av(kb, sbs[kb])
                    for j in range(G):
                        t_ = t0 + j
                        b0_, c0_ = oslot(j, 0)
                        b1_, c1_ = oslot(j, 1)
                        ya = yab[t_ % 2]
                        yk = f"yab{t_ % 2}"
                        fw.op("vector", lambda e: e.tensor_scalar(out=ast[:, 0:1], in0=ps[b0_][:, c0_ + 128:c0_ + 129], scalar1=1e-30,
                                                                  scalar2=None, op0=ALU.max), r=[PK[b0_]], w=["ast_r"])
                        fw.op("vector", lambda e: e.tensor_scalar(out=ast[:, 1:2], in0=ps[b1_][:, c1_ + 128:c1_ + 129], scalar1=1e-30,
                                                                  scalar2=None, op0=ALU.max), r=[PK[b1_]], w=["ast_r1"])
                        fw.op("vector", lambda e: e.reciprocal(out=ast[:, 0:2], in_=ast[:, 0:2]), r=["ast_r", "ast_r1"], w=["ast_r", "ast_r1"])
                        fw.op("vector", lambda e: e.tensor_tensor(out=ast[:, 2:3], in0=ast[:, 1:2], in1=lam_t[:, 3:4], op=ALU.mult),
                              r=["ast_r1", "lam"], w=["ast_r2"])
                        fw.op("vector", lambda e: e.tensor_scalar(out=av[:], in0=ps[b0_][:, c0_:c0_ + 128], scalar1=ast[:, 0:1], scalar2=None,
                                                                  op0=ALU.mult), r=[PK[b0_], "ast_r"], w=["av"])
                        fw.op("vector", lambda e: e.scalar_tensor_tensor(out=av[:], in0=ps[b1_][:, c1_:c1_ + 128], scalar=ast[:, 2:3], in1=av[:],
                                                                        op0=ALU.mult, op1=ALU.add), r=[PK[b1_], "ast_r2", "av"], w=["av"])
                        fw.op("scalar", lambda e: e.activation(out=avj[:], in_=av[:], func=AF.Square, accum_out=ast[:, 4:5]), r=["av"], w=["avj", "ast_s"])
                        rstd_from_ssq(ast[:, 4:5], ast[:, 6:7], 128, ["ast_s"], "ast_rs", ast[:, 5:6], "ast_ln")
                        fw.op("vector", lambda e: e.scalar_tensor_tensor(out=ya[:], in0=av[:], scalar=ast[:, 6:7], in1=sg[:],
                                                                        op0=ALU.mult, op1=ALU.mult), r=["av", "ast_rs", "sg"], w=[yk])
                        fw.dma("gpsimd", YA[t_ * 128:(t_ + 1) * 128, h * 128:(h + 1) * 128], ya[:], r=[yk], w=["YA"], sem=yk)
            fw.barrier()
          fw.new_epoch()

        last_tok = []
        if do_post:
          with ExitStack() as p3:
            gss = sb(p3, "gss", [128, D])
            fw.dma("sync", gss[:], ssd_norm_g.partition_broadcast(128), w=["gss"], sem="c0")
            ys_t = [sb(p3, f"ys{i}", [128, D], BF16) for i in range(2)]
            ya_t = [sb(p3, f"ya{i}", [128, D], BF16) for i in range(2)]
            gs_t = [sb(p3, f"gs{i}", [128, 2048], BF16) for i in range(2)]
            h_t = [sb(p3, f"h3_{i}", [128, D]) for i in range(2)]
            st3 = sb(p3, "st3", [128, 4])
            ysn = sb(p3, "ysn", [128, D], BF16)
            ysT = sb(p3, "ysT", [128, NKC, 128], BF16); yaT = sb(p3, "yaT", [128, NKC, 128], BF16)
            m1 = sb(p3, "m1", [128, D]); m2 = sb(p3, "m2", [128, D]); mb = sb(p3, "mb", [128, D], BF16)
            mT = sb(p3, "mT", [128, NKC, 128], BF16)
            h1s = sb(p3, "h1s", [128, D])

            def load3(t_):
                s_ = t_ % 2
                rows = slice(t_ * 128, (t_ + 1) * 128)
                fw.dma("sync", ys_t[s_][:], YS[rows, :], r=["YS"], w=[f"ys{s_}"], sem=f"l3{s_}")
                fw.dma("sync", ya_t[s_][:], YA[rows, :], r=["YA"], w=[f"ya{s_}"], sem=f"l3{s_}")
                fw.dma("sync", gs_t[s_][:], GS[rows, :], r=["GS"], w=[f"gs{s_}"], sem=f"l3{s_}")
                fw.dma("sync", h_t[s_][:], hv[2 * t_ * 128:(2 * t_ + 1) * 128, :], w=[f"h3{s_}"], sem=f"l3{s_}")

            load3(0)
            for t_ in range(nt):
                s_ = t_ % 2
                if t_ + 1 < nt:
                    load3(t_ + 1)
                rstd_from_ssq(ssq_all[:, t_:t_ + 1], st3[:, 1:2], D, ["ssq_all"], "st3_r", st3[:, 0:1], "st3_l")
                fw.op("vector", lambda e, s_=s_: e.scalar_tensor_tensor(out=ysn[:], in0=ys_t[s_][:], scalar=st3[:, 1:2], in1=gss[:],
                                                                       op0=ALU.mult, op1=ALU.mult), r=[f"ys{s_}", "st3_r", "gss"], w=["ysn"])
                transposes(ysn, "ysn", 8, 0, ysT[:], "ysT")
                transposes(ya_t[s_], f"ya{s_}", 8, 1, yaT[:], "yaT", eng="scalar")
                for half in range(2):
                    cs = slice(half * 512, (half + 1) * 512)
                    fw.pe([(lambda e, kc=kc, half=half, cs=cs: e.matmul(ps[2 + half][:, :], lhsT=ysT[:, kc, :], rhs=wS[:, kc, cs],
                                                                       start=(kc == 0), stop=(kc == NKC - 1))) for kc in range(NKC)],
                          r=["ysT", "wS"], w=[PK[2 + half]])
                    fw.pe([(lambda e, kc=kc, half=half, cs=cs: e.matmul(ps[4 + half][:, :], lhsT=yaT[:, kc, :], rhs=wT[:, kc, cs],
                                                                       start=(kc == 0), stop=(kc == NKC - 1))) for kc in range(NKC)],
                          r=["yaT", "wT"], w=[PK[4 + half]])
                    fw.op("vector", lambda e, half=half, cs=cs, s_=s_: e.tensor_tensor(out=m1[:, cs], in0=ps[2 + half][:], in1=gs_t[s_][:, cs], op=ALU.mult),
                          r=[PK[2 + half], f"gs{s_}"], w=[f"m1{half}"])
                    fw.op("vector", lambda e, half=half, cs=cs, s_=s_: e.tensor_tensor(
                        out=m2[:, cs], in0=ps[4 + half][:], in1=gs_t[s_][:, 1024 + half * 512:1024 + (half + 1) * 512], op=ALU.mult),
                        r=[PK[4 + half], f"gs{s_}"], w=[f"m2{half}"])
                    fw.op("vector", lambda e, cs=cs: e.tensor_tensor(out=mb[:, cs], in0=m1[:, cs], in1=m2[:, cs], op=ALU.add),
                          r=[f"m1{half}", f"m2{half}"], w=[f"mb{half}"])
                transposes(mb, ["mb0", "mb1"], 8, 6, mT[:], "mT")
                for half in range(2):
                    cs = slice(half * 512, (half + 1) * 512)
                    fw.pe([(lambda e, kc=kc, half=half, cs=cs: e.matmul(ps[2 + half][:, :], lhsT=mT[:, kc, :], rhs=wO[:, kc, cs],
                                                                       start=(kc == 0), stop=(kc == NKC - 1))) for kc in range(NKC)],
                          r=["mT", "wO"], w=[PK[2 + half]])
                    fw.op("vector", lambda e, half=half, cs=cs, s_=s_: e.tensor_tensor(out=h1s[:, cs], in0=ps[2 + half][:], in1=h_t[s_][:, cs], op=ALU.add),
                          r=[PK[2 + half], f"h3{s_}"], w=["h1s"])
                fw.dma("sync", H1[t_ * 128:(t_ + 1) * 128, :], h1s[:], r=["h1s"], w=["H1"], sem="st_h1")
            fw.barrier()
          fw.new_epoch()

          p23.close()
          with ExitStack() as p4:
            wG = sb(p4, "wG", [128, NKC, DFF], BF16); wU = sb(p4, "wU", [128, NKC, DFF], BF16)
            wD = sb(p4, "wD", [128, 22, D], BF16)
            load_w(wG, w_gate, DFF, "wG", step=2); load_w(wU, w_up, DFF, "wU", step=2)
            load_w(wD, w_down, D, "wD", kc=22, step=6)
            gff = sb(p4, "gff", [128, D]); gfin = sb(p4, "gfin", [128, D])
            fw.dma("sync", gff[:], norm_ffn_g.partition_broadcast(128), w=["gff"], sem="c0")
            fw.dma("sync", gfin[:], norm_final_g.partition_broadcast(128), w=["gfin"], sem="c0")
            h1_t = [sb(p4, f"h1_{i}", [128, D]) for i in range(3)]
            sqj = sb(p4, "sqj4", [128, D], BF16)
            sqj2 = sb(p4, "sqj42", [128, D], BF16)
            st4 = sb(p4, "st44", [128, 8])
            u2 = sb(p4, "u2", [128, D], BF16)
            u2T_l = [sb(p4, f"u2T{i}", [128, NKC, 128], BF16) for i in range(2)]
            eg = sb(p4, "eg", [128, 512]); gg = sb(p4, "gg", [128, 512])
            act_l = [sb(p4, f"act{i}", [128, DFF], BF16) for i in range(2)]
            actT_l = [sb(p4, f"actT{i}", [128, 22, 128], BF16) for i in range(2)]
            h2 = sb(p4, "h2", [128, D])
            ob_ = [sb(p4, f"ob{i}", [128, D]) for i in range(2)]

            def load4(t_):
                fw.dma("sync", h1_t[t_ % 3][:], H1[t_ * 128:(t_ + 1) * 128, :], r=["H1"], w=[f"h1{t_ % 3}"], sem=f"l4{t_ % 3}")

            cgs = [(i * 512, min(DFF, (i + 1) * 512)) for i in range(6)]

            def front4(t_):
                s_ = t_ % 2
                if t_ + 1 < nt:
                    load4(t_ + 1)
                h1 = h1_t[t_ % 3]
                h1k = f"h1{t_ % 3}"
                fw.op("scalar", lambda e: e.activation(out=sqj[:], in_=h1[:], func=AF.Square, accum_out=st4[:, 0:1]),
                      r=[h1k], w=["sqj", "st_ssq"])
                rstd_from_ssq(st4[:, 0:1], st4[:, 2:3], D, ["st_ssq"], "st_rstd", st4[:, 1:2], "st_ln")
                fw.op("vector", lambda e: e.scalar_tensor_tensor(out=u2[:], in0=h1[:], scalar=st4[:, 2:3], in1=gff[:],
                                                                 op0=ALU.mult, op1=ALU.mult), r=[h1k, "st_rstd", "gff"], w=["u2"])
                transposes(u2, "u2", 8, 0, u2T_l[s_][:], f"u2T{s_}")
                yield
                u2T = u2T_l[s_]
                u2Tk = f"u2T{s_}"
                act = act_l[s_]
                for gi, (c0, c1) in enumerate(cgs):
                    w_ = c1 - c0
                    bg = 1 + (gi % 2) * 2
                    bu = bg + 1
                    fw.pe([(lambda e, kc=kc: e.matmul(ps[bg][:, 0:w_], lhsT=u2T[:, kc, :], rhs=wG[:, kc, c0:c1],
                                                      start=(kc == 0), stop=(kc == NKC - 1))) for kc in range(NKC)],
                          r=[u2Tk, "wG"], w=[PK[bg]])
                    fw.pe([(lambda e, kc=kc: e.matmul(ps[bu][:, 0:w_], lhsT=u2T[:, kc, :], rhs=wU[:, kc, c0:c1],
                                                      start=(kc == 0), stop=(kc == NKC - 1))) for kc in range(NKC)],
                          r=[u2Tk, "wU"], w=[PK[bu]])
                    fw.op("scalar", lambda e: e.activation(out=gg[:, 0:w_], in_=ps[bg][:, 0:w_], func=AF.Silu),
                          r=[PK[bg]], w=["gg"])
                    fw.op("vector", lambda e: e.tensor_tensor(out=act[:, c0:c1], in0=ps[bu][:, 0:w_], in1=gg[:, 0:w_], op=ALU.mult),
                          r=[PK[bu], "gg"], w=[f"act{s_}_{gi}"])
                    yield

            def back4(t_):
                s_ = t_ % 2
                h1 = h1_t[t_ % 3]
                h1k = f"h1{t_ % 3}"
                act = act_l[s_]
                actT = actT_l[s_]
                for bi, (b0, b1) in enumerate(((0, 8), (8, 16), (16, 22))):
                    bank = 5 + bi
                    pv = psbf(bank)
                    fw.pe([(lambda e, j=j: e.transpose(out=pv[:, (j - b0) * 128:(j - b0 + 1) * 128],
                                                       in_=act[:, j * 128:(j + 1) * 128], identity=ident[:])) for j in range(b0, b1)],
                          r=[f"act{s_}_{i}" for i in range(6)] + ["ident"], w=[PK[bank]])
                    n_ = (b1 - b0) * 128
                    if bi == 1:
                        fw.op("scalar", lambda e: e.activation(
                            out=actT[:, b0:b1, :].rearrange("p a b -> p (a b)"), in_=pv[:, 0:n_], func=AF.Copy), r=[PK[bank]], w=[f"actT{s_}_{bi}"])
                    else:
                        fw.op("vector", lambda e: e.tensor_copy(
                            out=actT[:, b0:b1, :].rearrange("p a b -> p (a b)"), in_=pv[:, 0:n_]), r=[PK[bank]], w=[f"actT{s_}_{bi}"])
                    yield
                for half in range(2):
                    cs = slice(half * 512, (half + 1) * 512)
                    bank = 5 + half
                    fw.pe([(lambda e, kc=kc: e.matmul(ps[bank][:, :], lhsT=actT[:, kc, :], rhs=wD[:, kc, cs],
                                                      start=(kc == 0), stop=(kc == 21))) for kc in range(22)],
                          r=[f"actT{s_}_0", f"actT{s_}_1", f"actT{s_}_2", "wD"], w=[PK[bank]])
                    fw.op("vector", lambda e: e.tensor_tensor(out=h2[:, cs], in0=ps[bank][:], in1=h1[:, cs], op=ALU.add),
                          r=[PK[bank], h1k], w=["h2"])
                    yield
                fw.op("scalar", lambda e: e.activation(out=sqj2[:], in_=h2[:], func=AF.Square, accum_out=st4[:, 4:5]), r=["h2"], w=["sqj2", "st_ssq2"])
                rstd_from_ssq(st4[:, 4:5], st4[:, 6:7], D, ["st_ssq2"], "st_rstd2", st4[:, 5:6], "st_ln2")
                o_ = ob_[s_]
                fw.op("vector", lambda e: e.scalar_tensor_tensor(out=o_[:], in0=h2[:], scalar=st4[:, 6:7], in1=gfin[:],
                                                                 op0=ALU.mult, op1=ALU.mult), r=["h2", "st_rstd2", "gfin"], w=[f"ob{s_}"])
                last_tok.append(fw.dma("sync", out_d[t_ * 128:(t_ + 1) * 128, :], o_[:], r=[f"ob{s_}"], w=["out"], sem=f"st_o{s_}"))

            def drain4(g):
                for _ in g:
                    pass

            def interleave4(ga, gb):
                a_alive, b_alive = True, True
                while a_alive or b_alive:
                    if a_alive:
                        try:
                            next(ga)
                        except StopIteration:
                            a_alive = False
                    if b_alive:
                        try:
                            next(gb)
                        except StopIteration:
                            b_alive = False

            load4(0)
            drain4(front4(0))
            for t_ in range(nt):
                if t_ + 1 < nt:
                    interleave4(back4(t_), front4(t_ + 1))
                else:
                    drain4(back4(t_))
            fw.barrier()
        fw.barrier()
    return nc


SLOPES = [2.0 ** (-(h + 1)) for h in range(8)]


def _core_inputs(b, j, inputs, consts):
    x = inputs["x"][b]
    meta = inputs["meta_tokens"]
    hvirt = np.zeros((TV, D), np.float32)
    valid = np.ones((TV,), np.float32)
    if j == 0:
        hvirt[112:128] = meta
        hvirt[128:] = x
        valid[:112] = 0
    else:
        hvirt[240:256] = meta
        hvirt[256:] = x[:63 * 128]
        valid[:240] = 0
    kaug = np.zeros((8, 3, TV), np.float32)
    p = np.arange(TV) % 128
    v = np.arange(TV) // 128
    for h in range(8):
        kaug[h, 0] = 8 * SLOPES[h] * (p - 127) + np.where(valid > 0, 0.0, NEGB)
        kaug[h, 1] = 8 * SLOPES[h] * 128 * v
        kaug[h, 2] = 1.0
    m = dict(consts)
    m["hv"] = hvirt
    m["vmask"] = np.ascontiguousarray(valid.reshape(NV, 128).T)
    m["kaug"] = kaug.astype(ml_dtypes.bfloat16)
    return m


def _consts(inputs):
    qaug = np.zeros((8, 3, TO), np.float32)
    vq = 2 * (np.arange(TO) // 128)
    for h in range(8):
        qaug[h, 0] = 1.0
        qaug[h, 1] = 1.0
        qaug[h, 2] = -8 * SLOPES[h] * 128 * vq
    kk = np.arange(128)
    c = {
        "qaug": qaug.astype(ml_dtypes.bfloat16),
        "trim": np.where(kk[:, None] > kk[None, :], NEGB, 0.0).astype(ml_dtypes.bfloat16),
        "ident": np.eye(128, dtype=np.float32).astype(ml_dtypes.bfloat16),
        "umask": (kk[:, None] <= kk[None, :]).astype(np.float32),
        "slmask": (kk[:, None] > kk[None, :]).astype(np.float32),
    }
    for name in ("w_in", "norm_mix_g", "gate_bias", "conv_w", "conv_b", "dt_bias", "a_log", "d_skip", "ssd_norm_g",
                 "lambda_q1", "lambda_k1", "lambda_q2", "lambda_k2", "subln_g", "w_ssd_branch", "w_attn_branch",
                 "w_out", "norm_ffn_g", "w_gate_ffn", "w_up_ffn", "w_down_ffn"):
        c[name] = np.ascontiguousarray(np.asarray(inputs[name], np.float32)[0])
    c["norm_final_g"] = np.ascontiguousarray(np.asarray(inputs["norm_final_g"], np.float32))
    return c


def kernel(**inputs):
    inputs = {k: np.asarray(v) for k, v in inputs.items()}
    consts = _consts(inputs)
    nc = build_nc()
    in_maps = [_core_inputs(c // 2, c % 2, inputs, consts) for c in range(8)]
    res = run_bass_kernel_spmd(nc, in_maps, core_ids=list(range(8)))
    B = inputs["x"].shape[0]
    out = np.zeros((B, 8192, D), np.float32)
    for c in range(8):
        b, j = c // 2, c % 2
        o = np.asarray(res.results[c]["out"]).reshape(NT, 128, D)
        for t in range(1, NT):
            xc = 2 * t - 1 if j == 0 else 2 * t - 2
            out[b, xc * 128:(xc + 1) * 128] = o[t]
    return out
```

```python
import numpy as np
import ml_dtypes
from contextlib import ExitStack
import concourse.bass as bass
import concourse.mybir as mybir
from concourse.bass_utils import run_bass_kernel_spmd

F32 = mybir.dt.float32
BF16 = mybir.dt.bfloat16
AF = mybir.ActivationFunctionType
ALU = mybir.AluOpType

D = 1024
NV = 65
NT = 33
TV = NV * 128
TO = NT * 128
DFF = 2816
NKC = 8
EPS = 1e-6
NEGB = -60000.0
DEBUG = False

COMPUTE = ("tensor", "vector", "scalar", "gpsimd")
ALLQ = ("sync",) + COMPUTE


class Fw:
    def __init__(self, nc, es):
        self.nc = nc
        self.es = es
        self.E = {"sync": nc.sync, "tensor": nc.tensor, "vector": nc.vector, "scalar": nc.scalar, "gpsimd": nc.gpsimd}
        self.res = {}
        self.waited = {e: {} for e in ALLQ}
        self.esem = {}
        self.ecnt = {}
        self.dsem = {}
        self.dcnt = {}
        self.nsem = 0
        self.new_epoch()

    def _newsem(self, name):
        self.nsem += 1
        h = self.es.enter_context(self.nc.semaphore(f"s{self.nsem}_{name}"))
        self.keep = getattr(self, "keep", [])
        self.keep.append(h)
        return h

    def new_epoch(self):
        for e in COMPUTE:
            self.esem[e] = self._newsem(e)
            self.ecnt[e] = 0

    def _deps(self, r, w):
        deps = []
        for k in r:
            ent = self.res.get(k)
            if ent and ent[0] is not None:
                deps.append(ent[0])
        for k in w:
            ent = self.res.get(k)
            if ent:
                if ent[0] is not None:
                    deps.append(ent[0])
                deps.extend(ent[1])
        return deps

    def _emit_waits(self, eng, deps):
        wd = self.waited[eng]
        need = {}
        for (sem, val) in deps:
            if eng == "tensor" and sem is self.esem["tensor"]:
                continue
            if wd.get(id(sem), 0) < val:
                if need.get(id(sem), (None, 0))[1] < val:
                    need[id(sem)] = (sem, val)
        for sid, (sem, val) in need.items():
            wd[sid] = val
            self.E[eng].wait_ge(sem, val)

    def _record(self, tok, r, w):
        for k in r:
            ent = self.res.setdefault(k, [None, []])
            ent[1].append(tok)
        for k in w:
            self.res[k] = [tok, []]

    def op(self, eng, fn, r=(), w=()):
        self._emit_waits(eng, self._deps(r, w))
        self.ecnt[eng] += 1
        sem = self.esem[eng]
        tok = (sem, self.ecnt[eng])
        fn(self.E[eng]).then_inc(sem, 1)
        self._record(tok, r, w)
        return tok

    def pe(self, fns, r=(), w=()):
        eng = "tensor"
        self._emit_waits(eng, self._deps(r, w))
        for fn in fns[:-1]:
            fn(self.E[eng])
        self.ecnt[eng] += 1
        sem = self.esem[eng]
        tok = (sem, self.ecnt[eng])
        fns[-1](self.E[eng]).then_inc(sem, 1)
        self._record(tok, r, w)
        return tok

    def dma(self, qe, out, in_, r=(), w=(), sem=None):
        sem = "k_" + w[0]
        if sem not in self.dsem:
            self.dsem[sem] = self._newsem("d")
            self.dcnt[sem] = 0
        self._emit_waits(qe, self._deps(r, w))
        self.dcnt[sem] += 1
        s = self.dsem[sem]
        tok = (s, 16 * self.dcnt[sem])
        self.E[qe].dma_start(out=out, in_=in_).then_inc(s, 16)
        self._record(tok, r, w)
        return tok

    def barrier(self):
        toks = []
        for e in COMPUTE:
            if self.ecnt[e] > 0:
                toks.append((self.esem[e], self.ecnt[e]))
        for k, s in self.dsem.items():
            if self.dcnt[k] > 0:
                toks.append((s, 16 * self.dcnt[k]))
        for e in ALLQ:
            self._emit_waits(e, toks)
        self.res = {}

    def final_wait(self, qe, toks):
        self._emit_waits(qe, toks)

    def run(self, block):
        q = self.q

        @block.sync
        def _(e):
            for c in q["sync"]:
                c(e)

        @block.scalar
        def _(e):
            for c in q["scalar"]:
                c(e)

        @block.vector
        def _(e):
            for c in q["vector"]:
                c(e)

        @block.gpsimd
        def _(e):
            for c in q["gpsimd"]:
                c(e)

        @block.tensor
        def _(e):
            for c in q["tensor"]:
                c(e)


def bc3(ap2, n):
    return ap2.unsqueeze(2).to_broadcast([ap2.shape[0], ap2.shape[1], n])


def build_nc(nv=NV, nt=NT, do_attn=True, do_post=True, debug=DEBUG, steps=99):
    nc = bass.Bass("TRN2", target_bir_lowering=False)
    dk = "ExternalOutput" if debug else "Internal"

    def din(name, shape, dt=F32):
        return nc.dram_tensor(name, shape, dt, kind="ExternalInput").ap()

    hv = din("hv", [TV, D])
    vmask_d = din("vmask", [128, NV])
    kaug_d = din("kaug", [8, 3, TV], BF16)
    qaug_d = din("qaug", [8, 3, TO], BF16)
    trim_d = din("trim", [128, 128], BF16)
    ident_d = din("ident", [128, 128], BF16)
    umask_d = din("umask", [128, 128])
    slmask_d = din("slmask", [128, 128])
    w_in = din("w_in", [D, 7696])
    norm_mix_g = din("norm_mix_g", [D])
    gate_bias = din("gate_bias", [2048])
    conv_w = din("conv_w", [1536, 4])
    conv_b = din("conv_b", [1536])
    dt_bias = din("dt_bias", [16])
    a_log = din("a_log", [16])
    d_skip = din("d_skip", [16])
    ssd_norm_g = din("ssd_norm_g", [D])
    lq1 = din("lambda_q1", [64]); lk1 = din("lambda_k1", [64])
    lq2 = din("lambda_q2", [64]); lk2 = din("lambda_k2", [64])
    subln_g = din("subln_g", [128])
    w_ssd = din("w_ssd_branch", [D, D])
    w_att = din("w_attn_branch", [D, D])
    w_out = din("w_out", [D, D])
    norm_ffn_g = din("norm_ffn_g", [D])
    w_gate = din("w_gate_ffn", [D, DFF])
    w_up = din("w_up_ffn", [D, DFF])
    w_down = din("w_down_ffn", [DFF, D])
    norm_final_g = din("norm_final_g", [D])

    out_d = nc.dram_tensor("out", [TO, D], F32, kind="ExternalOutput").ap()
    KT = nc.dram_tensor("KT", [8, 128, TV], BF16, kind=dk).ap()
    VS = nc.dram_tensor("VS", [TV, D], BF16, kind=dk).ap()
    QS = nc.dram_tensor("QS", [8, 128, TO], BF16, kind=dk).ap()
    GS = nc.dram_tensor("GS", [TO, 2048], BF16, kind=dk).ap()
    YS = nc.dram_tensor("YS", [TO, D], BF16, kind=dk).ap()
    YA = nc.dram_tensor("YA", [TO, D], BF16, kind=dk).ap()
    H1 = nc.dram_tensor("H1", [TO, D], F32, kind=dk).ap()

    es = ExitStack()
    with es:
        fw = Fw(nc, es)

        def sb(es_, name, shape, dt=F32):
            return es_.enter_context(nc.sbuf_tensor("sb_" + name, shape, dt))

        ident = sb(es, "ident", [128, 128], BF16)
        ps = [es.enter_context(nc.psum_tensor(f"ps{i}", [128, 512], F32)) for i in range(8)]
        PK = [f"ps{i}" for i in range(8)]
        ssq_all = sb(es, "ssq_all", [128, NT])
        lam_t = sb(es, "lam_t", [128, 4])
        fw.dma("sync", ident[:], ident_d, w=["ident"], sem="c0")

        dumped = set()

        def dump(name, ap, key, shape, dt):
            if not debug or name in dumped:
                return
            dumped.add(name)
            dd = nc.dram_tensor("dbg_" + name, shape, dt, kind="ExternalOutput").ap()
            fw.dma("gpsimd", dd, ap, r=[key], w=["dbg_" + name], sem="dbg")

        def psbf(i):
            return ps[i][:].bitcast(BF16)

        def rstd_from_ssq(ssq_ap, out_ap, n, rkeys, wkey, tmp_ap, tmpkey):
            fw.op("scalar", lambda e: e.activation(out=tmp_ap, in_=ssq_ap, func=AF.Ln, scale=1.0 / n, bias=EPS),
                  r=rkeys, w=[tmpkey])
            fw.op("scalar", lambda e: e.activation(out=out_ap, in_=tmp_ap, func=AF.Exp, scale=-0.5),
                  r=[tmpkey], w=[wkey])

        def transposes(src, srckey, nb, bank, dst, dstkey, eng="vector"):
            pv = psbf(bank)
            fns = [(lambda e, j=j: e.transpose(out=pv[:, j * 128:(j + 1) * 128], in_=src[:, j * 128:(j + 1) * 128],
                                               identity=ident[:])) for j in range(nb)]
            sk = list(srckey) if isinstance(srckey, (list, tuple)) else [srckey]
            fw.pe(fns, r=sk + ["ident"], w=[PK[bank]])
            if eng == "vector":
                fw.op("vector", lambda e: e.tensor_copy(out=dst.rearrange("p a b -> p (a b)"), in_=pv[:, 0:nb * 128]),
                      r=[PK[bank]], w=[dstkey])
            else:
                fw.op("scalar", lambda e: e.activation(out=dst.rearrange("p a b -> p (a b)"), in_=pv[:, 0:nb * 128],
                                                       func=AF.Copy), r=[PK[bank]], w=[dstkey])

        def load_w(tile, src2d, ncols, key, kc=NKC, step=4):
            v = src2d.rearrange("(k p) c -> p k c", p=128)
            for k0 in range(0, kc, step):
                k1 = min(kc, k0 + step)
                fw.dma("gpsimd", tile[:, k0:k1, :], v[:, k0:k1, :], w=[key], sem="wload")

        with ExitStack() as p1:
            wA = sb(p1, "wA", [128, NKC, 4624], BF16)
            for k0 in range(0, NKC, 2):
                v = w_in.rearrange("(k p) c -> p k c", p=128)
                fw.dma("gpsimd", wA[:, k0:k0 + 2, 0:2576], v[:, k0:k0 + 2, 0:2576], w=["wA"], sem="wload")
                fw.dma("gpsimd", wA[:, k0:k0 + 2, 2576:4624], v[:, k0:k0 + 2, 3600:5648], w=["wA"], sem="wload")
            gmix = sb(p1, "gmix", [128, D])
            vmask = sb(p1, "vmask", [128, NV])
            dtb = sb(p1, "dtb", [128, 16]); aneg = sb(p1, "aneg", [128, 16]); dsk = sb(p1, "dsk", [128, 16])
            cw = sb(p1, "cw", [128, 12, 4]); cb = sb(p1, "cb", [128, 12])
            diagw = sb(p1, "diagw", [128, 12, 5, 128], BF16)
            ones_bf = sb(p1, "ones_bf", [128, 128], BF16)
            umask = sb(p1, "umask", [128, 128]); slmask = sb(p1, "slmask", [128, 128]); ones_f = sb(p1, "ones_f", [128, 128])
            fw.dma("sync", gmix[:], norm_mix_g.partition_broadcast(128), w=["gmix"], sem="c0")
            fw.dma("sync", vmask[:], vmask_d, w=["vmask"], sem="c0")
            fw.dma("sync", dtb[:], dt_bias.partition_broadcast(128), w=["dtb"], sem="c0")
            fw.dma("sync", aneg[:], a_log.partition_broadcast(128), w=["aneg"], sem="c0")
            fw.dma("sync", dsk[:], d_skip.partition_broadcast(128), w=["dsk"], sem="c0")
            fw.dma("sync", cw[:], conv_w.rearrange("(b p) k -> p b k", p=128), w=["cw"], sem="c0")
            cbv = conv_b.rearrange("(b p o) -> b p o", p=128, o=1)
            for blk in range(12):
                fw.dma("sync", cb[:, blk:blk + 1], cbv[blk], w=["cb"], sem="c0")
            fw.dma("sync", umask[:], umask_d, w=["umask"], sem="c0")
            fw.dma("sync", slmask[:], slmask_d, w=["slmask"], sem="c0")
            fw.op("vector", lambda e: e.memset(ones_bf[:], 1.0), w=["ones_bf"])
            fw.op("vector", lambda e: e.memset(ones_f[:], 1.0), w=["ones_f"])
            fw.op("scalar", lambda e: e.activation(out=aneg[:], in_=aneg[:], func=AF.Exp), r=["aneg"], w=["aneg"])
            fw.op("vector", lambda e: e.tensor_scalar(out=aneg[:], in0=aneg[:], scalar1=-1.0, scalar2=None, op0=ALU.mult),
                  r=["aneg"], w=["aneg"])
            for blk in range(12):
                for k in range(4):
                    fw.op("vector", lambda e, blk=blk, k=k: e.tensor_scalar(
                        out=diagw[:, blk, k, :], in0=ident[:], scalar1=cw[:, blk, k:k + 1], scalar2=None, op0=ALU.mult),
                        r=["ident", "cw"], w=["diagw"])
                fw.op("vector", lambda e, blk=blk: e.tensor_scalar(
                    out=diagw[:, blk, 4, :], in0=ident[:], scalar1=cb[:, blk:blk + 1], scalar2=None, op0=ALU.mult),
                    r=["ident", "cb"], w=["diagw"])

            hc = [sb(p1, f"hc{i}", [128, D]) for i in range(2)]
            sqj = sb(p1, "sqj", [128, D], BF16)
            st4 = sb(p1, "st4", [128, 8])
            ubf = sb(p1, "ubf", [128, D], BF16)
            uT = sb(p1, "uT", [128, NKC, 128], BF16)
            xr = [sb(p1, f"xr{i}", [128, 12, 131], BF16) for i in range(2)]
            kt_sb = sb(p1, "kt_sb", [128, 8, 128], BF16)
            v_sb = sb(p1, "v_sb", [128, D], BF16)
            ex = sb(p1, "ex", [128, 1536])
            xbcT_l = [sb(p1, f"xbcT{i}", [128, 12, 128], BF16) for i in range(2)]
            xtok = sb(p1, "xtok", [128, D], BF16)
            btok = sb(p1, "btok", [128, 256], BF16)
            dts_l = [sb(p1, f"dts{i}", [128, 8, 16]) for i in range(2)]
            xdd = sb(p1, "xdd", [128, D], BF16)
            state = sb(p1, "state", [128, D])
            statebf = sb(p1, "statebf", [128, D], BF16)
            zc_l = [sb(p1, f"zc{i}", [128, D]) for i in range(2)]; ez_l = [sb(p1, f"ez{i}", [128, D]) for i in range(2)]
            Xm = sb(p1, "Xm", [128, 8, 128])
            dec = sb(p1, "dec", [128, 8, 128], BF16)
            cbm = sb(p1, "cbm", [128, 2, 128], BF16)
            MT = sb(p1, "MT", [128, 16, 128], BF16)
            xdt = sb(p1, "xdt", [128, D], BF16)
            yacc = sb(p1, "yacc", [128, D]); ytmp = sb(p1, "ytmp", [128, D])
            ysb = sb(p1, "ysb", [128, D], BF16)

            fw.op("vector", lambda e: e.memset(state[:], 0.0), w=["state"])
            fw.op("vector", lambda e: e.memset(xr[0][:, :, 0:3], 0.0), w=["xrh0"])

            def load_h(v_):
                fw.dma("sync", hc[v_ % 2][:], hv[v_ * 128:(v_ + 1) * 128, :], w=[f"hc{v_ % 2}"], sem=f"hc{v_ % 2}")

            load_h(0)
            def front1a(v_):
                own = (v_ % 2 == 0)
                t_ = v_ // 2
                sl = v_ % 2
                hck = f"hc{sl}"
                xbcT = xbcT_l[sl]; dts = dts_l[sl]; zc = zc_l[t_ % 2]; ez = ez_l[t_ % 2]
                if v_ + 1 < nv:
                    load_h(v_ + 1)
                fw.op("scalar", lambda e, sl=sl: e.activation(out=sqj[:], in_=hc[sl][:], func=AF.Square, accum_out=st4[:, 0:1]),
                      r=[hck], w=["sqj", "st_ssq"])
                rstd_from_ssq(st4[:, 0:1], st4[:, 2:3], D, ["st_ssq"], "st_rstd", st4[:, 1:2], "st_ln")
                fw.op("vector", lambda e, v_=v_: e.tensor_tensor(out=st4[:, 3:4], in0=st4[:, 2:3], in1=vmask[:, v_:v_ + 1], op=ALU.mult),
                      r=["st_rstd", "vmask"], w=["st_rm"])
                fw.op("vector", lambda e, sl=sl: e.scalar_tensor_tensor(out=ubf[:], in0=hc[sl][:], scalar=st4[:, 3:4], in1=gmix[:],
                                                                      op0=ALU.mult, op1=ALU.mult),
                      r=[hck, "st_rm", "gmix"], w=["ubf"])
                transposes(ubf, "ubf", 8, 0, uT[:], "uT")
                dump("hc", hc[sl][:], hck, [128, D], F32)
                dump("st4", st4[:], "st_rm", [128, 8], F32)
                dump("ubf", ubf[:], "ubf", [128, D], BF16)
                dump("uT", uT[:].rearrange("p a b -> p (a b)"), "uT", [128, D], BF16)
                if steps < 2:
                    return
                xcur = xr[sl]; xnext = xr[1 - sl]
                groups = [("x", 0), ("x", 4), ("x", 8), ("k", 0), ("k", 4)]
                for gi, (kind, b0) in enumerate(groups):
                    bank = 1 + (gi % 2)
                    fns = []
                    for j in range(4):
                        c0 = (1024 + (b0 + j) * 128) if kind == "x" else (2576 + (b0 + j) * 128)
                        for kc in range(NKC):
                            fns.append(lambda e, bank=bank, j=j, c0=c0, kc=kc: e.matmul(
                                ps[bank][:, j * 128:(j + 1) * 128], lhsT=wA[:, kc, c0:c0 + 128], rhs=uT[:, kc, :],
                                start=(kc == 0), stop=(kc == NKC - 1)))
                    fw.pe(fns, r=["wA", "uT"], w=[PK[bank]])
                    if kind == "x":
                        fw.op("scalar", lambda e, bank=bank, b0=b0, xcur=xcur: e.activation(
                            out=xcur[:, b0:b0 + 4, 3:131], in_=ps[bank][:].rearrange("p (a b) -> p a b", a=4), func=AF.Copy),
                            r=[PK[bank]], w=[f"xr{sl}"])
                    else:
                        fw.op("vector", lambda e, bank=bank, b0=b0: e.tensor_copy(
                            out=kt_sb[:, b0:b0 + 4, :], in_=ps[bank][:].rearrange("p (a b) -> p a b", a=4)),
                            r=[PK[bank]], w=["kt_sb"])
                    yield
                fw.dma("sync", KT[:, :, v_ * 128:(v_ + 1) * 128].rearrange("h p t -> p h t"), kt_sb[:],
                       r=["kt_sb"], w=["KT"], sem="st_k")
                fw.op("gpsimd", lambda e, xcur=xcur, xnext=xnext: e.tensor_copy(out=xnext[:, :, 0:3], in_=xcur[:, :, 128:131]),
                      r=[f"xr{sl}"], w=[f"xrh{1 - sl}"])
                if steps < 3:
                    return
                for half in range(2):
                    bank = 1 + half
                    c0 = 2576 + 1024 + half * 512
                    fns = [(lambda e, bank=bank, c0=c0, kc=kc: e.matmul(ps[bank][:, :], lhsT=uT[:, kc, :], rhs=wA[:, kc, c0:c0 + 512],
                                                                        start=(kc == 0), stop=(kc == NKC - 1))) for kc in range(NKC)]
                    fw.pe(fns, r=["wA", "uT"], w=[PK[bank]])
                    fw.op("scalar", lambda e, bank=bank, half=half: e.activation(out=v_sb[:, half * 512:(half + 1) * 512], in_=ps[bank][:],
                                                                                 func=AF.Copy), r=[PK[bank]], w=["v_sb"])
                fw.dma("sync", VS[v_ * 128:(v_ + 1) * 128, :], v_sb[:], r=["v_sb"], w=["VS"], sem="st_v")
                yield
                if steps < 3.3:
                    return
                fns = [(lambda e, kc=kc: e.matmul(ps[5][:, 0:16], lhsT=uT[:, kc, :], rhs=wA[:, kc, 2560:2576],
                                                  start=(kc == 0), stop=(kc == NKC - 1))) for kc in range(NKC)]
                fw.pe(fns, r=["wA", "uT"], w=[PK[5]])
                fw.op("vector", lambda e: e.tensor_tensor(out=dts[:, 0, :], in0=ps[5][:, 0:16], in1=dtb[:], op=ALU.add),
                      r=[PK[5], "dtb"], w=[f"dt_y@{sl}"])
                fw.op("scalar", lambda e: e.activation(out=dts[:, 7, :], in_=dts[:, 0, :], func=AF.Exp), r=[f"dt_y@{sl}"], w=[f"dt_tmp@{sl}"])
                fw.op("scalar", lambda e: e.activation(out=dts[:, 7, :], in_=dts[:, 7, :], func=AF.Ln, bias=1.0), r=[f"dt_tmp@{sl}"], w=[f"dt_tmp@{sl}"])
                fw.op("vector", lambda e, v_=v_: e.tensor_scalar(out=dts[:, 1, :], in0=dts[:, 7, :], scalar1=vmask[:, v_:v_ + 1], scalar2=None,
                                                              op0=ALU.mult), r=[f"dt_tmp@{sl}", "vmask"], w=[f"dt_dt@{sl}"])
                fw.op("vector", lambda e: e.tensor_tensor(out=dts[:, 2, :], in0=dts[:, 1, :], in1=aneg[:], op=ALU.mult),
                      r=[f"dt_dt@{sl}", "aneg"], w=[f"dt_da@{sl}"])
                yield
                if steps < 3.6:
                    return
                if own:
                    for half in range(2):
                        bank = 1 + half
                        c0 = half * 512
                        fns = [(lambda e, bank=bank, c0=c0, kc=kc: e.matmul(ps[bank][:, :], lhsT=uT[:, kc, :], rhs=wA[:, kc, c0:c0 + 512],
                                                                            start=(kc == 0), stop=(kc == NKC - 1))) for kc in range(NKC)]
                        fw.pe(fns, r=["wA", "uT"], w=[PK[bank]])
                        fw.op("scalar", lambda e, bank=bank, half=half: e.activation(out=zc[:, half * 512:(half + 1) * 512], in_=ps[bank][:],
                                                                                     func=AF.Copy), r=[PK[bank]], w=[f"zc@{t_ % 2}"])
                        fw.op("scalar", lambda e, half=half: e.activation(out=ez[:, half * 512:(half + 1) * 512], in_=zc[:, half * 512:(half + 1) * 512],
                                                                          func=AF.Sigmoid), r=[f"zc@{t_ % 2}"], w=[f"ez@{t_ % 2}"])
                if steps < 4:
                    return
                for grp in range(3):
                    bank = (3, 4, 0)[grp]
                    fns = []
                    for j in range(4):
                        blk = grp * 4 + j
                        for k in range(4):
                            fns.append(lambda e, bank=bank, j=j, blk=blk, k=k, xcur=xcur: e.matmul(
                                ps[bank][:, j * 128:(j + 1) * 128], lhsT=diagw[:, blk, k, :], rhs=xcur[:, blk, k:k + 128],
                                start=(k == 0), stop=False))
                        fns.append(lambda e, bank=bank, j=j, blk=blk: e.matmul(
                            ps[bank][:, j * 128:(j + 1) * 128], lhsT=diagw[:, blk, 4, :], rhs=ones_bf[:], start=False, stop=True))
                    fw.pe(fns, r=["diagw", f"xr{sl}", f"xrh{sl}", "ones_bf"], w=[PK[bank]])
                    fw.op("scalar", lambda e, bank=bank, grp=grp: e.activation(out=ex[:, grp * 512:(grp + 1) * 512], in_=ps[bank][:],
                                                                               func=AF.Sigmoid), r=[PK[bank]], w=[f"ex{grp}"])
                    fw.op("vector", lambda e, bank=bank, grp=grp: e.tensor_tensor(
                        out=xbcT[:, grp * 4:(grp + 1) * 4, :].rearrange("p a b -> p (a b)"), in0=ps[bank][:],
                        in1=ex[:, grp * 512:(grp + 1) * 512], op=ALU.mult), r=[PK[bank], f"ex{grp}"], w=[f"xbcT{grp}@{sl}"])
                    yield
            def back1a(v_):
                own = (v_ % 2 == 0)
                t_ = v_ // 2
                sl = v_ % 2
                xbcT = xbcT_l[sl]; dts = dts_l[sl]; zc = zc_l[t_ % 2]; ez = ez_l[t_ % 2]
                if steps < 5:
                    return
                pv3 = psbf(6)
                fns = [(lambda e, j=j: e.transpose(out=pv3[:, j * 128:(j + 1) * 128], in_=xbcT[:, j, :], identity=ident[:])) for j in range(8)]
                fw.pe(fns, r=[f"xbcT0@{sl}", f"xbcT1@{sl}", "ident"], w=[PK[6]])
                fw.op("vector", lambda e: e.tensor_copy(out=xtok[:], in_=pv3[:, :]), r=[PK[6]], w=["xtok"])
                pv4 = psbf(7)
                fns = [(lambda e, j=j: e.transpose(out=pv4[:, j * 128:(j + 1) * 128], in_=xbcT[:, 8 + j, :], identity=ident[:])) for j in range(2)]
                fw.pe(fns, r=[f"xbcT2@{sl}", "ident"], w=[PK[7]])
                fw.op("vector", lambda e: e.tensor_copy(out=btok[:], in_=pv4[:, 0:256]), r=[PK[7]], w=["btok"])
                yield
                if steps < 6:
                    return
                fw.pe([lambda e: e.matmul(ps[5][:, 16:32], lhsT=slmask[:], rhs=dts[:, 2, :], start=True, stop=True),
                       lambda e: e.matmul(ps[5][:, 32:48], lhsT=ones_f[:], rhs=dts[:, 2, :], start=True, stop=True),
                       lambda e: e.matmul(ps[5][:, 48:64], lhsT=umask[:], rhs=dts[:, 2, :], start=True, stop=True)],
                      r=["slmask", "ones_f", "umask", f"dt_da@{sl}"], w=[PK[5]])
                fw.op("scalar", lambda e: e.activation(out=dts[:, 3, :], in_=ps[5][:, 16:32], func=AF.Exp), r=[PK[5]], w=[f"dt_de@{sl}"])
                fw.op("scalar", lambda e: e.activation(out=dts[:, 6, :], in_=ps[5][:, 32:48], func=AF.Exp), r=[PK[5]], w=[f"dt_cd@{sl}"])
                if own:
                    fw.op("scalar", lambda e: e.activation(out=dts[:, 5, :], in_=ps[5][:, 48:64], func=AF.Exp), r=[PK[5]], w=[f"dt_ea@{sl}"])
                fw.op("vector", lambda e: e.tensor_tensor(out=dts[:, 4, :], in0=dts[:, 1, :], in1=dts[:, 3, :], op=ALU.mult),
                      r=[f"dt_dt@{sl}", f"dt_de@{sl}"], w=[f"dt_w1@{sl}"])
                fw.op("vector", lambda e: e.tensor_tensor(out=xdd[:].rearrange("p (h c) -> p h c", h=16),
                                                          in0=xtok[:].rearrange("p (h c) -> p h c", h=16),
                                                          in1=bc3(dts[:, 4, :], 64), op=ALU.mult), r=["xtok", f"dt_w1@{sl}"], w=["xdd"])
                yield
                fw.pe([(lambda e, g=g: e.matmul(ps[6 + g][:, :], lhsT=btok[:, g * 128:(g + 1) * 128], rhs=xdd[:, g * 512:(g + 1) * 512],
                                                start=True, stop=True)) for g in range(2)],
                      r=["btok", "xdd"], w=[PK[6], PK[7]])
                yield
                if own:
                    fw.op("gpsimd", lambda e: e.tensor_copy(out=statebf[:], in_=state[:]), r=["state"], w=["statebf"])
                fw.op("vector", lambda e: e.tensor_tensor(out=state[:].rearrange("p (h c) -> p h c", h=16),
                                                          in0=state[:].rearrange("p (h c) -> p h c", h=16),
                                                          in1=bc3(dts[:, 6, :], 64), op=ALU.mult), r=["state", f"dt_cd@{sl}"], w=["state"])
                for g in range(2):
                    fw.op("vector", lambda e, g=g: e.tensor_tensor(out=state[:, g * 512:(g + 1) * 512], in0=state[:, g * 512:(g + 1) * 512],
                                                                   in1=ps[6 + g][:], op=ALU.add), r=["state", PK[6 + g]], w=["state"])
                if not own:
                    return
                if steps < 7:
                    return
                fw.pe([(lambda e, g=g: e.matmul(ps[6 + g][:, :], lhsT=xbcT[:, 10 + g, :], rhs=statebf[:, g * 512:(g + 1) * 512],
                                                start=True, stop=True)) for g in range(2)],
                      r=[f"xbcT2@{sl}", "statebf"], w=[PK[6], PK[7]])
                for g in range(2):
                    fw.op("vector", lambda e, g=g: e.tensor_tensor(
                        out=yacc[:, g * 512:(g + 1) * 512].rearrange("p (h c) -> p h c", h=8),
                        in0=ps[6 + g][:].rearrange("p (h c) -> p h c", h=8),
                        in1=bc3(dts[:, 5, g * 8:(g + 1) * 8], 64), op=ALU.mult), r=[PK[6 + g], f"dt_ea@{sl}"], w=["yacc"])
                yield
                fw.pe([(lambda e, g=g: e.matmul(ps[5][:, 128 + g * 128:256 + g * 128], lhsT=xbcT[:, 8 + g, :], rhs=xbcT[:, 10 + g, :],
                                                start=True, stop=True)) for g in range(2)], r=[f"xbcT2@{sl}"], w=[PK[5]])
                fw.op("vector", lambda e: e.tensor_tensor(out=cbm[:], in0=ps[5][:, 128:384].rearrange("p (g l) -> p g l", g=2),
                                                          in1=umask[:].unsqueeze(1).to_broadcast([128, 2, 128]), op=ALU.mult),
                      r=[PK[5], "umask"], w=["cbm"])
                fw.op("gpsimd", lambda e: e.tensor_tensor(out=xdt[:].rearrange("p (h c) -> p h c", h=16),
                                                          in0=xtok[:].rearrange("p (h c) -> p h c", h=16),
                                                          in1=bc3(dts[:, 1, :], 64), op=ALU.mult), r=["xtok", f"dt_dt@{sl}"], w=["xdt"])
                for g in range(2):
                    fw.op("vector", lambda e, g=g: e.tensor_tensor(out=Xm[:], in0=bc3(dts[:, 2, g * 8:(g + 1) * 8], 128),
                                                                   in1=umask[:].unsqueeze(1).to_broadcast([128, 8, 128]), op=ALU.mult),
                          r=[f"dt_da@{sl}", "umask"], w=["Xm"])
                    fw.pe([(lambda e, hh=hh: e.matmul(ps[6 + hh][:, :], lhsT=slmask[:],
                                                      rhs=Xm[:, hh * 4:(hh + 1) * 4, :].rearrange("p a b -> p (a b)"),
                                                      start=True, stop=True)) for hh in range(2)],
                          r=["slmask", "Xm"], w=[PK[6], PK[7]])
                    for hh in range(2):
                        fw.op("scalar", lambda e, hh=hh: e.activation(out=dec[:, hh * 4:(hh + 1) * 4, :].rearrange("p a b -> p (a b)"),
                                                                      in_=ps[6 + hh][:], func=AF.Exp), r=[PK[6 + hh]], w=[f"dec{hh}"])
                    fw.op("vector", lambda e, g=g: e.tensor_tensor(out=MT[:, g * 8:(g + 1) * 8, :], in0=dec[:],
                                                                   in1=cbm[:, g, :].unsqueeze(1).to_broadcast([128, 8, 128]), op=ALU.mult),
                          r=["dec0", "dec1", "cbm"], w=[f"MT{g}"])
                    yield
                for g in range(2):
                    fw.pe([(lambda e, g=g, hh=hh: e.matmul(ps[6 + g][:, hh * 64:(hh + 1) * 64], lhsT=MT[:, g * 8 + hh, :],
                                                           rhs=xdt[:, (g * 8 + hh) * 64:(g * 8 + hh + 1) * 64], start=True, stop=True))
                           for hh in range(8)], r=[f"MT{g}", "xdt"], w=[PK[6 + g]])
                    fw.op("vector", lambda e, g=g: e.tensor_tensor(out=yacc[:, g * 512:(g + 1) * 512], in0=yacc[:, g * 512:(g + 1) * 512],
                                                                   in1=ps[6 + g][:], op=ALU.add), r=["yacc", PK[6 + g]], w=["yacc"])
                    yield
                fw.op("gpsimd", lambda e: e.tensor_tensor(out=ytmp[:].rearrange("p (h c) -> p h c", h=16),
                                                          in0=xtok[:].rearrange("p (h c) -> p h c", h=16),
                                                          in1=bc3(dsk[:], 64), op=ALU.mult), r=["xtok", "dsk"], w=["ytmp"])
                fw.op("vector", lambda e: e.tensor_tensor(out=yacc[:], in0=yacc[:], in1=ytmp[:], op=ALU.add), r=["yacc", "ytmp"], w=["yacc"])
                fw.op("gpsimd", lambda e: e.tensor_tensor(out=zc[:], in0=zc[:], in1=ez[:], op=ALU.mult), r=[f"zc@{t_ % 2}", f"ez@{t_ % 2}"], w=[f"zc@{t_ % 2}"])
                fw.op("vector", lambda e: e.tensor_tensor(out=yacc[:], in0=yacc[:], in1=zc[:], op=ALU.mult), r=["yacc", f"zc@{t_ % 2}"], w=["yacc"])
                fw.op("scalar", lambda e, t_=t_: e.activation(out=ytmp[:], in_=yacc[:], func=AF.Square, accum_out=ssq_all[:, t_:t_ + 1]),
                      r=["yacc"], w=["ytmp", "ssq_all"])
                fw.op("gpsimd", lambda e: e.tensor_copy(out=ysb[:], in_=yacc[:]), r=["yacc"], w=["ysb"])
                fw.dma("sync", YS[t_ * 128:(t_ + 1) * 128, :], ysb[:], r=["ysb"], w=["YS"], sem="st_y")

            def drain(g):
                for _ in g:
                    pass

            def interleave(ga, gb):
                a_alive, b_alive = True, True
                while a_alive or b_alive:
                    if a_alive:
                        try:
                            next(ga)
                        except StopIteration:
                            a_alive = False
                    if b_alive:
                        try:
                            next(gb)
                        except StopIteration:
                            b_alive = False

            drain(front1a(0))
            for v_ in range(nv):
                if v_ + 1 < nv:
                    interleave(back1a(v_), front1a(v_ + 1))
                else:
                    drain(back1a(v_))
            fw.barrier()
        fw.new_epoch()

        with ExitStack() as p1:
            wB = sb(p1, "wB", [128, NKC, 3072], BF16)
            v = w_in.rearrange("(k p) c -> p k c", p=128)
            for gi in range(2):
                fw.dma("gpsimd", wB[:, :, gi * 512:(gi + 1) * 512], v[:, :, 2576 + gi * 512:2576 + (gi + 1) * 512], w=[f"wBq{gi}"])
            for gi in range(4):
                fw.dma("gpsimd", wB[:, :, 1024 + gi * 512:1024 + (gi + 1) * 512], v[:, :, 5648 + gi * 512:5648 + (gi + 1) * 512], w=[f"wBg{gi}"])
            gmix = sb(p1, "gmixb", [128, D])
            gbias = sb(p1, "gbias", [128, 2048])
            vmask = sb(p1, "vmaskb", [128, NV])
            fw.dma("sync", gmix[:], norm_mix_g.partition_broadcast(128), w=["gmix"], sem="c0")
            fw.dma("sync", gbias[:], gate_bias.partition_broadcast(128), w=["gbias"], sem="c0")
            fw.dma("sync", vmask[:], vmask_d, w=["vmask"], sem="c0")
            hc = [sb(p1, f"hcb{i}", [128, D]) for i in range(2)]
            sqj = sb(p1, "sqjb", [128, D], BF16)
            st4 = sb(p1, "st4b", [128, 8])
            ubf = sb(p1, "ubfb", [128, D], BF16)
            uTb_l = [sb(p1, f"uTb{i}", [128, NKC, 128], BF16) for i in range(2)]
            q_sb = sb(p1, "q_sb", [128, 8, 128], BF16)
            gsf = sb(p1, "gsf", [128, 2048])
            gsb = sb(p1, "gsb", [128, 2048], BF16)

            def load_hb(t_):
                v_ = 2 * t_
                fw.dma("sync", hc[t_ % 2][:], hv[v_ * 128:(v_ + 1) * 128, :], w=[f"hc{t_ % 2}"], sem=f"hcb{t_ % 2}")

            def norm1b(tt):
                vv = 2 * tt
                ss = tt % 2
                hk = f"hc{ss}"
                fw.op("scalar", lambda e: e.activation(out=sqj[:], in_=hc[ss][:], func=AF.Square, accum_out=st4[:, 0:1]),
                      r=[hk], w=["sqj", "st_ssq"])
                rstd_from_ssq(st4[:, 0:1], st4[:, 2:3], D, ["st_ssq"], "st_rstd", st4[:, 1:2], "st_ln")
                fw.op("vector", lambda e: e.tensor_tensor(out=st4[:, 3:4], in0=st4[:, 2:3], in1=vmask[:, vv:vv + 1], op=ALU.mult),
                      r=["st_rstd", "vmask"], w=["st_rm"])
                fw.op("vector", lambda e: e.scalar_tensor_tensor(out=ubf[:], in0=hc[ss][:], scalar=st4[:, 3:4], in1=gmix[:],
                                                                 op0=ALU.mult, op1=ALU.mult),
                      r=[hk, "st_rm", "gmix"], w=["ubf"])
                transposes(ubf, "ubf", 8, 0, uTb_l[ss][:], f"uT{ss}")

            load_hb(0)
            if steps >= 8:
                norm1b(0)
            for t_ in range(nt if steps >= 8 else 0):
                v_ = 2 * t_
                sl = t_ % 2
                hck = f"hc{sl}"
                if t_ + 1 < nt:
                    load_hb(t_ + 1)
                uT = uTb_l[sl]
                uTk = f"uT{sl}"
                for gi in range(2):
                    bank = 1 + gi
                    fns = []
                    for j in range(4):
                        c0 = (gi * 4 + j) * 128
                        for kc in range(NKC):
                            fns.append(lambda e, bank=bank, j=j, c0=c0, kc=kc: e.matmul(
                                ps[bank][:, j * 128:(j + 1) * 128], lhsT=wB[:, kc, c0:c0 + 128], rhs=uT[:, kc, :],
                                start=(kc == 0), stop=(kc == NKC - 1)))
                    fw.pe(fns, r=[f"wBq{gi}", uTk], w=[PK[bank]])
                    fw.op("vector", lambda e, bank=bank, gi=gi: e.tensor_copy(
                        out=q_sb[:, gi * 4:(gi + 1) * 4, :], in_=ps[bank][:].rearrange("p (a b) -> p a b", a=4)),
                        r=[PK[bank]], w=["q_sb"])
                fw.dma("sync", QS[:, :, t_ * 128:(t_ + 1) * 128].rearrange("h p t -> p h t"), q_sb[:], r=["q_sb"], w=["QS"], sem="st_q")
                if t_ + 1 < nt:
                    norm1b(t_ + 1)
                for gi in range(4):
                    bank = 3 + gi
                    c0 = 1024 + gi * 512
                    fns = [(lambda e, bank=bank, c0=c0, kc=kc: e.matmul(ps[bank][:, :], lhsT=uT[:, kc, :], rhs=wB[:, kc, c0:c0 + 512],
                                                                        start=(kc == 0), stop=(kc == NKC - 1))) for kc in range(NKC)]
                    fw.pe(fns, r=[f"wBg{gi}", uTk], w=[PK[bank]])
                    sl_ = slice(gi * 512, (gi + 1) * 512)
                    fw.op("vector", lambda e, bank=bank, sl_=sl_: e.tensor_tensor(out=gsf[:, sl_], in0=ps[bank][:], in1=gbias[:, sl_], op=ALU.add),
                          r=[PK[bank], "gbias"], w=[f"gsf{gi}"])
                    fw.op("scalar", lambda e, sl_=sl_: e.activation(out=gsb[:, sl_], in_=gsf[:, sl_], func=AF.Sigmoid),
                          r=[f"gsf{gi}"], w=[f"gsb{gi}"])
                fw.dma("sync", GS[t_ * 128:(t_ + 1) * 128, :], gsb[:], r=[f"gsb{i}" for i in range(4)], w=["GS"], sem="st_g")
            fw.barrier()
        fw.new_epoch()

        p23 = ExitStack()
        p23.__enter__()
        wS = sb(p23, "wS", [128, NKC, D], BF16); wT = sb(p23, "wT", [128, NKC, D], BF16); wO = sb(p23, "wO", [128, NKC, D], BF16)
        if do_post:
            load_w(wS, w_ssd, D, "wS"); load_w(wT, w_att, D, "wT"); load_w(wO, w_out, D, "wO")
        if do_attn:
          with ExitStack() as p2:
            KTa = [sb(p2, f"KTa{i}", [67, 2, TV], BF16) for i in range(2)]
            QTa = [sb(p2, f"QTa{i}", [67, 2, TO], BF16) for i in range(2)]
            Va = [sb(p2, f"Va{i}", [128, NV, 130], BF16) for i in range(2)]
            trim = sb(p2, "trim", [128, 128], BF16)
            sg = sb(p2, "sg", [128, 128])
            lw = sb(p2, "lw", [128, 4, 64])
            Pt = [sb(p2, f"Pt{i}", [128, 512], BF16) for i in range(4)]
            OSETS = [[4, 5], [6, 7]]
            SBANKS = [0, 1, 2, 3]
            ogrp = 0
            av = sb(p2, "av", [128, 128]); avj = sb(p2, "avj", [128, 128])
            ast = sb(p2, "ast", [128, 8])
            yab = [sb(p2, f"yab{i}", [128, 128], BF16) for i in range(2)]
            fw.dma("sync", trim[:], trim_d, w=["trim"], sem="c0")
            fw.dma("sync", sg[:], subln_g.partition_broadcast(128), w=["sg"], sem="c0")
            fw.op("vector", lambda e: e.tensor_scalar(out=sg[:], in0=sg[:], scalar1=0.8, scalar2=None, op0=ALU.mult), r=["sg"], w=["sg"])
            for i, l_ in enumerate((lq1, lk1, lq2, lk2)):
                fw.dma("sync", lw[:, i, :], l_.partition_broadcast(128), w=["lw"], sem="c0")
            fw.op("vector", lambda e: e.tensor_tensor(out=lw[:, 0, :], in0=lw[:, 0, :], in1=lw[:, 1, :], op=ALU.mult), r=["lw"], w=["lw"])
            fw.op("vector", lambda e: e.tensor_tensor(out=lw[:, 2, :], in0=lw[:, 2, :], in1=lw[:, 3, :], op=ALU.mult), r=["lw"], w=["lw"])
            fw.op("vector", lambda e: e.reduce_sum(out=lam_t[:, 0:1], in_=lw[:, 0, :], axis=mybir.AxisListType.X), r=["lw"], w=["lam"])
            fw.op("vector", lambda e: e.reduce_sum(out=lam_t[:, 1:2], in_=lw[:, 2, :], axis=mybir.AxisListType.X), r=["lw", "lam"], w=["lam"])
            fw.op("scalar", lambda e: e.activation(out=lam_t[:, 0:2], in_=lam_t[:, 0:2], func=AF.Exp), r=["lam"], w=["lam"])
            fw.op("vector", lambda e: e.tensor_tensor(out=lam_t[:, 2:3], in0=lam_t[:, 1:2], in1=lam_t[:, 0:1], op=ALU.subtract), r=["lam"], w=["lam"])
            fw.op("vector", lambda e: e.tensor_scalar(out=lam_t[:, 3:4], in0=lam_t[:, 2:3], scalar1=-0.2, scalar2=None, op0=ALU.add),
                  r=["lam"], w=["lam"])
            for i in range(2):
                fw.op("vector", lambda e, i=i: e.memset(Va[i][:, :, 128:130], 1.0), w=[f"Va{i}"])

            def load_head(h):
                s_ = h % 2
                fw.dma("sync", KTa[s_][0:64, :, :], KT[h].rearrange("(c d) t -> d c t", c=2), r=["KT"], w=[f"KTa{s_}"], sem=f"hd{s_}")
                for c in range(2):
                    fw.dma("sync", KTa[s_][64:67, c, :], kaug_d[h], w=[f"KTa{s_}"], sem=f"hd{s_}")
                    fw.dma("sync", QTa[s_][64:67, c, :], qaug_d[h], w=[f"QTa{s_}"], sem=f"hd{s_}")
                fw.dma("sync", QTa[s_][0:64, :, :], QS[h].rearrange("(c d) t -> d c t", c=2), r=["QS"], w=[f"QTa{s_}"], sem=f"hd{s_}")
                vsv = VS[:, h * 128:(h + 1) * 128].rearrange("(v p) e -> p v e", p=128)
                for v0 in range(0, NV, 13):
                    fw.dma("sync", Va[s_][:, v0:v0 + 13, 0:128], vsv[:, v0:v0 + 13, :], r=["VS"], w=[f"Va{s_}"], sem=f"hd{s_}")

            load_head(0)
            sidx = 0
            oidx = 0
            for h in range(8):
                s_ = h % 2
                if h + 1 < 8:
                    load_head(h + 1)
                K_, Q_, V_ = KTa[s_], QTa[s_], Va[s_]
                rk = [f"KTa{s_}", f"QTa{s_}"]
                for g0 in range(0, nt, 2):
                    G = min(2, nt - g0)
                    t0 = g0
                    oset = OSETS[ogrp % 2]; ogrp += 1
                    started = set()
                    kb_max = 2 * (t0 + G - 1)

                    def oslot(j, c):
                        idx = c * G + j
                        return oset[idx // 3], (idx % 3) * 130

                    def jmin_of(kb):
                        return 0 if kb <= 2 * t0 else 1

                    def emit_qk(kb, sbk):
                        jm = jmin_of(kb)
                        dj = None
                        if kb >= 2 * t0 and (kb - 2 * t0) % 2 == 0:
                            dj = (kb - 2 * t0) // 2
                        fns = []
                        for c in range(2):
                            base = c * 256
                            fns.append(lambda e, sbk=sbk, base=base, jm=jm, c=c, kb=kb, dj=dj: e.matmul(
                                ps[sbk][:, base + jm * 128:base + G * 128], lhsT=K_[:, c, kb * 128:(kb + 1) * 128],
                                rhs=Q_[:, c, (t0 + jm) * 128:(t0 + G) * 128], start=True, stop=(dj is None)))
                            if dj is not None:
                                fns.append(lambda e, sbk=sbk, base=base, dj=dj: e.matmul(
                                    ps[sbk][:, base + dj * 128:base + (dj + 1) * 128], lhsT=ident[:], rhs=trim[:],
                                    start=False, stop=True))
                        fw.pe(fns, r=rk + ["ident", "trim"], w=[PK[sbk]])

                    def emit_act_av(kb, sbk):
                        jm = jmin_of(kb)

                        def view(ap):
                            return ap.rearrange("p (c j q) -> p c j q", c=2, j=2)[:, :, jm:G, :]
                        fw.op("scalar", lambda e: e.activation(out=view(Pt[sbk][:, 0:512]), in_=view(ps[sbk][:, 0:512]),
                                                               func=AF.Exp, scale=0.125), r=[PK[sbk]], w=[f"Pt{sbk}"])
                        fns = []
                        for c in range(2):
                            for j in range(jm, G):
                                bk, c0 = oslot(j, c)
                                st = bk not in started
                                started.add(bk)
                                col = c * 256 + j * 128
                                fns.append(lambda e, bk=bk, c0=c0, st=st, col=col, j=j: e.matmul(
                                    ps[bk][:, c0:c0 + 130], lhsT=Pt[sbk][:, col:col + 128], rhs=V_[:, kb, :],
                                    start=st, stop=(kb == 2 * (t0 + j)), skip_group_check=True))
                        fw.pe(fns, r=[f"Pt{sbk}", f"Va{s_}"], w=[PK[b_] for b_ in oset])

                    sbs = {}
                    for kb in range(min(2, kb_max + 1)):
                        sbs[kb] = SBANKS[sidx % 4]; sidx += 1
                        emit_qk(kb, sbs[kb])
                    for kb in range(kb_max + 1):
                        if kb + 2 <= kb_max:
                            sbs[kb + 2] = SBANKS[sidx % 4]; sidx += 1
                            emit_qk(kb + 2, sbs[kb + 2])
                        emit_act_av(kb, sbs[kb])
                    for j in range(G):
                        t_ = t0 + j
                        b0_, c0_ = oslot(j, 0)
                        b1_, c1_ = oslot(j, 1)
                        ya = yab[t_ % 2]
                        yk = f"yab{t_ % 2}"
                        fw.op("vector", lambda e: e.tensor_scalar(out=ast[:, 0:1], in0=ps[b0_][:, c0_ + 128:c0_ + 129], scalar1=1e-30,
                                                                  scalar2=None, op0=ALU.max), r=[PK[b0_]], w=["ast_r"])
                        fw.op("vector", lambda e: e.tensor_scalar(out=ast[:, 1:2], in0=ps[b1_][:, c1_ + 128:c1_ + 129], scalar1=1e-30,
                                                                  scalar2=None, op0=ALU.max), r=[PK[b1_]], w=["ast_r1"])
                        fw.op("vector", lambda e: e.reciprocal(out=ast[:, 0:2], in_=ast[:, 0:2]), r=["ast_r", "ast_r1"], w=["ast_r", "ast_r1"])
                        fw.op("vector", lambda e: e.tensor_tensor(out=ast[:, 2:3], in0=ast[:, 1:2], in1=lam_t[:, 3:4], op=ALU.mult),
                              r=["ast_r1", "lam"], w=["ast_r2"])
                        fw.op("vector", lambda e: e.tensor_scalar(out=av[:], in0=ps[b0_][:, c0_:c0_ + 128], scalar1=ast[:, 0:1], scalar2=None,
                                                                  op0=ALU.mult), r=[PK[b0_], "ast_r"], w=["av"])
                        fw.op("vector", lambda e: e.scalar_tensor_tensor(out=av[:], in0=ps[b1_][:, c1_:c1_ + 128], scalar=ast[:, 2:3], in1=av[:],
                                                                        op0=ALU.mult, op1=ALU.add), r=[PK[b1_], "ast_r2", "av"], w=["av"])
                        fw.op("scalar", lambda e: e.activation(out=avj[:], in_=av[:], func=AF.Square, accum_out=ast[:, 4:5]), r=["av"], w=["avj", "ast_s"])
                        rstd_from_ssq(ast[:, 4:5], ast[:, 6:7], 128, ["ast_s"], "ast_rs", ast[:, 5:6], "ast_ln")
                        fw.op("vector", lambda e: e.scalar_tensor_tensor(out=ya[:], in0=av[:], scalar=ast[:, 6:7], in1=sg[:],
                                                                        op0=ALU.mult, op1=ALU.mult), r=["av", "ast_rs", "sg"], w=[yk])
                        fw.dma("gpsimd", YA[t_ * 128:(t_ + 1) * 128, h * 128:(h + 1) * 128], ya[:], r=[yk], w=["YA"], sem=yk)
            fw.barrier()
          fw.new_epoch()

        last_tok = []
        if do_post:
          with ExitStack() as p3:
            gss = sb(p3, "gss", [128, D])
            fw.dma("sync", gss[:], ssd_norm_g.partition_broadcast(128), w=["gss"], sem="c0")
            ys_t = [sb(p3, f"ys{i}", [128, D], BF16) for i in range(2)]
            ya_t = [sb(p3, f"ya{i}", [128, D], BF16) for i in range(2)]
            gs_t = [sb(p3, f"gs{i}", [128, 2048], BF16) for i in range(2)]
            h_t = [sb(p3, f"h3_{i}", [128, D]) for i in range(2)]
            st3 = sb(p3, "st3", [128, 4])
            ysn = sb(p3, "ysn", [128, D], BF16)
            ysT = sb(p3, "ysT", [128, NKC, 128], BF16); yaT = sb(p3, "yaT", [128, NKC, 128], BF16)
            m1 = sb(p3, "m1", [128, D]); m2 = sb(p3, "m2", [128, D]); mb = sb(p3, "mb", [128, D], BF16)
            mT = sb(p3, "mT", [128, NKC, 128], BF16)
            h1s = sb(p3, "h1s", [128, D])

            def load3(t_):
                s_ = t_ % 2
                rows = slice(t_ * 128, (t_ + 1) * 128)
                fw.dma("sync", ys_t[s_][:], YS[rows, :], r=["YS"], w=[f"ys{s_}"], sem=f"l3{s_}")
                fw.dma("sync", ya_t[s_][:], YA[rows, :], r=["YA"], w=[f"ya{s_}"], sem=f"l3{s_}")
                fw.dma("sync", gs_t[s_][:], GS[rows, :], r=["GS"], w=[f"gs{s_}"], sem=f"l3{s_}")
                fw.dma("sync", h_t[s_][:], hv[2 * t_ * 128:(2 * t_ + 1) * 128, :], w=[f"h3{s_}"], sem=f"l3{s_}")

            load3(0)
            for t_ in range(nt):
                s_ = t_ % 2
                if t_ + 1 < nt:
                    load3(t_ + 1)
                rstd_from_ssq(ssq_all[:, t_:t_ + 1], st3[:, 1:2], D, ["ssq_all"], "st3_r", st3[:, 0:1], "st3_l")
                fw.op("vector", lambda e, s_=s_: e.scalar_tensor_tensor(out=ysn[:], in0=ys_t[s_][:], scalar=st3[:, 1:2], in1=gss[:],
                                                                       op0=ALU.mult, op1=ALU.mult), r=[f"ys{s_}", "st3_r", "gss"], w=["ysn"])
                transposes(ysn, "ysn", 8, 0, ysT[:], "ysT")
                transposes(ya_t[s_], f"ya{s_}", 8, 1, yaT[:], "yaT", eng="scalar")
                for half in range(2):
                    cs = slice(half * 512, (half + 1) * 512)
                    fw.pe([(lambda e, kc=kc, half=half, cs=cs: e.matmul(ps[2 + half][:, :], lhsT=ysT[:, kc, :], rhs=wS[:, kc, cs],
                                                                       start=(kc == 0), stop=(kc == NKC - 1))) for kc in range(NKC)],
                          r=["ysT", "wS"], w=[PK[2 + half]])
                    fw.pe([(lambda e, kc=kc, half=half, cs=cs: e.matmul(ps[4 + half][:, :], lhsT=yaT[:, kc, :], rhs=wT[:, kc, cs],
                                                                       start=(kc == 0), stop=(kc == NKC - 1))) for kc in range(NKC)],
                          r=["yaT", "wT"], w=[PK[4 + half]])
                    fw.op("vector", lambda e, half=half, cs=cs, s_=s_: e.tensor_tensor(out=m1[:, cs], in0=ps[2 + half][:], in1=gs_t[s_][:, cs], op=ALU.mult),
                          r=[PK[2 + half], f"gs{s_}"], w=[f"m1{half}"])
                    fw.op("vector", lambda e, half=half, cs=cs, s_=s_: e.tensor_tensor(
                        out=m2[:, cs], in0=ps[4 + half][:], in1=gs_t[s_][:, 1024 + half * 512:1024 + (half + 1) * 512], op=ALU.mult),
                        r=[PK[4 + half], f"gs{s_}"], w=[f"m2{half}"])
                    fw.op("vector", lambda e, cs=cs: e.tensor_tensor(out=mb[:, cs], in0=m1[:, cs], in1=m2[:, cs], op=ALU.add),
                          r=[f"m1{half}", f"m2{half}"], w=[f"mb{half}"])
                transposes(mb, ["mb0", "mb1"], 8, 6, mT[:], "mT")
                for half in range(2):
                    cs = slice(half * 512, (half + 1) * 512)
                    fw.pe([(lambda e, kc=kc, half=half, cs=cs: e.matmul(ps[2 + half][:, :], lhsT=mT[:, kc, :], rhs=wO[:, kc, cs],
                                                                       start=(kc == 0), stop=(kc == NKC - 1))) for kc in range(NKC)],
                          r=["mT", "wO"], w=[PK[2 + half]])
                    fw.op("vector", lambda e, half=half, cs=cs, s_=s_: e.tensor_tensor(out=h1s[:, cs], in0=ps[2 + half][:], in1=h_t[s_][:, cs], op=ALU.add),
                          r=[PK[2 + half], f"h3{s_}"], w=["h1s"])
                fw.dma("sync", H1[t_ * 128:(t_ + 1) * 128, :], h1s[:], r=["h1s"], w=["H1"], sem="st_h1")
            fw.barrier()
          fw.new_epoch()

          p23.close()
          with ExitStack() as p4:
            wG = sb(p4, "wG", [128, NKC, DFF], BF16); wU = sb(p4, "wU", [128, NKC, DFF], BF16)
            wD = sb(p4, "wD", [128, 22, D], BF16)
            cgs = [(i * 512, min(DFF, (i + 1) * 512)) for i in range(6)]
            vG = w_gate.rearrange("(k p) c -> p k c", p=128)
            vU = w_up.rearrange("(k p) c -> p k c", p=128)
            for gi, (c0, c1) in enumerate(cgs):
                fw.dma("gpsimd", wG[:, :, c0:c1], vG[:, :, c0:c1], w=[f"wG{gi}"])
                fw.dma("gpsimd", wU[:, :, c0:c1], vU[:, :, c0:c1], w=[f"wU{gi}"])
            load_w(wD, w_down, D, "wD", kc=22, step=6)
            gff = sb(p4, "gff", [128, D]); gfin = sb(p4, "gfin", [128, D])
            fw.dma("sync", gff[:], norm_ffn_g.partition_broadcast(128), w=["gff"], sem="c0")
            fw.dma("sync", gfin[:], norm_final_g.partition_broadcast(128), w=["gfin"], sem="c0")
            h1_t = [sb(p4, f"h1_{i}", [128, D]) for i in range(3)]
            sqj = sb(p4, "sqj4", [128, D], BF16)
            sqj2 = sb(p4, "sqj42", [128, D], BF16)
            st4 = sb(p4, "st44", [128, 8])
            u2 = sb(p4, "u2", [128, D], BF16)
            u2T_l = [sb(p4, f"u2T{i}", [128, NKC, 128], BF16) for i in range(2)]
            eg = sb(p4, "eg", [128, 512]); gg = sb(p4, "gg", [128, 512])
            act_l = [sb(p4, f"act{i}", [128, DFF], BF16) for i in range(2)]
            actT_l = [sb(p4, f"actT{i}", [128, 22, 128], BF16) for i in range(2)]
            h2 = sb(p4, "h2", [128, D])
            ob_ = [sb(p4, f"ob{i}", [128, D]) for i in range(2)]

            def load4(t_):
                fw.dma("sync", h1_t[t_ % 3][:], H1[t_ * 128:(t_ + 1) * 128, :], r=["H1"], w=[f"h1{t_ % 3}"], sem=f"l4{t_ % 3}")

            cgs = [(i * 512, min(DFF, (i + 1) * 512)) for i in range(6)]

            def front4(t_):
                s_ = t_ % 2
                if t_ + 1 < nt:
                    load4(t_ + 1)
                h1 = h1_t[t_ % 3]
                h1k = f"h1{t_ % 3}"
                fw.op("scalar", lambda e: e.activation(out=sqj[:], in_=h1[:], func=AF.Square, accum_out=st4[:, 0:1]),
                      r=[h1k], w=["sqj", "st_ssq"])
                rstd_from_ssq(st4[:, 0:1], st4[:, 2:3], D, ["st_ssq"], "st_rstd", st4[:, 1:2], "st_ln")
                fw.op("vector", lambda e: e.scalar_tensor_tensor(out=u2[:], in0=h1[:], scalar=st4[:, 2:3], in1=gff[:],
                                                                 op0=ALU.mult, op1=ALU.mult), r=[h1k, "st_rstd", "gff"], w=["u2"])
                transposes(u2, "u2", 8, 0, u2T_l[s_][:], f"u2T{s_}")
                yield
                u2T = u2T_l[s_]
                u2Tk = f"u2T{s_}"
                act = act_l[s_]
                for gi, (c0, c1) in enumerate(cgs):
                    w_ = c1 - c0
                    bg = 1 + (gi % 2) * 2
                    bu = bg + 1
                    fw.pe([(lambda e, kc=kc: e.matmul(ps[bg][:, 0:w_], lhsT=u2T[:, kc, :], rhs=wG[:, kc, c0:c1],
                                                      start=(kc == 0), stop=(kc == NKC - 1))) for kc in range(NKC)],
                          r=[u2Tk, f"wG{gi}"], w=[PK[bg]])
                    fw.pe([(lambda e, kc=kc: e.matmul(ps[bu][:, 0:w_], lhsT=u2T[:, kc, :], rhs=wU[:, kc, c0:c1],
                                                      start=(kc == 0), stop=(kc == NKC - 1))) for kc in range(NKC)],
                          r=[u2Tk, f"wU{gi}"], w=[PK[bu]])
                    fw.op("scalar", lambda e: e.activation(out=gg[:, 0:w_], in_=ps[bg][:, 0:w_], func=AF.Silu),
                          r=[PK[bg]], w=["gg"])
                    fw.op("vector", lambda e: e.tensor_tensor(out=act[:, c0:c1], in0=ps[bu][:, 0:w_], in1=gg[:, 0:w_], op=ALU.mult),
                          r=[PK[bu], "gg"], w=[f"act{s_}_{gi}"])
                    yield

            def back4(t_):
                s_ = t_ % 2
                h1 = h1_t[t_ % 3]
                h1k = f"h1{t_ % 3}"
                act = act_l[s_]
                actT = actT_l[s_]
                for bi, (b0, b1) in enumerate(((0, 8), (8, 16), (16, 22))):
                    bank = 5 + bi
                    pv = psbf(bank)
                    fw.pe([(lambda e, j=j: e.transpose(out=pv[:, (j - b0) * 128:(j - b0 + 1) * 128],
                                                       in_=act[:, j * 128:(j + 1) * 128], identity=ident[:])) for j in range(b0, b1)],
                          r=[f"act{s_}_{i}" for i in range(6)] + ["ident"], w=[PK[bank]])
                    n_ = (b1 - b0) * 128
                    if bi == 1:
                        fw.op("scalar", lambda e: e.activation(
                            out=actT[:, b0:b1, :].rearrange("p a b -> p (a b)"), in_=pv[:, 0:n_], func=AF.Copy), r=[PK[bank]], w=[f"actT{s_}_{bi}"])
                    else:
                        fw.op("vector", lambda e: e.tensor_copy(
                            out=actT[:, b0:b1, :].rearrange("p a b -> p (a b)"), in_=pv[:, 0:n_]), r=[PK[bank]], w=[f"actT{s_}_{bi}"])
                    yield
                for half in range(2):
                    cs = slice(half * 512, (half + 1) * 512)
                    bank = 5 + half
                    fw.pe([(lambda e, kc=kc: e.matmul(ps[bank][:, :], lhsT=actT[:, kc, :], rhs=wD[:, kc, cs],
                                                      start=(kc == 0), stop=(kc == 21))) for kc in range(22)],
                          r=[f"actT{s_}_0", f"actT{s_}_1", f"actT{s_}_2", "wD"], w=[PK[bank]])
                    fw.op("vector", lambda e: e.tensor_tensor(out=h2[:, cs], in0=ps[bank][:], in1=h1[:, cs], op=ALU.add),
                          r=[PK[bank], h1k], w=["h2"])
                    yield
                fw.op("scalar", lambda e: e.activation(out=sqj2[:], in_=h2[:], func=AF.Square, accum_out=st4[:, 4:5]), r=["h2"], w=["sqj2", "st_ssq2"])
                rstd_from_ssq(st4[:, 4:5], st4[:, 6:7], D, ["st_ssq2"], "st_rstd2", st4[:, 5:6], "st_ln2")
                o_ = ob_[s_]
                fw.op("vector", lambda e: e.scalar_tensor_tensor(out=o_[:], in0=h2[:], scalar=st4[:, 6:7], in1=gfin[:],
                                                                 op0=ALU.mult, op1=ALU.mult), r=["h2", "st_rstd2", "gfin"], w=[f"ob{s_}"])
                last_tok.append(fw.dma("sync", out_d[t_ * 128:(t_ + 1) * 128, :], o_[:], r=[f"ob{s_}"], w=["out"], sem=f"st_o{s_}"))

            def drain4(g):
                for _ in g:
                    pass

            def interleave4(ga, gb):
                a_alive, b_alive = True, True
                while a_alive or b_alive:
                    if a_alive:
                        try:
                            next(ga)
                        except StopIteration:
                            a_alive = False
                    if b_alive:
                        try:
                            next(gb)
                        except StopIteration:
                            b_alive = False

            load4(0)
            drain4(front4(0))
            for t_ in range(nt):
                if t_ + 1 < nt:
                    interleave4(back4(t_), front4(t_ + 1))
                else:
                    drain4(back4(t_))
            fw.barrier()
        fw.barrier()
    return nc


SLOPES = [2.0 ** (-(h + 1)) for h in range(8)]


def _core_inputs(b, j, inputs, consts):
    x = inputs["x"][b]
    meta = inputs["meta_tokens"]
    hvirt = np.zeros((TV, D), np.float32)
    valid = np.ones((TV,), np.float32)
    if j == 0:
        hvirt[112:128] = meta
        hvirt[128:] = x
        valid[:112] = 0
    else:
        hvirt[240:256] = meta
        hvirt[256:] = x[:63 * 128]
        valid[:240] = 0
    kaug = np.zeros((8, 3, TV), np.float32)
    p = np.arange(TV) % 128
    v = np.arange(TV) // 128
    for h in range(8):
        kaug[h, 0] = 8 * SLOPES[h] * (p - 127) + np.where(valid > 0, 0.0, NEGB)
        kaug[h, 1] = 8 * SLOPES[h] * 128 * v
        kaug[h, 2] = 1.0
    m = dict(consts)
    m["hv"] = hvirt
    m["vmask"] = np.ascontiguousarray(valid.reshape(NV, 128).T)
    m["kaug"] = kaug.astype(ml_dtypes.bfloat16)
    return m


def _consts(inputs):
    qaug = np.zeros((8, 3, TO), np.float32)
    vq = 2 * (np.arange(TO) // 128)
    for h in range(8):
        qaug[h, 0] = 1.0
        qaug[h, 1] = 1.0
        qaug[h, 2] = -8 * SLOPES[h] * 128 * vq
    kk = np.arange(128)
    c = {
        "qaug": qaug.astype(ml_dtypes.bfloat16),
        "trim": np.where(kk[:, None] > kk[None, :], NEGB, 0.0).astype(ml_dtypes.bfloat16),
        "ident": np.eye(128, dtype=np.float32).astype(ml_dtypes.bfloat16),
        "umask": (kk[:, None] <= kk[None, :]).astype(np.float32),
        "slmask": (kk[:, None] > kk[None, :]).astype(np.float32),
    }
    for name in ("w_in", "norm_mix_g", "gate_bias", "conv_w", "conv_b", "dt_bias", "a_log", "d_skip", "ssd_norm_g",
                 "lambda_q1", "lambda_k1", "lambda_q2", "lambda_k2", "subln_g", "w_ssd_branch", "w_attn_branch",
                 "w_out", "norm_ffn_g", "w_gate_ffn", "w_up_ffn", "w_down_ffn"):
        c[name] = np.ascontiguousarray(np.asarray(inputs[name], np.float32)[0])
    c["norm_final_g"] = np.ascontiguousarray(np.asarray(inputs["norm_final_g"], np.float32))
    return c


def kernel(**inputs):
    inputs = {k: np.asarray(v) for k, v in inputs.items()}
    consts = _consts(inputs)
    nc = build_nc()
    in_maps = [_core_inputs(c // 2, c % 2, inputs, consts) for c in range(8)]
    res = run_bass_kernel_spmd(nc, in_maps, core_ids=list(range(8)))
    B = inputs["x"].shape[0]
    out = np.zeros((B, 8192, D), np.float32)
    for c in range(8):
        b, j = c // 2, c % 2
        o = np.asarray(res.results[c]["out"]).reshape(NT, 128, D)
        for t in range(1, NT):
            xc = 2 * t - 1 if j == 0 else 2 * t - 2
            out[b, xc * 128:(xc + 1) * 128] = o[t]
    return out
```
